# Optimizing a Trainium2 kernel written in Bass

```python
import math
import jax, jax.numpy as jnp
from jax import lax
import numpy as np

D_MODEL = 1024
BATCH = 8
SEQ = 2048
DEPTH = 1
DEC_BATCH = 128
DEC_SEQ = 1
PAST_LEN = 16384
PAGE_SIZE = 128

D_INNER_M = 2 * D_MODEL
P_M = 64
H_M = D_INNER_M // P_M
G_M = 4
HG_M = H_M // G_M
N_M = 128
CONV_M = D_INNER_M + 2 * G_M * N_M
H_G = 8
DK = 128
DV = 128
CONV_G = 2 * H_G * DK + H_G * DV
CONV_K = 4
CHUNK = 64
D_FF = ((8 * D_MODEL // 3 + 255) // 256) * 256
PLE_DIM = 256
DT_MIN = 0.001
DT_MAX = 0.1
EPS = 1e-6
IN_SIZES = (D_INNER_M, CONV_M, H_M, CONV_G, H_G * DV, H_G, H_G, 2 * D_MODEL)
N_IN = sum(IN_SIZES)

kernel_name = "hybrid_ssd_gdn_gated_merge_step"


def _rms_norm(x, g):
    xf = x.astype(jnp.float32)
    y = xf * lax.rsqrt(jnp.mean(xf * xf, axis=-1, keepdims=True) + EPS)
    return (y * g.astype(jnp.float32)).astype(x.dtype)


def _l2norm(t):
    return t * lax.rsqrt(jnp.sum(t * t, axis=-1, keepdims=True) + EPS)


def _split(t, sizes):
    outs, start = [], 0
    for s in sizes:
        outs.append(t[..., start:start + s])
        start += s
    return outs


def _causal_conv(u, buf, w, b):
    seq = u.shape[1]
    full = jnp.concatenate([buf, u], axis=1)
    out = full[:, 0:seq] * w[0]
    for j in range(1, CONV_K):
        out = out + full[:, j:j + seq] * w[j]
    if b is not None:
        out = out + b
    return out, full[:, seq:]


def _chunking(seq):
    c = min(CHUNK, seq)
    n = -(-seq // c)
    return c, n, n * c - seq


def _pad_seq(t, pad):
    return jnp.pad(t, [(0, 0), (0, pad)] + [(0, 0)] * (t.ndim - 2))


def _ssd_chunked(x, dt, a, bm, cm, s0):
    bsz, seq = x.shape[:2]
    c, n, pad = _chunking(seq)
    x, dt, bm, cm = (_pad_seq(t, pad) for t in (x, dt, bm, cm))
    xdt = (x * dt[..., None]).reshape(bsz, n, c, G_M, HG_M, P_M)
    da = (dt * a).reshape(bsz, n, c, G_M, HG_M)
    bm = bm.reshape(bsz, n, c, G_M, N_M)
    cm = cm.reshape(bsz, n, c, G_M, N_M)
    acum = jnp.cumsum(da, axis=2)
    acum_t = jnp.moveaxis(acum, 2, -1)
    tril = jnp.tril(jnp.ones((c, c), bool))
    lmat = jnp.exp(jnp.where(tril, acum_t[..., :, None] - acum_t[..., None, :], -jnp.inf))
    cb = jnp.einsum('bnlgd,bnsgd->bngls', cm, bm)
    y_diag = jnp.einsum('bngls,bnghls,bnsghp->bnlghp', cb, lmat, xdt)
    decay_out = jnp.exp(acum[:, :, -1:] - acum)
    chunk_states = jnp.einsum('bnlgd,bnlgh,bnlghp->bnghpd', bm, decay_out, xdt)
    chunk_decay = jnp.exp(acum[:, :, -1])

    def step(s, inp):
        st, dec = inp
        return s * dec[..., None, None] + st, s

    s0 = s0.reshape(bsz, G_M, HG_M, P_M, N_M)
    s_final, prev = lax.scan(step, s0, (jnp.moveaxis(chunk_states, 1, 0), jnp.moveaxis(chunk_decay, 1, 0)))
    prev = jnp.moveaxis(prev, 0, 1)
    y_off = jnp.einsum('bnlgd,bnghpd,bnlgh->bnlghp', cm, prev, jnp.exp(acum))
    y = (y_diag + y_off).reshape(bsz, n * c, H_M, P_M)[:, :seq]
    return y, s_final.reshape(bsz, H_M, P_M, N_M)


def _gated_delta_chunked(q, k, v, g, beta, s0):
    bsz, seq = q.shape[:2]
    c, n, pad = _chunking(seq)
    q, k, v, g, beta = (_pad_seq(t, pad) for t in (q, k, v, g, beta))
    to_chunks = lambda t: jnp.moveaxis(t.reshape((bsz, n, c) + t.shape[2:]), 3, 1)
    q, k, v, g, beta = (to_chunks(t) for t in (q, k, v, g, beta))
    gcum = jnp.cumsum(g, axis=-1)
    tril = jnp.tril(jnp.ones((c, c), bool))
    strict = jnp.tril(jnp.ones((c, c), bool), -1)
    dmask = jnp.exp(jnp.where(tril, gcum[..., :, None] - gcum[..., None, :], -jnp.inf))
    kb = k * beta[..., None]
    a_mat = jnp.where(strict, jnp.einsum('bhnld,bhnsd->bhnls', kb, k) * dmask, 0.0)
    rhs = jnp.concatenate([v * beta[..., None], kb * jnp.exp(gcum)[..., None]], axis=-1)
    eye = jnp.eye(c, dtype=a_mat.dtype)
    sol = lax.linalg.triangular_solve(a_mat + eye, rhs, left_side=True, lower=True, unit_diagonal=True)
    u, w = sol[..., :DV], sol[..., DV:]
    qk = jnp.where(tril, jnp.einsum('bhnld,bhnsd->bhnls', q, k) * dmask, 0.0)
    qg = q * jnp.exp(gcum)[..., None]
    kg = k * jnp.exp(gcum[..., -1:] - gcum)[..., None]
    glast = jnp.exp(gcum[..., -1])

    def step(s, inp):
        u_c, w_c, qg_c, kg_c, qk_c, gl_c = inp
        v_new = u_c - jnp.einsum('bhld,bhde->bhle', w_c, s)
        o = jnp.einsum('bhld,bhde->bhle', qg_c, s) + jnp.einsum('bhls,bhse->bhle', qk_c, v_new)
        s = s * gl_c[..., None, None] + jnp.einsum('bhld,bhle->bhde', kg_c, v_new)
        return s, o

    xs = tuple(jnp.moveaxis(t, 2, 0) for t in (u, w, qg, kg, qk, glast))
    s_final, o = lax.scan(step, s0, xs)
    o = jnp.transpose(o, (1, 0, 3, 2, 4)).reshape(bsz, n * c, H_G, DV)[:, :seq]
    return o, s_final


def _layer(x, p, s_ssm, s_ssm_conv, s_gdn, s_gdn_conv,
           norm_mix, w_in, ssm_conv_w, ssm_conv_b, ssm_dt_bias, ssm_a_log, ssm_d, ssm_norm,
           gdn_conv_w, gdn_dt_bias, gdn_a_log, gdn_norm, w_branch_ssm, w_branch_gdn, w_out,
           norm_ffn, w_ffn_in, w_ffn_out, norm_pl, w_pl_gate, w_pl_proj):
    bsz, seq, _ = x.shape
    f32 = jnp.float32
    dtype = x.dtype
    h = _rms_norm(x, norm_mix)
    proj = jnp.einsum('bld,de->ble', h, w_in).astype(f32)
    z, xbc, dt_raw, qkv, gate, b_raw, a_raw, mgate = _split(proj, IN_SIZES)

    xbc, new_ssm_conv = _causal_conv(xbc, s_ssm_conv.astype(f32), ssm_conv_w.astype(f32), ssm_conv_b.astype(f32))
    xbc = jax.nn.silu(xbc)
    xs_m, bm, cm = _split(xbc, (D_INNER_M, G_M * N_M, G_M * N_M))
    xs_m = xs_m.reshape(bsz, seq, H_M, P_M)
    bm = bm.reshape(bsz, seq, G_M, N_M)
    cm = cm.reshape(bsz, seq, G_M, N_M)
    dt = jax.nn.softplus(dt_raw + ssm_dt_bias.astype(f32))
    a = -jnp.exp(ssm_a_log.astype(f32))
    y, new_ssm = _ssd_chunked(xs_m, dt, a, bm, cm, s_ssm.astype(f32))
    y = (y + xs_m * ssm_d.astype(f32)[:, None]).reshape(bsz, seq, D_INNER_M) * jax.nn.silu(z)
    y = _rms_norm(y.reshape(bsz, seq, G_M, D_INNER_M // G_M), ssm_norm.reshape(G_M, D_INNER_M // G_M))
    y = y.reshape(bsz, seq, D_INNER_M)

    qkv, new_gdn_conv = _causal_conv(qkv, s_gdn_conv.astype(f32), gdn_conv_w.astype(f32), None)
    qkv = jax.nn.silu(qkv)
    q, k, v = _split(qkv, (H_G * DK, H_G * DK, H_G * DV))
    q = _l2norm(q.reshape(bsz, seq, H_G, DK)) * (DK ** -0.5)
    k = _l2norm(k.reshape(bsz, seq, H_G, DK))
    v = v.reshape(bsz, seq, H_G, DV)
    beta = jax.nn.sigmoid(b_raw)
    g = -jnp.exp(gdn_a_log.astype(f32)) * jax.nn.softplus(a_raw + gdn_dt_bias.astype(f32))
    o, new_gdn = _gated_delta_chunked(q, k, v, g, beta, s_gdn.astype(f32))
    o = _rms_norm(o, gdn_norm) * jax.nn.silu(gate.reshape(bsz, seq, H_G, DV))
    o = o.reshape(bsz, seq, H_G * DV)

    g_ssm, g_gdn = jnp.split(jax.nn.sigmoid(mgate).astype(dtype), 2, axis=-1)
    mix = (g_ssm * jnp.einsum('ble,ed->bld', y.astype(dtype), w_branch_ssm)
           + g_gdn * jnp.einsum('ble,ed->bld', o.astype(dtype), w_branch_gdn))
    x = x + jnp.einsum('bld,de->ble', mix, w_out)

    gu = jnp.einsum('bld,df->blf', _rms_norm(x, norm_ffn), w_ffn_in)
    gt, up = jnp.split(gu, 2, axis=-1)
    x = x + jnp.einsum('blf,fd->bld', jax.nn.silu(gt) * up, w_ffn_out)

    pe = jnp.einsum('blp,pd->bld', p, w_pl_proj)
    x = x + pe * jax.nn.sigmoid(jnp.einsum('bld,de->ble', _rms_norm(x, norm_pl), w_pl_gate))
    return x, (new_ssm.astype(dtype), new_ssm_conv.astype(dtype), new_gdn.astype(dtype), new_gdn_conv.astype(dtype))


def setup_inputs(seed: int = 0) -> dict:
    key = jax.random.key(seed)
    ks = iter(jax.random.split(key, 48))
    f32 = jnp.float32

    def nrm(shape, scale):
        return jax.random.normal(next(ks), shape, f32) * scale

    def gain(shape):
        return 1.0 + nrm(shape, 0.02)

    def dt_bias(shape):
        u = jax.random.uniform(next(ks), shape, f32)
        dt = jnp.exp(u * (math.log(DT_MAX) - math.log(DT_MIN)) + math.log(DT_MIN))
        return dt + jnp.log(-jnp.expm1(-dt))

    def a_log(shape):
        return jnp.log(jax.random.uniform(next(ks), shape, f32, 1.0, 16.0))

    L = DEPTH
    return {
        "x_prompt": nrm((BATCH, SEQ, D_MODEL), 1.0),
        "x_sample": nrm((DEC_BATCH, DEC_SEQ, D_MODEL), 1.0),
        "p_prompt": nrm((DEPTH, BATCH, SEQ, PLE_DIM), 1.0),
        "p_sample": nrm((DEPTH, DEC_BATCH, DEC_SEQ, PLE_DIM), 1.0),
        "state_ssm": nrm((DEPTH, DEC_BATCH, H_M, P_M, N_M), 0.1),
        "state_ssm_conv": nrm((DEPTH, DEC_BATCH, CONV_K - 1, CONV_M), 1.0),
        "state_gdn": nrm((DEPTH, DEC_BATCH, H_G, DK, DV), 0.1),
        "state_gdn_conv": nrm((DEPTH, DEC_BATCH, CONV_K - 1, CONV_G), 1.0),
        "norm_mix": gain((L, D_MODEL)),
        "w_in": nrm((L, D_MODEL, N_IN), D_MODEL ** -0.5),
        "ssm_conv_w": nrm((L, CONV_K, CONV_M), CONV_K ** -0.5),
        "ssm_conv_b": nrm((L, CONV_M), 0.01),
        "ssm_dt_bias": dt_bias((L, H_M)),
        "ssm_a_log": a_log((L, H_M)),
        "ssm_d": 1.0 + nrm((L, H_M), 0.1),
        "ssm_norm": gain((L, D_INNER_M)),
        "gdn_conv_w": nrm((L, CONV_K, CONV_G), CONV_K ** -0.5),
        "gdn_dt_bias": dt_bias((L, H_G)),
        "gdn_a_log": a_log((L, H_G)),
        "gdn_norm": gain((L, DV)),
        "w_branch_ssm": nrm((L, D_INNER_M, D_MODEL), D_INNER_M ** -0.5),
        "w_branch_gdn": nrm((L, H_G * DV, D_MODEL), (H_G * DV) ** -0.5),
        "w_out": nrm((L, D_MODEL, D_MODEL), D_MODEL ** -0.5),
        "norm_ffn": gain((L, D_MODEL)),
        "w_ffn_in": nrm((L, D_MODEL, 2 * D_FF), D_MODEL ** -0.5),
        "w_ffn_out": nrm((L, D_FF, D_MODEL), D_FF ** -0.5),
        "norm_pl": gain((L, D_MODEL)),
        "w_pl_gate": nrm((L, D_MODEL, D_MODEL), D_MODEL ** -0.5),
        "w_pl_proj": nrm((L, PLE_DIM, D_MODEL), PLE_DIM ** -0.5),
        "norm_final": gain((D_MODEL,)),
    }


def reference(x_prompt, x_sample, p_prompt, p_sample, state_ssm, state_ssm_conv, state_gdn, state_gdn_conv,
              norm_mix, w_in, ssm_conv_w, ssm_conv_b, ssm_dt_bias, ssm_a_log, ssm_d, ssm_norm,
              gdn_conv_w, gdn_dt_bias, gdn_a_log, gdn_norm, w_branch_ssm, w_branch_gdn, w_out,
              norm_ffn, w_ffn_in, w_ffn_out, norm_pl, w_pl_gate, w_pl_proj, norm_final):
    def run(x, p, st_ssm, st_ssm_conv, st_gdn, st_gdn_conv):
        new = ([], [], [], [])
        for i in range(DEPTH):
            x, states = _layer(x, p[i], st_ssm[i], st_ssm_conv[i], st_gdn[i], st_gdn_conv[i],
                               norm_mix[i], w_in[i], ssm_conv_w[i], ssm_conv_b[i], ssm_dt_bias[i],
                               ssm_a_log[i], ssm_d[i], ssm_norm[i], gdn_conv_w[i], gdn_dt_bias[i],
                               gdn_a_log[i], gdn_norm[i], w_branch_ssm[i], w_branch_gdn[i], w_out[i],
                               norm_ffn[i], w_ffn_in[i], w_ffn_out[i], norm_pl[i], w_pl_gate[i], w_pl_proj[i])
            for lst, s in zip(new, states):
                lst.append(s)
        return _rms_norm(x, norm_final), [jnp.stack(lst) for lst in new]

    bp = x_prompt.shape[0]
    dt = x_prompt.dtype
    z_ssm = jnp.zeros((DEPTH, bp, H_M, P_M, N_M), dt)
    z_ssm_conv = jnp.zeros((DEPTH, bp, CONV_K - 1, CONV_M), dt)
    z_gdn = jnp.zeros((DEPTH, bp, H_G, DK, DV), dt)
    z_gdn_conv = jnp.zeros((DEPTH, bp, CONV_K - 1, CONV_G), dt)
    y_prompt, ps = run(x_prompt, p_prompt, z_ssm, z_ssm_conv, z_gdn, z_gdn_conv)
    y_sample, ss = run(x_sample, p_sample, state_ssm, state_ssm_conv, state_gdn, state_gdn_conv)
    return (y_prompt, y_sample, ps[0], ps[1], ps[2], ps[3], ss[0], ss[1], ss[2], ss[3])
```

```python
from contextlib import ExitStack
import numpy as np
import concourse.bass as bass
import concourse.mybir as mybir
from concourse.bass_utils import run_bass_kernel_spmd

F32 = mybir.dt.float32
BF16 = mybir.dt.bfloat16
AF = mybir.ActivationFunctionType
ALU = mybir.AluOpType
AX = mybir.AxisListType

ENGS = ["pe", "act", "dve", "pool", "sp"]


class _Op:
    __slots__ = ("eng", "fn", "idx", "signal", "deps", "dma", "slot", "val", "ticket")

    def __init__(self, eng, fn, dma):
        self.eng = eng
        self.fn = fn
        self.dma = dma
        self.signal = False
        self.deps = []
        self.slot = None
        self.val = None
        self.ticket = None


class _St:
    __slots__ = ("writer", "readers")

    def __init__(self):
        self.writer = None
        self.readers = []


def _k(key):
    return key if isinstance(key, tuple) else (key,)


class Sched:
    NDMA = 16

    def __init__(self, nc):
        self.nc = nc
        self.stack = ExitStack()
        self.ops = {e: [] for e in ENGS}
        self.res = {}
        self.ndma = 0
        self.ndmaq = {}
        self.dma_prev = {}
        self.finished = False
        self._cap = None

    def sb(self, name, shape, dtype):
        return self.stack.enter_context(self.nc.sbuf_tensor(name, list(shape), dtype))

    def ps(self, name, shape, dtype=F32):
        return self.stack.enter_context(self.nc.psum_tensor(name, list(shape), dtype))

    def _overlaps(self, key):
        grp = self.res.get(key[0])
        if not grp:
            return []
        out = []
        n = len(key)
        for k2, st in grp.items():
            m = min(n, len(k2))
            if key[:m] == k2[:m]:
                out.append(st)
        return out

    def _state(self, key):
        grp = self.res.setdefault(key[0], {})
        st = grp.get(key)
        if st is None:
            st = grp[key] = _St()
        return st

    def _add(self, op, reads, writes):
        reads = [_k(r) for r in reads]
        writes = [_k(w) for w in writes]
        deps = {}

        def need(d, raw):
            if d is op:
                return
            if (not d.dma) and (not op.dma) and d.eng == op.eng and not raw and op.eng != "pool":
                return
            if d.dma:
                deps[id(d)] = d
            else:
                cur = deps.get(d.eng)
                if cur is None or cur.idx < d.idx:
                    deps[d.eng] = d

        for k in reads:
            psum = k[0][0] == "P" and k[0][1:].isdigit()
            for st in self._overlaps(k):
                if st.writer is not None:
                    need(st.writer, True)
                if psum:
                    for r in st.readers:
                        if r.eng != op.eng:
                            need(r, True)
        for k in writes:
            for st in self._overlaps(k):
                if st.writer is not None:
                    need(st.writer, False)
                for r in st.readers:
                    need(r, False)
        for d in deps.values():
            if not d.dma:
                d.signal = True
        op.deps = list(deps.values())
        op.idx = len(self.ops[op.eng])
        self.ops[op.eng].append(op)
        for k in reads:
            self._state(k).readers.append(op)
        for k in writes:
            st = self._state(k)
            st.writer = op
            st.readers = []
            grp = self.res[k[0]]
            n = len(k)
            for k2 in [k2 for k2 in grp if len(k2) > n and k2[:n] == k]:
                del grp[k2]

    def capture(self, f):
        self._cap = []
        f()
        c, self._cap = self._cap, None
        return c

    def replay(self, items):
        for it in items:
            if it[0] == "op":
                self.op(*it[1:])
            else:
                self.dma(*it[1:])

    @staticmethod
    def interleave(a, b, front=1.0):
        out = []
        ia = ib = 0
        while ia < len(a) or ib < len(b):
            if ib >= len(b) or (ia < len(a) and ia * len(b) * front <= ib * len(a)):
                out.append(a[ia]); ia += 1
            else:
                out.append(b[ib]); ib += 1
        return out

    def op(self, eng, fn, reads=(), writes=()):
        if self._cap is not None:
            self._cap.append(("op", eng, fn, tuple(reads), tuple(writes)))
            return None
        o = _Op(eng, fn, False)
        self._add(o, reads, writes)
        return o

    def dma(self, eng, out, in_, reads=(), writes=(), slow=False):
        if self._cap is not None:
            self._cap.append(("dma", eng, out, in_, tuple(reads), tuple(writes), slow))
            return None
        if slow:
            o = _Op(eng, lambda e: e.dma_start(out=out, in_=in_, allow_slow_non_contiguous=True), True)
        else:
            o = _Op(eng, lambda e: e.dma_start(out=out, in_=in_), True)
        j = self.ndmaq.get(eng, 0)
        self.ndmaq[eng] = j + 1
        self.ndma += 1
        o.slot = (eng, j % self.NDMA)
        o.val = 16 * (j // self.NDMA + 1)
        prev = self.dma_prev.get(o.slot)
        self._add(o, reads, writes)
        if prev is not None:
            o.deps.append(prev)
        self.dma_prev[o.slot] = o
        return o

    def finish(self, final_wait_eng="sp"):
        nc = self.nc
        last = _Op(final_wait_eng, None, False)
        last.deps = [o for e in ENGS for o in self.ops[e] if o.dma]
        for e in ENGS:
            if e != final_wait_eng and self.ops[e]:
                tail = [o for o in self.ops[e] if not o.dma and o.fn is not None]
                if tail:
                    tail[-1].signal = True
                    last.deps.append(tail[-1])
        last.idx = len(self.ops[final_wait_eng])
        self.ops[final_wait_eng].append(last)
        for e in ENGS:
            t = 0
            for o in self.ops[e]:
                if o.signal:
                    t += 1
                    o.ticket = t
        sems = {e: self.stack.enter_context(nc.semaphore("s_" + e)) for e in ENGS}
        dsem = {}
        for q, cnt in self.ndmaq.items():
            for i in range(min(self.NDMA, cnt)):
                dsem[(q, i)] = self.stack.enter_context(nc.semaphore("d_%s_%d" % (q, i)))
        ops = self.ops
        nwaits = [0]

        def emit(eng_name, eng):
            waited = {}
            for o in ops[eng_name]:
                for d in o.deps:
                    if d.dma:
                        key, val = ("d", d.slot), d.val
                        sem = dsem[d.slot]
                    else:
                        key, val = ("e", d.eng), d.ticket
                        sem = sems[d.eng]
                    if waited.get(key, 0) >= val:
                        continue
                    waited[key] = val
                    eng.wait_ge(sem, val)
                    nwaits[0] += 1
                if o.fn is None:
                    continue
                ins = o.fn(eng)
                if o.dma:
                    ins.then_inc(dsem[o.slot], 16)
                elif o.signal:
                    ins.then_inc(sems[eng_name], 1)

        with nc.Block() as block:
            @block.tensor
            def _(e):
                emit("pe", e)

            @block.scalar
            def _(e):
                emit("act", e)

            @block.vector
            def _(e):
                emit("dve", e)

            @block.gpsimd
            def _(e):
                emit("pool", e)

            @block.sync
            def _(e):
                emit("sp", e)
        self.stack.close()
        self.finished = True
        self.stats = {e: len(ops[e]) for e in ENGS}
        self.stats["waits"] = nwaits[0]


D = 1024
NP = 2048
NSMP = 16
SEGLEN = 512
CACHE_W = True
DBG_H = 0
HM = 32
OFF_Z = 0
OFF_XBC = 2048
OFF_DT = 5120
OFF_QKV = 5152
OFF_GATE = 8224
OFF_B = 9248
OFF_A = 9256
OFF_MG = 9264
N_IN = 11312
DFF = 2816
EPS = 1e-6


def _blocks(n):
    out = []
    c = 0
    while c < n:
        w = min(512, n - c)
        out.append((c, w))
        c += w
    return out


class _Stop(Exception):
    pass


STOP_AT = None
INPUT_SHAPES = {}


def build_nc(debug=None):
    nc = bass.Bass("TRN2", target_bir_lowering=False)
    S = Sched(nc)

    cnt = {}

    def ckpt(name):
        if STOP_AT is None:
            return
        cnt[name] = cnt.get(name, 0) + 1
        want, _, k = STOP_AT.partition("#")
        if name == want and cnt[name] == int(k or 1):
            raise _Stop()
    try:
        _build_body(nc, S, debug, ckpt)
    except _Stop:
        pass
    S.finish()
    return nc, S


def _build_body(nc, S, debug, ckpt):

    def din(name, shape):
        INPUT_SHAPES[name] = list(shape)
        return nc.dram_tensor(name, list(shape), F32, kind="ExternalInput").ap()

    def dout(name, shape):
        return nc.dram_tensor(name, list(shape), F32, kind="ExternalOutput").ap()

    x_p = din("x_p", [NP, D]); x_s = din("x_s", [NSMP, D])
    p_p = din("p_p", [NP, 256]); p_s = din("p_s", [NSMP, 256])
    st_ssm = din("st_ssm", [NSMP, 2048, 128])
    st_ssm_conv = din("st_ssm_conv", [NSMP, 3, 3072])
    st_gdn = din("st_gdn", [NSMP, 8, 128, 128])
    st_gdn_conv = din("st_gdn_conv", [NSMP, 3, 3072])
    norm_mix = din("norm_mix", [D]); w_in = din("w_in", [D, N_IN])
    ssm_conv_w = din("ssm_conv_w", [4, 3072]); ssm_conv_b = din("ssm_conv_b", [3072])
    ssm_dt_bias = din("ssm_dt_bias", [32]); ssm_a_log = din("ssm_a_log", [32]); ssm_d = din("ssm_d", [32])
    ssm_norm = din("ssm_norm", [2048]); gdn_conv_w = din("gdn_conv_w", [4, 3072])
    gdn_dt_bias = din("gdn_dt_bias", [8]); gdn_a_log = din("gdn_a_log", [8]); gdn_norm = din("gdn_norm", [128])
    w_bs = din("w_branch_ssm", [2048, D]); w_bg = din("w_branch_gdn", [1024, D]); w_out = din("w_out", [D, D])
    norm_ffn = din("norm_ffn", [D]); w_ffn_in = din("w_ffn_in", [D, 2 * DFF]); w_ffn_out = din("w_ffn_out", [DFF, D])
    norm_pl = din("norm_pl", [D]); w_pl_gate = din("w_pl_gate", [D, D]); w_pl_proj = din("w_pl_proj", [256, D])
    norm_final = din("norm_final", [D])

    y_p = dout("y_p", [NP, D]); y_s = dout("y_s", [NSMP, D])
    o_ssm_p = dout("o_ssm_p", [2048, 128]); o_ssmc_p = dout("o_ssmc_p", [3, 3072])
    o_gdn_p = dout("o_gdn_p", [8, 128, 128]); o_gdnc_p = dout("o_gdnc_p", [3, 3072])
    o_ssm_s = dout("o_ssm_s", [NSMP, 2048, 128]); o_ssmc_s = dout("o_ssmc_s", [NSMP, 3, 3072])
    o_gdn_s = dout("o_gdn_s", [NSMP, 8, 128, 128]); o_gdnc_s = dout("o_gdnc_s", [NSMP, 3, 3072])

    dbg_outs = {}

    def dbg(name, ap, keys, shape):
        if debug is None or name not in debug:
            return
        t = nc.dram_tensor("dbg_" + name, list(shape), ap.dtype, kind="ExternalOutput").ap()
        dbg_outs[name] = t
        S.dma("sp", t, ap, reads=list(keys), writes=[("dbg", name)])

    NCMAX = SEGLEN + NSMP
    NTMAX = SEGLEN // 128 + NSMP

    ident_f = S.sb("ident_f", [128, 128], F32)
    ident_b = S.sb("ident_b", [128, 128], BF16)
    ones_f = S.sb("ones_f", [128, 128], F32)
    ones_b = S.sb("ones_b", [128, 128], BF16)
    mU = S.sb("mU", [128, 128], F32)
    mT = S.sb("mT", [128, 128], F32)
    mNEG = S.sb("mNEG", [128, 128], F32)
    mNUS = S.sb("mNUS", [128, 128], F32)
    epsc = S.sb("epsc", [128, 1], F32)
    S.op("pool", lambda e: e.memset(epsc[:], EPS), writes=["epsc"])
    S.op("pool", lambda e: e.memset(ones_f[:], 1.0), writes=["ones_f"])
    S.op("pool", lambda e: e.memset(ones_b[:], 1.0), writes=["ones_b"])
    S.op("pool", lambda e: e.memset(ident_f[:], 1.0), writes=["ident_f"])
    S.op("pool", lambda e: e.affine_select(out=ident_f[:], in_=ident_f[:], pattern=[[1, 128]], compare_op=ALU.is_ge, fill=0.0, base=0, channel_multiplier=-1), reads=["ident_f"], writes=["ident_f"])
    S.op("pool", lambda e: e.affine_select(out=ident_f[:], in_=ident_f[:], pattern=[[-1, 128]], compare_op=ALU.is_ge, fill=0.0, base=0, channel_multiplier=1), reads=["ident_f"], writes=["ident_f"])
    S.op("pool", lambda e: e.tensor_copy(out=ident_b[:], in_=ident_f[:]), reads=["ident_f"], writes=["ident_b"])
    S.op("pool", lambda e: e.memset(mU[:], 1.0), writes=["mU"])
    S.op("pool", lambda e: e.affine_select(out=mU[:], in_=mU[:], pattern=[[1, 128]], compare_op=ALU.is_ge, fill=0.0, base=0, channel_multiplier=-1), reads=["mU"], writes=["mU"])
    S.op("pool", lambda e: e.memset(mT[:], 1.0), writes=["mT"])
    S.op("pool", lambda e: e.affine_select(out=mT[:], in_=mT[:], pattern=[[-1, 128]], compare_op=ALU.is_ge, fill=0.0, base=-1, channel_multiplier=1), reads=["mT"], writes=["mT"])
    S.op("pool", lambda e: e.memset(mNEG[:], 0.0), writes=["mNEG"])
    S.op("pool", lambda e: e.affine_select(out=mNEG[:], in_=mNEG[:], pattern=[[1, 128]], compare_op=ALU.is_ge, fill=-30000.0, base=0, channel_multiplier=-1), reads=["mNEG"], writes=["mNEG"])
    mUS = S.sb("mUS", [128, 128], F32)
    S.op("pool", lambda e: e.memset(mUS[:], 1.0), writes=["mUS"])
    S.op("pool", lambda e: e.affine_select(out=mUS[:], in_=mUS[:], pattern=[[1, 128]], compare_op=ALU.is_ge, fill=0.0, base=-1, channel_multiplier=-1), reads=["mUS"], writes=["mUS"])
    cmask = S.sb("cmask", [128, 14, 128], BF16)
    cmask_d = din("cmask_in", [128, 14, 128])
    S.dma("pool", cmask[:], cmask_d, writes=["cmask"])
    S.op("pool", lambda e: e.memset(mNUS[:], -1.0), writes=["mNUS"])
    S.op("pool", lambda e: e.affine_select(out=mNUS[:], in_=mNUS[:], pattern=[[1, 128]], compare_op=ALU.is_ge, fill=0.0, base=-1, channel_multiplier=-1), reads=["mNUS"], writes=["mNUS"])

    def col_param(name, src, nk):
        t = S.sb(name, [128, nk], F32)
        S.dma("sp", t[:], src.rearrange("(k p) -> p k", p=128), writes=[name], slow=True)
        return t

    g_mix = col_param("g_mix", norm_mix, 8)
    g_ffn = col_param("g_ffn", norm_ffn, 8)
    g_pl = col_param("g_pl", norm_pl, 8)
    g_fin = col_param("g_fin", norm_final, 8)
    g_ssm = col_param("g_ssm", ssm_norm, 16)
    g_gdn = col_param("g_gdn", gdn_norm, 1)
    cb_ssm = col_param("cb_ssm", ssm_conv_b, 24)
    cw_ssm = S.sb("cw_ssm", [128, 4, 24], F32)
    cw_gdn = S.sb("cw_gdn", [128, 4, 24], F32)
    for j in range(4):
        S.dma("sp", cw_ssm[:, j, :], ssm_conv_w[j].rearrange("(k p) -> p k", p=128), writes=[("cw_ssm", j)], slow=True)
        S.dma("sp", cw_gdn[:, j, :], gdn_conv_w[j].rearrange("(k p) -> p k", p=128), writes=[("cw_gdn", j)], slow=True)
    hp = S.sb("hp", [32, 8], F32)
    S.dma("sp", hp[:, 0:1], ssm_dt_bias.rearrange("(p o) -> p o", o=1), writes=[("hp", 0)], slow=True)
    S.dma("sp", hp[:, 1:2], ssm_a_log.rearrange("(p o) -> p o", o=1), writes=[("hp", 1)], slow=True)
    S.dma("sp", hp[0:8, 3:4], gdn_dt_bias.rearrange("(p o) -> p o", o=1), writes=[("hp", 3)], slow=True)
    S.dma("sp", hp[0:8, 4:5], gdn_a_log.rearrange("(p o) -> p o", o=1), writes=[("hp", 4)], slow=True)
    S.dma("sp", hp[:, 6:7], ssm_d.rearrange("(p o) -> p o", o=1), writes=[("hp", 6)], slow=True)
    S.op("act", lambda e: e.activation(out=hp[:, 2:3], in_=hp[:, 1:2], func=AF.Exp), reads=[("hp", 1)], writes=[("hp", 2)])
    S.op("act", lambda e: e.mul(out=hp[:, 2:3], in_=hp[:, 2:3], mul=-1.0), reads=[("hp", 2)], writes=[("hp", 2)])
    S.op("act", lambda e: e.activation(out=hp[0:8, 5:6], in_=hp[0:8, 4:5], func=AF.Exp), reads=[("hp", 4)], writes=[("hp", 5)])
    S.op("act", lambda e: e.mul(out=hp[0:8, 5:6], in_=hp[0:8, 5:6], mul=-1.0), reads=[("hp", 5)], writes=[("hp", 5)])
    dskB = S.sb("dskB", [128, 32], F32)
    S.dma("sp", dskB[:], ssm_d.partition_broadcast(128), writes=["dskB"])

    ckpt("params")
    P = [S.ps("P%d" % i, [128, 512], F32) for i in range(8)]

    xT = S.sb("xT", [128, 8, NCMAX], F32)
    hT = S.sb("hT", [128, 8, NCMAX], BF16)
    yT = S.sb("yT", [128, 16, NCMAX], BF16)
    oT = S.sb("oT", [128, 8, NCMAX], BF16)
    NB4 = 10
    arena = S.sb("arena", [128, max(NB4 * (SEGLEN // 128) * 128, 8 * NCMAX)], BF16)
    mixT = arena[:, 0:8 * NCMAX].rearrange("p (k c) -> p k c", k=8)
    NWST = 3
    wst = [S.sb("wst%d" % i, [128, 16, 128], BF16) for i in range(NWST)]
    wcnt = [0]

    NWT = 200
    wscr = nc.dram_tensor("wscr", [NWT, 128, 16 * 128], BF16).ap()
    wcache = {}

    def load_w(Wap, kc, m0, mw=128, r0=0):
        i = wcnt[0] % NWST
        wcnt[0] += 1
        ck = (Wap.name if hasattr(Wap, "name") else id(Wap), kc, m0, mw, r0)
        if ck in wcache:
            idx = wcache[ck]
            S.dma("sp", wst[i][:, 0:kc, 0:mw], wscr[idx, :, 0:kc * mw].rearrange("p (k m) -> p k m", m=mw), reads=[("wscr", idx)], writes=[("wst", i)])
            return wst[i], ("wst", i)
        S.dma("pool", wst[i][:, 0:kc, 0:mw], Wap[r0:r0 + kc * 128, m0:m0 + mw].rearrange("(kc p) m -> p kc m", p=128), writes=[("wst", i)])
        if CACHE_W and len(wcache) < NWT:
            idx = len(wcache)
            wcache[ck] = idx
            S.dma("sp", wscr[idx, :, 0:kc * mw].rearrange("p (k m) -> p k m", m=mw), wst[i][:, 0:kc, 0:mw], reads=[("wst", i)], writes=[("wscr", idx)])
        return wst[i], ("wst", i)

    def mm_acc(pbanks, mw, wbuf, wkey, kc, rhs_fn, rhs_keys, blks, first=True, last=True):
        for k in range(kc):
            for bi, (c0, n) in enumerate(blks):
                S.op("pe", lambda e, k=k, bi=bi, c0=c0, n=n: e.matmul(pbanks[bi][0:mw, 0:n], wbuf[:, k, 0:mw], rhs_fn(k, c0, n), start=(first and k == 0), stop=(last and k == kc - 1)),
                     reads=[wkey] + list(rhs_keys), writes=[("P", pbanks[bi].name)])

    def pk(t):
        return ("P", t.name)

    sqb = S.sb("sqb", [128, 8, 256], BF16)
    rstd = S.sb("rstd", [128, 512], F32)

    def PK(i, r=None):
        return ("P%d" % i,)

    def _blocks256(n):
        return [(c, min(256, n - c)) for c in range(0, n, 256)]

    def rmsnorm(src, skey, gcol, dst, dkey, blks):
        for (c0, n) in blks:
            S.op("act", lambda e, c0=c0, n=n: e.activation(out=sqb[:, :, 0:n], in_=src[:, :, c0:c0 + n], func=AF.Square), reads=[skey], writes=["sqb"])
            for k in range(8):
                S.op("pe", lambda e, k=k, n=n: e.matmul(P[7][:, 0:n], ones_b[:], sqb[:, k, 0:n], start=(k == 0), stop=(k == 7)), reads=["ones_b", "sqb"], writes=[PK(7)])
            S.op("act", lambda e, n=n: e.activation(out=rstd[:, 0:n], in_=P[7][:, 0:n], func=AF.Ln, bias=epsc[:, 0:1], scale=1.0 / D), reads=[PK(7), "epsc"], writes=["rstd"])
            S.op("act", lambda e, n=n: e.activation(out=rstd[:, 0:n], in_=rstd[:, 0:n], func=AF.Exp, scale=-0.5), reads=["rstd"], writes=["rstd"])
            for k in range(8):
                S.op("dve", lambda e, k=k, c0=c0, n=n: e.scalar_tensor_tensor(out=dst[:, k, c0:c0 + n], in0=src[:, k, c0:c0 + n], scalar=gcol[:, k:k + 1], in1=rstd[:, 0:n], op0=ALU.mult, op1=ALU.mult),
                     reads=[skey, "rstd", gcol.name], writes=[dkey])

    class Rot:
        def __init__(self, name, shape, dtype, n=2):
            self.bufs = [S.sb("%s%d" % (name, i), shape, dtype) for i in range(n)]
            self.i = 0
            self.name = name

        def get(self):
            j = self.i % len(self.bufs)
            self.i += 1
            return self.bufs[j], (self.name, j)

    def PK(i, r=None):
        return ("P%d" % i,)

    def mm(pi, mw, wbuf, wkey, kc, rhs_fn, rhs_keys, blks, pbase=0):
        for k in range(kc):
            for bi, (c0, n) in enumerate(blks):
                S.op("pe", lambda e, k=k, bi=bi, c0=c0, n=n: e.matmul(P[pbase + bi][0:mw, 0:n], wbuf[:, k, 0:mw], rhs_fn(k, c0, n), start=(k == 0), stop=(k == kc - 1)),
                     reads=[wkey] + list(rhs_keys), writes=[PK(pbase + bi)])

    xtok = S.sb("xtok", [128, 1024], F32)
    ubuf = S.sb("ubuf", [128, 3 + NCMAX], F32)
    cacc = S.sb("cacc", [128, NCMAX], F32)
    cacc2 = S.sb("cacc2", [128, NCMAX], F32)
    xsT = S.sb("xsT", [128, NCMAX], F32)
    szT = S.sb("szT", [128, NCMAX], F32)
    bmT = S.sb("bmT", [128, NCMAX], BF16)
    cmT = S.sb("cmT", [128, NCMAX], BF16)
    yg = S.sb("yg", [128, 4, NCMAX], F32)
    qT = S.sb("qT", [128, NCMAX], BF16)
    kT = S.sb("kT", [128, NCMAX], BF16)
    vT = S.sb("vT", [128, NCMAX], BF16)
    sgate = S.sb("sgate", [128, NCMAX], F32)
    class NS_:
        pass
    xsT2 = S.sb("xsT2", [128, NCMAX], F32)
    szT2 = S.sb("szT2", [128, NCMAX], F32)
    qT2 = S.sb("qT2", [128, NCMAX], BF16)
    kT2 = S.sb("kT2", [128, NCMAX], BF16)
    vT2 = S.sb("vT2", [128, NCMAX], BF16)
    sgate2 = S.sb("sgate2", [128, NCMAX], F32)
    BSs, BGs = [], []
    for i_, (a_, b_) in enumerate(((xsT, szT), (xsT2, szT2))):
        o_ = NS_(); o_.xsT, o_.szT, o_.kxs, o_.kzs = a_, b_, "xsT%d" % i_, "szT%d" % i_
        BSs.append(o_)
    for i_, (a_, b_, c_, d_) in enumerate(((qT, kT, vT, sgate), (qT2, kT2, vT2, sgate2))):
        o_ = NS_(); o_.qT, o_.kT, o_.vT, o_.sgate = a_, b_, c_, d_
        o_.kq, o_.kk, o_.kv, o_.kg = "qT%d" % i_, "kT%d" % i_, "vT%d" % i_, "sgate%d" % i_
        BGs.append(o_)
    dtT = S.sb("dtT", [32, 2, NCMAX], F32)
    gbT = S.sb("gbT", [8, 2, NCMAX], F32)
    tokS = S.sb("tokS", [128, SEGLEN // 128, 5, 32], F32)
    tokG = S.sb("tokG", [128, NTMAX, 6, 8], F32)
    cbm = S.sb("cbm", [128, SEGLEN // 128, 128], BF16)
    bmtok = S.sb("bmtok", [128, SEGLEN // 128, 128], BF16)
    prevT = S.sb("prevT", [128, 2048], F32)
    prevTb = S.sb("prevTb", [128, 2048], BF16)
    Sg = S.sb("Sg", [128, 8, 128], F32)
    Sgb = S.sb("Sgb", [128, 8, 128], BF16)
    tails = {"ssm": S.sb("tl_ssm", [128, 24, 3], F32), "gdn": S.sb("tl_gdn", [128, 24, 3], F32)}
    S.op("pool", lambda e: e.memset(tails["ssm"][:], 0.0), writes=["tl_ssm"])
    S.op("pool", lambda e: e.memset(tails["gdn"][:], 0.0), writes=["tl_gdn"])
    S.op("pool", lambda e: e.memset(prevT[:], 0.0), writes=["prevT"])
    S.op("pool", lambda e: e.memset(prevTb[:], 0.0), writes=["prevTb"])
    S.op("pool", lambda e: e.memset(Sg[:], 0.0), writes=["Sg"])
    S.op("pool", lambda e: e.memset(Sgb[:], 0.0), writes=["Sgb"])
    S.dma("sp", o_ssmc_s[:, 0:2, :], st_ssm_conv[:, 1:3, :], writes=["o_ssmc_s01"])
    S.dma("sp", o_gdnc_s[:, 0:2, :], st_gdn_conv[:, 1:3, :], writes=["o_gdnc_s01"])

    r_sc_in = Rot("sc_in", [3 * NSMP, 128], F32)
    r_sc = Rot("sc", [128, 3 * NSMP], F32)
    r_tl = Rot("tlo", [3 + NSMP, 128], F32)
    r_f = Rot("tf", [128, 128], F32, 3)
    r_gv = Rot("gv", [128, 128], BF16, 2)
    r_gk = Rot("gk", [128, 128], BF16, 2)
    r_gq = Rot("gq", [128, 128], BF16, 2)
    r_gn = Rot("gn", [128, 128], BF16, 2)
    r_gy = Rot("gy", [128, 128], BF16, 2)
    r_sst = Rot("sst", [128, 128], F32, 1)
    r_sprev = Rot("sprev", [128, 128], F32, 2)
    r_sprevb = Rot("sprevb", [128, 128], BF16, 2)
    r_c1 = Rot("c1", [128, 1], F32, 4)
    class Alias:
        def __init__(self, items):
            self.items = items
            self.i = 0

        def get(self):
            j = self.i % len(self.items)
            self.i += 1
            return self.items[j]

    r_sig = Alias([(yg[:, 0, :], ("yg", 0)), (yg[:, 1, :], ("yg", 1))])
    r_t1 = Alias([(yg[:, 2, :], ("yg", 2)), (yg[:, 3, :], ("yg", 3))])

    conv_w = {"ssm": cw_ssm, "gdn": cw_gdn}
    conv_in = {"ssm": st_ssm_conv, "gdn": st_gdn_conv}
    conv_out_p = {"ssm": o_ssmc_p, "gdn": o_gdnc_p}
    conv_out_s = {"ssm": o_ssmc_s, "gdn": o_gdnc_s}

    def conv_tile(kind, ci, npr, ns, blks, dst, dkey, seg_last):
        cw = conv_w[kind]
        wk = "cw_" + kind
        tl = tails[kind]
        tlk = ("tl_" + kind, ci)
        NC = npr + ns
        for bi, (c0, n) in enumerate(blks):
            S.op("act", lambda e, bi=bi, c0=c0, n=n: e.copy(out=ubuf[:, 3 + c0:3 + c0 + n], in_=P[bi][:, 0:n]), reads=[PK(bi)], writes=[("ubuf", 1 + bi)])
        S.op("pool", lambda e: e.tensor_copy(out=ubuf[:, 0:3], in_=tl[:, ci, :]), reads=[tlk], writes=[("ubuf", 0)])
        if kind == "ssm":
            S.op("dve", lambda e: e.tensor_scalar(out=cacc[:, 0:npr], in0=ubuf[:, 0:npr], scalar1=cw[:, 0, ci:ci + 1], scalar2=cb_ssm[:, ci:ci + 1], op0=ALU.mult, op1=ALU.add), reads=["ubuf", wk, "cb_ssm"], writes=["cacc"])
        else:
            S.op("dve", lambda e: e.tensor_scalar(out=cacc[:, 0:npr], in0=ubuf[:, 0:npr], scalar1=cw[:, 0, ci:ci + 1], scalar2=None, op0=ALU.mult), reads=["ubuf", wk], writes=["cacc"])
        for j in range(1, 4):
            S.op("dve", lambda e, j=j: e.scalar_tensor_tensor(out=cacc[:, 0:npr], in0=ubuf[:, j:j + npr], scalar=cw[:, j, ci:ci + 1], in1=cacc[:, 0:npr], op0=ALU.mult, op1=ALU.add), reads=["ubuf", wk, "cacc"], writes=["cacc"])
        S.op("pool", lambda e: e.tensor_copy(out=tl[:, ci, :], in_=ubuf[:, npr:npr + 3]), reads=["ubuf"], writes=[tlk])
        if ns:
            sci, scik = r_sc_in.get()
            sc, sck = r_sc.get()
            S.dma("sp", sci[:], conv_in[kind].rearrange("b j c -> (b j) c")[:, ci * 128:(ci + 1) * 128], writes=[scik])
            S.op("pe", lambda e: e.transpose(out=P[1][:, 256:256 + 3 * ns], in_=sci[:], identity=ident_f[0:3 * ns, 0:3 * ns]), reads=[scik, "ident_f"], writes=[PK(1)])
            S.op("act", lambda e: e.copy(out=sc[:], in_=P[1][:, 256:256 + 3 * ns]), reads=[PK(1)], writes=[sck])
            scv = sc[:].rearrange("p (b j) -> p j b", j=3)
            sacc = cacc[:, npr:NC]
            if kind == "ssm":
                S.op("dve", lambda e: e.tensor_scalar(out=sacc, in0=scv[:, 0, :], scalar1=cw[:, 0, ci:ci + 1], scalar2=cb_ssm[:, ci:ci + 1], op0=ALU.mult, op1=ALU.add), reads=[sck, wk, "cb_ssm"], writes=["cacc"])
            else:
                S.op("dve", lambda e: e.tensor_scalar(out=sacc, in0=scv[:, 0, :], scalar1=cw[:, 0, ci:ci + 1], scalar2=None, op0=ALU.mult), reads=[sck, wk], writes=["cacc"])
            for j in (1, 2):
                S.op("dve", lambda e, j=j: e.scalar_tensor_tensor(out=sacc, in0=scv[:, j, :], scalar=cw[:, j, ci:ci + 1], in1=sacc, op0=ALU.mult, op1=ALU.add), reads=[sck, wk, "cacc"], writes=["cacc"])
            S.op("dve", lambda e: e.scalar_tensor_tensor(out=sacc, in0=ubuf[:, 3 + npr:3 + NC], scalar=cw[:, 3, ci:ci + 1], in1=sacc, op0=ALU.mult, op1=ALU.add), reads=["ubuf", wk, "cacc"], writes=["cacc"])
        if seg_last:
            nt = 3 + ns
            tlo, tlok = r_tl.get()
            S.op("pe", lambda e: e.transpose(out=P[1][0:nt, 128:256], in_=ubuf[:, npr:npr + nt], identity=ident_f[:]), reads=["ubuf", "ident_f"], writes=[PK(1)])
            S.op("act", lambda e: e.copy(out=tlo[0:nt, :], in_=P[1][0:nt, 128:256]), reads=[PK(1)], writes=[tlok])
            S.dma("sp", conv_out_p[kind][:, ci * 128:(ci + 1) * 128], tlo[0:3, :], reads=[tlok], writes=[("o_cp", kind, ci)])
            if ns:
                S.dma("sp", conv_out_s[kind][:, 2, ci * 128:(ci + 1) * 128], tlo[3:3 + ns, :], reads=[tlok], writes=[("o_cs", kind, ci)])
        S.op("act", lambda e: e.activation(out=dst[:, 0:NC], in_=cacc[:, 0:NC], func=AF.Silu), reads=["cacc"], writes=[dkey])

    def proj_tile(col0, mw, blks):
        wb, wk = load_w(w_in, 8, col0, mw)
        mm(0, mw, wb, wk, 8, lambda k, c0, n: hT[:, k, c0:c0 + n], ["hT"], blks)

    nseg = NP // SEGLEN
    SEGS = [(i * SEGLEN, SEGLEN, NSMP if i == nseg - 1 else 0) for i in range(nseg)]

    def segment(si, t0, npr, ns):
        NC = npr + ns
        blks = _blocks(NC)
        seg_last = (si == len(SEGS) - 1)
        tiles = [(i * 128, 128, None) for i in range(npr // 128)] + [(npr + b, 1, b) for b in range(ns)]
        NT = len(tiles)

        def load_xT(dst, dkey, src_p, src_s, width, nkc):
            for ti in range(npr // 128 + (1 if ns else 0)):
                if ti < npr // 128:
                    rows, c0, src = 128, ti * 128, src_p[t0 + ti * 128:t0 + (ti + 1) * 128, :]
                else:
                    rows, c0, src = ns, npr, src_s[:, :]
                S.dma("sp", xtok[0:rows, 0:width], src, writes=["xtok"])
                for k in range(nkc):
                    S.op("pe", lambda e, k=k, rows=rows: e.transpose(out=P[2 + k % 2][:, (k // 2 % 4) * 128:(k // 2 % 4) * 128 + rows], in_=xtok[0:rows, k * 128:(k + 1) * 128], identity=ident_f[0:rows, 0:rows]),
                         reads=["xtok", "ident_f"], writes=[PK(2 + k % 2, k // 2 % 4)])
                    S.op("act" if k % 2 else "dve", (lambda e, k=k, rows=rows, c0=c0: e.copy(out=dst[:, k, c0:c0 + rows], in_=P[2 + k % 2][:, (k // 2 % 4) * 128:(k // 2 % 4) * 128 + rows])) if k % 2 else
                         (lambda e, k=k, rows=rows, c0=c0: e.tensor_copy(out=dst[:, k, c0:c0 + rows], in_=P[2 + k % 2][:, (k // 2 % 4) * 128:(k // 2 % 4) * 128 + rows])),
                         reads=[PK(2 + k % 2, k // 2 % 4)], writes=[(dkey, k)])

        load_xT(xT, "xT", x_p, x_s, 1024, 8)
        ckpt("loadx")
        rmsnorm(xT, "xT", g_mix, hT, "hT", _blocks256(NC))
        ckpt("norm1")

        proj_tile(OFF_DT, 32, blks)
        for bi, (c0, n) in enumerate(blks):
            S.op("act", lambda e, bi=bi, c0=c0, n=n: e.activation(out=dtT[:, 0, c0:c0 + n], in_=P[bi][0:32, 0:n], func=AF.Exp, bias=hp[:, 0:1], scale=1.0), reads=[PK(bi), "hp"], writes=["dtT"])
        S.op("act", lambda e: e.activation(out=dtT[:, 0, 0:NC], in_=dtT[:, 0, 0:NC], func=AF.Ln, bias=1.0, scale=1.0), reads=["dtT"], writes=["dtT"])
        S.op("dve", lambda e: e.tensor_scalar(out=dtT[:, 1, 0:NC], in0=dtT[:, 0, 0:NC], scalar1=hp[:, 2:3], scalar2=None, op0=ALU.mult), reads=["dtT", "hp"], writes=["dtT"])
        proj_tile(OFF_A, 8, blks)
        for bi, (c0, n) in enumerate(blks):
            S.op("act", lambda e, bi=bi, c0=c0, n=n: e.activation(out=gbT[:, 0, c0:c0 + n], in_=P[bi][0:8, 0:n], func=AF.Exp, bias=hp[0:8, 3:4], scale=1.0), reads=[PK(bi), "hp"], writes=["gbT"])
        S.op("act", lambda e: e.activation(out=gbT[:, 0, 0:NC], in_=gbT[:, 0, 0:NC], func=AF.Ln, bias=1.0, scale=1.0), reads=["gbT"], writes=["gbT"])
        S.op("dve", lambda e: e.tensor_scalar(out=gbT[:, 0, 0:NC], in0=gbT[:, 0, 0:NC], scalar1=hp[0:8, 5:6], scalar2=None, op0=ALU.mult), reads=["gbT", "hp"], writes=["gbT"])
        proj_tile(OFF_B, 8, blks)
        for bi, (c0, n) in enumerate(blks):
            S.op("act", lambda e, bi=bi, c0=c0, n=n: e.activation(out=gbT[:, 1, c0:c0 + n], in_=P[bi][0:8, 0:n], func=AF.Sigmoid), reads=[PK(bi)], writes=["gbT"])

        ckpt("smallproj")
        for ti, (c0, W, sb_) in enumerate(tiles):
            if sb_ is None:
                S.op("pe", lambda e, c0=c0, W=W: e.transpose(out=P[4][0:W, 0:32], in_=dtT[:, 0, c0:c0 + W], identity=ident_f[0:32, 0:32]), reads=["dtT", "ident_f"], writes=[PK(4)])
                S.op("pe", lambda e, c0=c0, W=W: e.transpose(out=P[4][0:W, 32:64], in_=dtT[:, 1, c0:c0 + W], identity=ident_f[0:32, 0:32]), reads=["dtT", "ident_f"], writes=[PK(4)])
                S.op("act", lambda e, ti=ti, W=W: e.copy(out=tokS[0:W, ti, 0:2, :], in_=P[4][0:W, 0:64].rearrange("p (a h) -> p a h", a=2)), reads=[PK(4)], writes=[("tokS", ti)])
                S.op("pe", lambda e, ti=ti, W=W: e.matmul(P[4][0:W, 64:96], mU[0:W, 0:W], tokS[0:W, ti, 1, :], start=True, stop=True), reads=["mU", ("tokS", ti)], writes=[PK(4)])
                S.op("pe", lambda e, ti=ti, W=W: e.matmul(P[4][:, 96:128], ones_f[0:W, :], tokS[0:W, ti, 1, :], start=True, stop=True), reads=["ones_f", ("tokS", ti)], writes=[PK(4)])
                S.op("act", lambda e, ti=ti, W=W: e.activation(out=tokS[0:W, ti, 3, :], in_=P[4][0:W, 64:96], func=AF.Exp), reads=[PK(4)], writes=[("tokS", ti)])
                S.op("act", lambda e, ti=ti: e.activation(out=tokS[:, ti, 4, :], in_=P[4][:, 96:128], func=AF.Exp), reads=[PK(4)], writes=[("tokS", ti)])
                S.op("dve", lambda e, ti=ti, W=W: e.tensor_copy(out=tokS[0:W, ti, 2, :], in_=P[4][0:W, 64:96]), reads=[PK(4)], writes=[("tokS", ti)])
                S.op("dve", lambda e, ti=ti, W=W: e.tensor_tensor(out=tokS[0:W, ti, 2, :], in0=P[4][0:W, 96:128], in1=tokS[0:W, ti, 2, :], op=ALU.subtract), reads=[PK(4), ("tokS", ti)], writes=[("tokS", ti)])
                S.op("act", lambda e, ti=ti, W=W: e.activation(out=tokS[0:W, ti, 2, :], in_=tokS[0:W, ti, 2, :], func=AF.Exp), reads=[("tokS", ti)], writes=[("tokS", ti)])
                S.op("dve", lambda e, ti=ti, W=W: e.tensor_tensor(out=tokS[0:W, ti, 2, :], in0=tokS[0:W, ti, 2, :], in1=tokS[0:W, ti, 0, :], op=ALU.mult), reads=[("tokS", ti)], writes=[("tokS", ti)])
            S.op("pe", lambda e, c0=c0, W=W: e.transpose(out=P[4][0:W, 128:136], in_=gbT[:, 0, c0:c0 + W], identity=ident_f[0:8, 0:8]), reads=["gbT", "ident_f"], writes=[PK(4)])
            S.op("pe", lambda e, c0=c0, W=W: e.transpose(out=P[4][0:W, 136:144], in_=gbT[:, 1, c0:c0 + W], identity=ident_f[0:8, 0:8]), reads=["gbT", "ident_f"], writes=[PK(4)])
            S.op("act", lambda e, ti=ti, W=W: e.copy(out=tokG[0:W, ti, 0:2, :], in_=P[4][0:W, 128:144].rearrange("p (a h) -> p a h", a=2)), reads=[PK(4)], writes=[("tokG", ti)])
            S.op("pe", lambda e, ti=ti, W=W: e.matmul(P[4][0:W, 144:152], mU[0:W, 0:W], tokG[0:W, ti, 0, :], start=True, stop=True), reads=["mU", ("tokG", ti)], writes=[PK(4)])
            S.op("pe", lambda e, ti=ti, W=W: e.matmul(P[4][:, 152:160], ones_f[0:W, :], tokG[0:W, ti, 0, :], start=True, stop=True), reads=["ones_f", ("tokG", ti)], writes=[PK(4)])
            S.op("act", lambda e, ti=ti, W=W: e.activation(out=tokG[0:W, ti, 2, :], in_=P[4][0:W, 144:152], func=AF.Exp), reads=[PK(4)], writes=[("tokG", ti)])
            S.op("dve", lambda e, ti=ti, W=W: e.tensor_scalar(out=tokG[0:W, ti, 3, :], in0=tokG[0:W, ti, 2, :], scalar1=-1.0, scalar2=None, op0=ALU.mult), reads=[("tokG", ti)], writes=[("tokG", ti)])
            S.op("act", lambda e, ti=ti: e.activation(out=tokG[:, ti, 5, :], in_=P[4][:, 152:160], func=AF.Exp), reads=[PK(4)], writes=[("tokG", ti)])
            S.op("dve", lambda e, ti=ti, W=W: e.tensor_copy(out=tokG[0:W, ti, 4, :], in_=P[4][0:W, 144:152]), reads=[PK(4)], writes=[("tokG", ti)])
            S.op("dve", lambda e, ti=ti, W=W: e.tensor_tensor(out=tokG[0:W, ti, 4, :], in0=P[4][0:W, 152:160], in1=tokG[0:W, ti, 4, :], op=ALU.subtract), reads=[PK(4), ("tokG", ti)], writes=[("tokG", ti)])
            S.op("act", lambda e, ti=ti, W=W: e.activation(out=tokG[0:W, ti, 4, :], in_=tokG[0:W, ti, 4, :], func=AF.Exp), reads=[("tokG", ti)], writes=[("tokG", ti)])

        ckpt("tokscal")
        def ssd_A(j):
            BS = BSs[j % 2]
            proj_tile(OFF_XBC + j * 128, 128, blks)
            conv_tile("ssm", j, npr, ns, blks, BS.xsT, BS.kxs, seg_last)
            proj_tile(OFF_Z + j * 128, 128, blks)
            for bi, (c0, n) in enumerate(blks):
                S.op("act", lambda e, bi=bi, c0=c0, n=n, BS=BS: e.activation(out=BS.szT[:, c0:c0 + n], in_=P[bi][:, 0:n], func=AF.Silu), reads=[PK(bi)], writes=[BS.kzs])

        S.replay(S.capture(lambda: ssd_A(0)))
        for g in range(4):
            proj_tile(OFF_XBC + 2048 + g * 128, 128, blks)
            conv_tile("ssm", 16 + g, npr, ns, blks, bmT, "bmT", seg_last)
            proj_tile(OFF_XBC + 2560 + g * 128, 128, blks)
            conv_tile("ssm", 20 + g, npr, ns, blks, cmT, "cmT", seg_last)
            if ns:
                ssd_samples_group(npr, ns)
            for ti, (c0, W, sb_) in enumerate(tiles[:npr // 128]):
                S.op("pe", lambda e, c0=c0, W=W: e.transpose(out=P[2].bitcast(BF16)[0:W, 0:128], in_=bmT[:, c0:c0 + W], identity=ident_b[:]), reads=["bmT", "ident_b"], writes=[PK(2)])
                S.op("act", lambda e, ti=ti, W=W: e.copy(out=bmtok[0:W, ti, :], in_=P[2].bitcast(BF16)[0:W, 0:128]), reads=[PK(2)], writes=[("bmtok", ti)])
                S.op("pe", lambda e, c0=c0, W=W: e.matmul(P[3][0:W, 384:384 + W], bmT[:, c0:c0 + W], cmT[:, c0:c0 + W], start=True, stop=True), reads=["bmT", "cmT"], writes=[PK(3, 3)])
                S.op("dve", lambda e, ti=ti, W=W: e.tensor_tensor(out=cbm[0:W, ti, 0:W], in0=P[3][0:W, 384:384 + W], in1=mU[0:W, 0:W], op=ALU.mult), reads=[PK(3, 3), "mU"], writes=[("cbm", ti)])
            ckpt("ssd_bc")
            for jj in range(4):
                j = 4 * g + jj
                ca = S.capture(lambda: ssd_A(j + 1)) if j + 1 < 16 else []

                def _b():
                    ssd_pair_prompt(g, jj, j, npr, BSs[j % 2])
                    if ns:
                        ssd_samples_pair(jj, j, npr, ns, BSs[j % 2])
                cb = S.capture(_b)
                S.replay(S.interleave(ca, cb, front=0.7))
            ckpt("ssd_steps")
            for (c0, n) in _blocks256(NC):
                S.op("act", lambda e, c0=c0, n=n: e.activation(out=sqb[:, 0:4, 0:n], in_=yg[:, :, c0:c0 + n], func=AF.Square), reads=["yg"], writes=["sqb"])
                for k in range(4):
                    S.op("pe", lambda e, k=k, n=n: e.matmul(P[7][:, 0:n], ones_b[:], sqb[:, k, 0:n], start=(k == 0), stop=(k == 3)), reads=["ones_b", "sqb"], writes=[PK(7)])
                S.op("act", lambda e, n=n: e.activation(out=rstd[:, 0:n], in_=P[7][:, 0:n], func=AF.Ln, bias=epsc[:, 0:1], scale=1.0 / 512), reads=[PK(7), "epsc"], writes=["rstd"])
                S.op("act", lambda e, n=n: e.activation(out=rstd[:, 0:n], in_=rstd[:, 0:n], func=AF.Exp, scale=-0.5), reads=["rstd"], writes=["rstd"])
                for k in range(4):
                    S.op("dve", lambda e, k=k, c0=c0, n=n, g=g: e.scalar_tensor_tensor(out=yT[:, 4 * g + k, c0:c0 + n], in0=yg[:, k, c0:c0 + n], scalar=g_ssm[:, 4 * g + k:4 * g + k + 1], in1=rstd[:, 0:n], op0=ALU.mult, op1=ALU.mult),
                         reads=["yg", "rstd", "g_ssm"], writes=[("yT", 4 * g + k)])

        ckpt("ssd_done")
        scr, kscr = cacc2, "cacc2"

        def gdn_A(h):
            BG = BGs[h % 2]
            for (nm, off, dst, dk_) in (("q", 0, BG.qT, BG.kq), ("k", 1024, BG.kT, BG.kk), ("v", 2048, BG.vT, BG.kv)):
                proj_tile(OFF_QKV + off + h * 128, 128, blks)
                if nm == "v":
                    conv_tile("gdn", (off // 128) + h, npr, ns, blks, dst, dk_, seg_last)
                    continue
                conv_tile("gdn", (off // 128) + h, npr, ns, blks, scr, kscr, seg_last)
                for (c0, n) in _blocks256(NC):
                    S.op("act", lambda e, c0=c0, n=n: e.activation(out=sqb[:, 0, 0:n], in_=scr[:, c0:c0 + n], func=AF.Square), reads=[kscr], writes=["sqb"])
                    S.op("pe", lambda e, n=n: e.matmul(P[1][:, 256:256 + n], ones_b[:], sqb[:, 0, 0:n], start=True, stop=True), reads=["ones_b", "sqb"], writes=[PK(1)])
                    S.op("act", lambda e, n=n: e.activation(out=rstd[:, 0:n], in_=P[1][:, 256:256 + n], func=AF.Ln, bias=epsc[:, 0:1], scale=1.0), reads=[PK(1), "epsc"], writes=["rstd"])
                    S.op("act", lambda e, n=n: e.activation(out=rstd[:, 0:n], in_=rstd[:, 0:n], func=AF.Exp, scale=-0.5), reads=["rstd"], writes=["rstd"])
                    if nm == "q":
                        S.op("dve", lambda e, c0=c0, n=n, dst=dst: e.scalar_tensor_tensor(out=dst[:, c0:c0 + n], in0=scr[:, c0:c0 + n], scalar=128.0 ** -0.5, in1=rstd[:, 0:n], op0=ALU.mult, op1=ALU.mult), reads=[kscr, "rstd"], writes=[dk_])
                    else:
                        S.op("dve", lambda e, c0=c0, n=n, dst=dst: e.tensor_tensor(out=dst[:, c0:c0 + n], in0=scr[:, c0:c0 + n], in1=rstd[:, 0:n], op=ALU.mult), reads=[kscr, "rstd"], writes=[dk_])
            proj_tile(OFF_GATE + h * 128, 128, blks)
            for bi, (c0, n) in enumerate(blks):
                S.op("act", lambda e, bi=bi, c0=c0, n=n, BG=BG: e.activation(out=BG.sgate[:, c0:c0 + n], in_=P[bi][:, 0:n], func=AF.Silu), reads=[PK(bi)], writes=[BG.kg])

        S.replay(S.capture(lambda: gdn_A(0)))
        for h in range(8):
            ca = S.capture(lambda: gdn_A(h + 1)) if h + 1 < 8 else []
            BG = BGs[h % 2]

            cbp = S.capture(lambda: gdn_head_prompt(h, npr, BG))

            def _bs():
                for ti, (c0, W, sb_) in enumerate(tiles):
                    if sb_ is not None:
                        gdn_step(h, ti, c0, W, sb_, BG)
                if ns:
                    gdn_samples_norm(h, npr, ns, BG)
            cbs = S.capture(_bs)
            S.replay(S.interleave(S.interleave(ca, cbp, front=0.7), cbs))

        ckpt("gdn_done")
        phase2(si, t0, npr, ns, NC, blks, load_xT)
        ckpt("seg_done")

    def decay_mat(ti, W, col_ap, colkey, pslot):
        tf, tfk = r_f.get()
        S.op("pool", lambda e: e.tensor_scalar(out=tf[0:W, 0:W], in0=mU[0:W, 0:W], scalar1=col_ap, scalar2=None, op0=ALU.mult), reads=["mU", colkey], writes=[tfk])
        ckpt("ssdA3")
        S.op("pe", lambda e: e.matmul(P[3][0:W, pslot * 128:pslot * 128 + W], mT[0:W, 0:W], tf[0:W, 0:W], start=True, stop=False), reads=["mT", tfk], writes=[PK(3, pslot)])
        ckpt("ssdA4")
        S.op("pe", lambda e: e.matmul(P[3][0:W, pslot * 128:pslot * 128 + W], ident_f[0:W, 0:W], mNEG[0:W, 0:W], start=False, stop=True), reads=["ident_f", "mNEG"], writes=[PK(3, pslot)])


    P2b = P[2].bitcast(BF16)
    P7b = P[7].bitcast(BF16)
    bdB = S.sb("bdB", [NSMP, NSMP, 128], BF16)
    bdC = S.sb("bdC", [NSMP, NSMP, 128], BF16)
    esel = S.sb("esel", [32, 2, 64], F32)
    r_s0 = Rot("s0c", [128, 4, 128], F32, 2)
    r_stmp = Rot("stmp", [128, 4, 128], F32, 1)
    r_sm = Rot("ssm16", [128, NSMP], F32, 4)
    r_xtk = Rot("xtk", [NSMP, 128], BF16, 2)

    def ssd_samples_group(npr, ns):
        cs = slice(npr, npr + ns)
        for (srcT, skey, dst, dkey) in ((bmT, "bmT", bdB, "bdB"), (cmT, "cmT", bdC, "bdC")):
            tk_, tkk = r_xtk.get()
            S.op("pe", lambda e, srcT=srcT: e.transpose(out=P2b[0:ns, 0:128], in_=srcT[:, cs], identity=ident_b[:]), reads=[skey, "ident_b"], writes=[PK(2)])
            S.op("act", lambda e, tk_=tk_: e.copy(out=tk_[0:ns, :], in_=P2b[0:ns, 0:128]), reads=[PK(2)], writes=[tkk])
            S.op("pool", lambda e, tk_=tk_, dst=dst: e.tensor_tensor(out=dst[0:ns, 0:ns, :], in0=tk_[0:ns, :].unsqueeze(1).to_broadcast([ns, ns, 128]), in1=ident_b[0:ns, 0:ns].unsqueeze(2).to_broadcast([ns, ns, 128]), op=ALU.mult), reads=[tkk, "ident_b"], writes=[dkey])

    def ssd_samples_pair(jj, j, npr, ns, BS):
        cs = slice(npr, npr + ns)
        h0 = 2 * j
        S.op("pool", lambda e: e.memset(esel[:], 1.0), writes=["esel"])
        S.op("pool", lambda e: e.affine_select(out=esel[:], in_=esel[:], pattern=[[1, 2], [0, 64]], compare_op=ALU.is_equal, fill=0.0, base=h0, channel_multiplier=-1), reads=["esel"], writes=["esel"])
        ev = esel[:].rearrange("h a p -> h (a p)")
        S.op("pe", lambda e: e.matmul(P[4][:, 0:ns], ev, dtT[:, 0, cs], start=True, stop=True), reads=["esel", "dtT"], writes=[PK(4)])
        S.op("pe", lambda e: e.matmul(P[4][:, 32:32 + ns], ev, dtT[:, 1, cs], start=True, stop=True), reads=["esel", "dtT"], writes=[PK(4)])
        S.op("pe", lambda e: e.matmul(P[4][:, 64:65], ev, hp[:, 6:7], start=True, stop=True), reads=["esel", "hp"], writes=[PK(4)])
        xdtE, xdtEk = r_sm.get()
        decE, decEk = r_sm.get()
        dE, dEk = r_c1.get()
        ysum, ysumk = r_sm.get()
        S.op("dve", lambda e: e.tensor_tensor(out=xdtE[:, 0:ns], in0=P[4][:, 0:ns], in1=BS.xsT[:, cs], op=ALU.mult), reads=[PK(4), BS.kxs], writes=[xdtEk])
        S.op("act", lambda e: e.activation(out=decE[:, 0:ns], in_=P[4][:, 32:32 + ns], func=AF.Exp), reads=[PK(4)], writes=[decEk])
        S.op("act", lambda e: e.copy(out=dE[:, 0:1], in_=P[4][:, 64:65]), reads=[PK(4)], writes=[dEk])
        xtk, xtkk = r_xtk.get()
        S.op("pe", lambda e: e.transpose(out=P[2][0:ns, 128:256], in_=xdtE[:, 0:ns], identity=ident_f[:]), reads=[xdtEk, "ident_f"], writes=[PK(2)])
        S.op("act", lambda e: e.copy(out=xtk[0:ns, :], in_=P[2][0:ns, 128:256]), reads=[PK(2)], writes=[xtkk])
        nch = (ns + 3) // 4
        for c in range(nch):
            b0 = 4 * c
            nb = min(4, ns - b0)
            s0c, s0k = r_s0.get()
            tmp, tmpk = r_stmp.get()
            S.dma("sp", s0c[:, 0:nb, :], st_ssm[b0:b0 + nb, j * 128:(j + 1) * 128, :].rearrange("b p d -> p b d"), writes=[s0k])
            S.op("pe", lambda e, b0=b0, nb=nb: e.matmul(P[5][:, 0:nb * 128], xtk[0:ns, :], bdB[0:ns, b0:b0 + nb, :].rearrange("k b d -> k (b d)"), start=True, stop=True), reads=[xtkk, "bdB"], writes=[PK(5)])
            S.op("pe", lambda e, b0=b0, nb=nb: e.matmul(P[6][:, 0:nb * 128], ones_b[0:ns, :], bdC[0:ns, b0:b0 + nb, :].rearrange("k b d -> k (b d)"), start=True, stop=True), reads=["ones_b", "bdC"], writes=[PK(6)])
            S.op("pool", lambda e, s0c=s0c, b0=b0, nb=nb: e.tensor_tensor(out=s0c[:, 0:nb, :], in0=s0c[:, 0:nb, :], in1=decE[:, b0:b0 + nb].unsqueeze(2).to_broadcast([128, nb, 128]), op=ALU.mult), reads=[s0k, decEk], writes=[s0k])
            S.op("dve", lambda e, s0c=s0c, nb=nb: e.tensor_tensor(out=s0c[:, 0:nb, :], in0=s0c[:, 0:nb, :], in1=P[5][:, 0:nb * 128].rearrange("p (b d) -> p b d", d=128), op=ALU.add), reads=[s0k, PK(5)], writes=[s0k])
            S.dma("sp", o_ssm_s[b0:b0 + nb, j * 128:(j + 1) * 128, :].rearrange("b p d -> p b d"), s0c[:, 0:nb, :], reads=[s0k], writes=[("o_ssm_s", j, c)])
            S.op("dve", lambda e, s0c=s0c, tmp=tmp, nb=nb: e.tensor_tensor(out=tmp[:, 0:nb, :], in0=s0c[:, 0:nb, :], in1=P[6][:, 0:nb * 128].rearrange("p (b d) -> p b d", d=128), op=ALU.mult), reads=[s0k, PK(6)], writes=[tmpk])
            S.op("dve", lambda e, tmp=tmp, b0=b0, nb=nb: e.tensor_reduce(out=ysum[:, b0:b0 + nb], in_=tmp[:, 0:nb, :], axis=AX.X, op=ALU.add), reads=[tmpk], writes=[ysumk])
        S.op("dve", lambda e: e.scalar_tensor_tensor(out=ysum[:, 0:ns], in0=BS.xsT[:, cs], scalar=dE[:, 0:1], in1=ysum[:, 0:ns], op0=ALU.mult, op1=ALU.add), reads=[BS.kxs, dEk, ysumk], writes=[ysumk])
        S.op("dve", lambda e: e.tensor_tensor(out=yg[:, jj, cs], in0=ysum[:, 0:ns], in1=BS.szT[:, cs], op=ALU.mult), reads=[ysumk, BS.kzs], writes=[("yg", jj)])

    def ssd_step(g, jj, j, ti, c0, W, sb_):
        h0 = 2 * j
        tk = ("tokS", ti)
        if sb_ is None:
            st_f, stk_f = prevT[:, j * 128:(j + 1) * 128], ("prevT", j)
            st_b, stk_b = prevTb[:, j * 128:(j + 1) * 128], ("prevTb", j)
        else:
            sst, sstk = r_sst.get()
            spv, stk_f = r_sprev.get()
            spb, stk_b = r_sprevb.get()
            S.dma("sp", sst[:], st_ssm[sb_, j * 128:(j + 1) * 128, :], writes=[sstk])
            ckpt("ssdS1")
            S.op("pe", lambda e: e.transpose(out=P[6][:, 128:256], in_=sst[:], identity=ident_f[:]), reads=[sstk, "ident_f"], writes=[PK(6, 1)])
            ckpt("ssdS2")
            S.op("act", lambda e: e.copy(out=spv[:], in_=P[6][:, 128:256]), reads=[PK(6, 1)], writes=[stk_f])
            ckpt("ssdS3")
            S.op("dve", lambda e: e.tensor_copy(out=spb[:], in_=P[6][:, 128:256]), reads=[PK(6, 1)], writes=[stk_b])
            ckpt("ssdS4")
            st_f, st_b = spv[:], spb[:]
        S.op("pe", lambda e: e.transpose(out=P[2][0:W, 128:256], in_=xsT[:, c0:c0 + W], identity=ident_f[:]), reads=["xsT", "ident_f"], writes=[PK(2, 1)])
        ckpt("ssdA")
        xsk, xskk = r_f.get()
        xdt, xdtk = r_b.get()
        xdd, xddk = r_b.get()
        S.op("act", lambda e: e.copy(out=xsk[0:W, :], in_=P[2][0:W, 128:256]), reads=[PK(2, 1)], writes=[xskk])
        ckpt("ssdA1")
        for hh in range(2):
            h = h0 + hh
            S.op("act", lambda e, hh=hh, h=h: e.activation(out=xdt[0:W, hh * 64:(hh + 1) * 64], in_=P[2][0:W, 128 + hh * 64:128 + (hh + 1) * 64], func=AF.Copy, scale=tokS[0:W, ti, 0, h:h + 1]), reads=[PK(2, 1), tk], writes=[xdtk])
            S.op("act", lambda e, hh=hh, h=h: e.activation(out=xdd[0:W, hh * 64:(hh + 1) * 64], in_=P[2][0:W, 128 + hh * 64:128 + (hh + 1) * 64], func=AF.Copy, scale=tokS[0:W, ti, 2, h:h + 1]), reads=[PK(2, 1), tk], writes=[xddk])
            ckpt("ssdA2")
            decay_mat(ti, W, tokS[0:W, ti, 1, h:h + 1], tk, hh)
        ckpt("ssdB")
        Lh, Lhk = r_f2.get()
        MT, MTk = r_b2.get()
        S.op("act", lambda e: e.activation(out=Lh[0:W, :, 0:W], in_=P[3][0:W, 0:256].rearrange("p (a l) -> p a l", a=2)[:, :, 0:W], func=AF.Exp), reads=[PK(3, 0), PK(3, 1)], writes=[Lhk])
        S.op("dve", lambda e: e.tensor_tensor(out=MT[0:W, :, 0:W], in0=Lh[0:W, :, 0:W], in1=cbm[0:W, ti, 0:W].unsqueeze(1).to_broadcast([W, 2, W]), op=ALU.mult), reads=[Lhk, ("cbm", ti)], writes=[MTk])
        ckpt("ssdC")
        for hh in range(2):
            S.op("pe", lambda e, hh=hh: e.matmul(P[5][0:W, hh * 64:(hh + 1) * 64], MT[0:W, hh, 0:W], xdt[0:W, hh * 64:(hh + 1) * 64], start=True, stop=True), reads=[MTk, xdtk], writes=[PK(5, 0)])
        S.op("pe", lambda e: e.matmul(P[5][0:W, 128:256], cmT[:, c0:c0 + W], st_b, start=True, stop=True), reads=["cmT", stk_b], writes=[PK(5, 1)])
        ckpt("ssdD")
        t1, t1k = r_f.get()
        ys, ysk = r_f.get()
        for hh in range(2):
            h = h0 + hh
            S.op("act", lambda e, hh=hh, h=h: e.activation(out=t1[0:W, hh * 64:(hh + 1) * 64], in_=P[5][0:W, 128 + hh * 64:128 + (hh + 1) * 64], func=AF.Copy, scale=tokS[0:W, ti, 3, h:h + 1]), reads=[PK(5, 1), tk], writes=[t1k])
        ckpt("ssdE")
        S.op("dve", lambda e: e.tensor_tensor(out=ys[0:W, :], in0=P[5][0:W, 0:128], in1=t1[0:W, :], op=ALU.add), reads=[PK(5, 0), t1k], writes=[ysk])
        for hh in range(2):
            h = h0 + hh
            S.op("dve", lambda e, hh=hh, h=h: e.scalar_tensor_tensor(out=ys[0:W, hh * 64:(hh + 1) * 64], in0=xsk[0:W, hh * 64:(hh + 1) * 64], scalar=dskB[0:W, h:h + 1], in1=ys[0:W, hh * 64:(hh + 1) * 64], op0=ALU.mult, op1=ALU.add), reads=[xskk, "dskB", ysk], writes=[ysk])
        ckpt("ssdF")
        S.op("pe", lambda e: e.transpose(out=P[2][:, 256:256 + W], in_=ys[0:W, :], identity=ident_f[0:W, 0:W]), reads=[ysk, "ident_f"], writes=[PK(2, 2)])
        S.op("dve", lambda e: e.tensor_tensor(out=yg[:, jj, c0:c0 + W], in0=P[2][:, 256:256 + W], in1=szT[:, c0:c0 + W], op=ALU.mult), reads=[PK(2, 2), "szT"], writes=[("yg", jj)])
        ckpt("ssdG")
        S.op("pe", lambda e: e.matmul(P[6][:, 0:128], bmtok[0:W, ti, :], xdd[0:W, :], start=True, stop=True), reads=[("bmtok", ti), xddk], writes=[PK(6, 0)])
        for hh in range(2):
            h = h0 + hh
            S.op("dve", lambda e, hh=hh, h=h: e.scalar_tensor_tensor(out=st_f[:, hh * 64:(hh + 1) * 64], in0=st_f[:, hh * 64:(hh + 1) * 64], scalar=tokS[:, ti, 4, h:h + 1], in1=P[6][:, hh * 64:(hh + 1) * 64], op0=ALU.mult, op1=ALU.add), reads=[stk_f, tk, PK(6, 0)], writes=[stk_f])
        if sb_ is None:
            S.op("act", lambda e: e.copy(out=st_b, in_=st_f), reads=[stk_f], writes=[stk_b])
        else:
            so, sok = r_sst.get()
            S.op("pe", lambda e: e.transpose(out=P[6][:, 256:384], in_=st_f, identity=ident_f[:]), reads=[stk_f, "ident_f"], writes=[PK(6, 2)])
            S.op("act", lambda e: e.copy(out=so[:], in_=P[6][:, 256:384]), reads=[PK(6, 2)], writes=[sok])
            S.dma("sp", o_ssm_s[sb_, j * 128:(j + 1) * 128, :], so[:], reads=[sok], writes=[("o_ssm_s", sb_, j)])

    def gdn_step(h, ti, c0, W, sb_, BG):
        tk = ("tokG", ti)
        if sb_ is None:
            st_f, stk_f = Sg[:, h, :], ("Sg", h)
            st_b, stk_b = Sgb[:, h, :], ("Sgb", h)
        else:
            spv, stk_f = r_sprev.get()
            spb, stk_b = r_sprevb.get()
            S.dma("sp", spv[:], st_gdn[sb_, h], writes=[stk_f])
            S.op("pool", lambda e: e.tensor_copy(out=spb[:], in_=spv[:]), reads=[stk_f], writes=[stk_b])
            st_f, st_b = spv[:], spb[:]
        vtok, vtokk = r_gv.get()
        kgk, kgkk = r_gk.get()
        S.op("pe", lambda e: e.transpose(out=P7b[0:W, 0:128], in_=BG.kT[:, c0:c0 + W], identity=ident_b[:]), reads=[BG.kk, "ident_b"], writes=[PK(7)])
        S.op("pe", lambda e: e.transpose(out=P7b[0:W, 128:256], in_=BG.vT[:, c0:c0 + W], identity=ident_b[:]), reads=[BG.kv, "ident_b"], writes=[PK(7)])
        S.op("act", lambda e: e.copy(out=vtok[0:W, :], in_=P7b[0:W, 128:256]), reads=[PK(7)], writes=[vtokk])
        ckpt("g1")
        S.op("act", lambda e: e.activation(out=kgk[0:W, :], in_=P7b[0:W, 0:128], func=AF.Copy, scale=tokG[0:W, ti, 4, h:h + 1]), reads=[PK(7), tk], writes=[kgkk])
        ckpt("g2")
        S.op("pe", lambda e: e.matmul(P[7][0:W, 128:256], BG.kT[:, c0:c0 + W], st_b, start=True, stop=True), reads=[BG.kk, stk_b], writes=[PK(7)])
        Y, Yk = r_f.get()
        Yb, Ybk = r_gy.get()
        S.op("dve", lambda e, Y=Y: e.scalar_tensor_tensor(out=Y[0:W, :], in0=P[7][0:W, 128:256], scalar=tokG[0:W, ti, 3, h:h + 1], in1=vtok[0:W, :], op0=ALU.mult, op1=ALU.add), reads=[PK(7), tk, vtokk], writes=[Yk])
        if W > 1:
            S.op("act", lambda e, Y=Y, Yb=Yb: e.copy(out=Yb[0:W, :], in_=Y[0:W, :]), reads=[Yk], writes=[Ybk])
        ckpt("g3")
        if W > 1:
            decay_mat(ti, W, tokG[0:W, ti, 0, h:h + 1], tk, 2)
            Dge, Dgek = r_f.get()
            S.op("act", lambda e: e.activation(out=Dge[0:W, 0:W], in_=P[3][0:W, 256:256 + W], func=AF.Exp), reads=[PK(3, 2)], writes=[Dgek])
        S.op("pe", lambda e: e.matmul(P[7][0:W, 256:256 + W], BG.kT[:, c0:c0 + W], BG.qT[:, c0:c0 + W], start=True, stop=True), reads=[BG.kk, BG.kq], writes=[PK(7)])
        qkT, qkTk = r_gq.get()
        if W > 1:
            S.op("dve", lambda e: e.tensor_tensor(out=qkT[0:W, 0:W], in0=P[7][0:W, 256:256 + W], in1=Dge[0:W, 0:W], op=ALU.mult), reads=[PK(7), Dgek], writes=[qkTk])
        else:
            S.op("dve", lambda e: e.tensor_copy(out=qkT[0:W, 0:W], in_=P[7][0:W, 256:256 + W]), reads=[PK(7)], writes=[qkTk])
        ckpt("g4")
        if W > 1:
            S.op("pe", lambda e: e.matmul(P[4][0:W, 0:W], BG.kT[:, c0:c0 + W], BG.kT[:, c0:c0 + W], start=True, stop=True), reads=[BG.kk], writes=[PK(4, 0)])
            tmp, tmpk = r_f.get()
            S.op("dve", lambda e: e.scalar_tensor_tensor(out=tmp[0:W, 0:W], in0=P[4][0:W, 0:W], scalar=tokG[0:W, ti, 1, h:h + 1], in1=Dge[0:W, 0:W], op0=ALU.mult, op1=ALU.mult), reads=[PK(4, 0), tk, Dgek], writes=[tmpk])
            MTa, MTak = r_gm.get()
            Ma, Mak = r_gm.get()
            S.op("pool", lambda e: e.tensor_tensor(out=MTa[0:W, 0:W], in0=tmp[0:W, 0:W], in1=mUS[0:W, 0:W], op=ALU.mult), reads=[tmpk, "mUS"], writes=[MTak])
            S.op("pe", lambda e: e.transpose(out=P7b[0:W, 768:768 + W], in_=MTa[0:W, 0:W], identity=ident_b[0:W, 0:W]), reads=[MTak, "ident_b"], writes=[PK(7)])
            S.op("act", lambda e: e.copy(out=Ma[0:W, 0:W], in_=P7b[0:W, 768:768 + W]), reads=[PK(7)], writes=[Mak])
            ckpt("g5")
            nlev = 0
            while (1 << nlev) < W:
                nlev += 1
            T, Tk = r_b.get()
            TT, TTk = r_b.get()
            L0, L0k = r_b.get()
            N0, N0k = r_b.get()
            S.op("pool", lambda e, L0=L0: e.tensor_tensor(out=L0[0:W, 0:W], in0=Ma[0:W, 0:W], in1=cmask[0:W, 0, 0:W], op=ALU.mult), reads=[Mak, "cmask"], writes=[L0k])
            S.op("pool", lambda e, N0=N0: e.tensor_tensor(out=N0[0:W, 0:W], in0=MTa[0:W, 0:W], in1=cmask[0:W, 7, 0:W], op=ALU.mult), reads=[MTak, "cmask"], writes=[N0k])
            S.op("pool", lambda e, T=T, L0=L0: e.tensor_tensor(out=T[0:W, 0:W], in0=ident_b[0:W, 0:W], in1=L0[0:W, 0:W], op=ALU.subtract), reads=["ident_b", L0k], writes=[Tk])
            S.op("pool", lambda e, TT=TT, N0=N0: e.tensor_tensor(out=TT[0:W, 0:W], in0=ident_b[0:W, 0:W], in1=N0[0:W, 0:W], op=ALU.subtract), reads=["ident_b", N0k], writes=[TTk])
            ckpt("g6")
            for lv in range(1, nlev):
                last = (lv == nlev - 1)
                Lj, Ljk = r_b.get()
                S.op("pool", lambda e, Lj=Lj, lv=lv: e.tensor_tensor(out=Lj[0:W, 0:W], in0=Ma[0:W, 0:W], in1=cmask[0:W, lv, 0:W], op=ALU.mult), reads=[Mak, "cmask"], writes=[Ljk])
                Ub, Ubk = r_b.get()
                S.op("pe", lambda e, Lj=Lj, TT=TT: e.matmul(P[5][0:W, 384:384 + W], Lj[0:W, 0:W], TT[0:W, 0:W], start=True, stop=True), reads=[Ljk, TTk], writes=[PK(5, 3)])
                S.op("act", lambda e, Ub=Ub: e.copy(out=Ub[0:W, 0:W], in_=P[5][0:W, 384:384 + W]), reads=[PK(5, 3)], writes=[Ubk])
                S.op("pe", lambda e, T=T, Ub=Ub: e.matmul(P[6][0:W, 128:128 + W], T[0:W, 0:W], Ub[0:W, 0:W], start=True, stop=True), reads=[Tk, Ubk], writes=[PK(6, 1)])
                TTn, TTnk = r_b.get()
                if not last:
                    Nj, Njk = r_b.get()
                    S.op("pool", lambda e, Nj=Nj, lv=lv: e.tensor_tensor(out=Nj[0:W, 0:W], in0=MTa[0:W, 0:W], in1=cmask[0:W, 7 + lv, 0:W], op=ALU.mult), reads=[MTak, "cmask"], writes=[Njk])
                    Zb, Zbk = r_b.get()
                    S.op("pe", lambda e, Nj=Nj, T=T: e.matmul(P[4][0:W, 256:256 + W], Nj[0:W, 0:W], T[0:W, 0:W], start=True, stop=True), reads=[Njk, Tk], writes=[PK(4, 2)])
                    S.op("act", lambda e, Zb=Zb: e.copy(out=Zb[0:W, 0:W], in_=P[4][0:W, 256:256 + W]), reads=[PK(4, 2)], writes=[Zbk])
                    S.op("pe", lambda e, TT=TT, Zb=Zb: e.matmul(P[3][0:W, 0:W], TT[0:W, 0:W], Zb[0:W, 0:W], start=True, stop=True), reads=[TTk, Zbk], writes=[PK(3, 0)])
                    Tn, Tnk = r_b.get()
                    S.op("dve", lambda e, T=T, Tn=Tn: e.tensor_tensor(out=Tn[0:W, 0:W], in0=T[0:W, 0:W], in1=P[3][0:W, 0:W], op=ALU.subtract), reads=[Tk, PK(3, 0)], writes=[Tnk])
                S.op("dve", lambda e, TT=TT, TTn=TTn: e.tensor_tensor(out=TTn[0:W, 0:W], in0=TT[0:W, 0:W], in1=P[6][0:W, 128:128 + W], op=ALU.subtract), reads=[TTk, PK(6, 1)], writes=[TTnk])
                TT, TTk = TTn, TTnk
                if not last:
                    T, Tk = Tn, Tnk
            ckpt("g7")
            S.op("pe", lambda e, TT=TT, Yb=Yb: e.matmul(P[5][0:W, 384:512], TT[0:W, 0:W], Yb[0:W, :], start=True, stop=True), reads=[TTk, Ybk], writes=[PK(5, 3)])
            Y2, Y2k = r_f.get()
            S.op("act", lambda e, Y2=Y2: e.copy(out=Y2[0:W, :], in_=P[5][0:W, 384:512]), reads=[PK(5, 3)], writes=[Y2k])
            Y, Yk = Y2, Y2k
        if h == DBG_H and ti == 0 and c0 == 0 and not dbg_outs.get("_done"):
            dbg("Dge", Dge[:], [Dgek], [128, 128]); dbg("qkT", qkT[:], [qkTk], [128, 128]); dbg("Yfin", Y[:], [Yk], [128, 128])
            dbg("kgk", kgk[:], [kgkk], [128, 128]); dbg("vtok", vtok[:], [vtokk], [128, 128]); dbg("tokG", tokG[:, 0, :, :], [tk], [128, 6, 8])
            dbg(BG.kk, BG.kT[:, 0:128], [BG.kk], [128, 128]); dbg(BG.kq, BG.qT[:, 0:128], [BG.kq], [128, 128])
        vnew, vnewk = r_gn.get()
        S.op("dve", lambda e, Y=Y: e.tensor_scalar(out=vnew[0:W, :], in0=Y[0:W, :], scalar1=tokG[0:W, ti, 1, h:h + 1], scalar2=None, op0=ALU.mult), reads=[Yk, tk], writes=[vnewk])
        ckpt("g8")
        S.op("pe", lambda e: e.matmul(P[7][0:W, 256:384], BG.qT[:, c0:c0 + W], st_b, start=True, stop=True), reads=[BG.kq, stk_b], writes=[PK(7)])
        S.op("pe", lambda e: e.matmul(P[7][0:W, 384:512], qkT[0:W, 0:W], vnew[0:W, :], start=True, stop=True), reads=[qkTk, vnewk], writes=[PK(7)])
        t1, t1k = r_f.get()
        o, ok_ = r_f.get()
        S.op("act", lambda e: e.activation(out=t1[0:W, :], in_=P[7][0:W, 256:384], func=AF.Copy, scale=tokG[0:W, ti, 2, h:h + 1]), reads=[PK(7), tk], writes=[t1k])
        S.op("dve", lambda e: e.tensor_tensor(out=o[0:W, :], in0=P[7][0:W, 384:512], in1=t1[0:W, :], op=ALU.add), reads=[PK(7), t1k], writes=[ok_])
        if W == 1:
            S.op("pe", lambda e: e.transpose(out=P[7][:, 0:W], in_=o[0:W, :], identity=ident_f[0:W, 0:W]), reads=[ok_, "ident_f"], writes=[PK(7)])
            S.op("act", lambda e: e.copy(out=oraw[:, sb_:sb_ + 1], in_=P[7][:, 0:1]), reads=[PK(7)], writes=[("oraw", sb_)])
        else:
            ckpt("g9")
            ss, ssk = r_c1.get()
            S.op("act", lambda e: e.activation(out=t1[0:W, :], in_=o[0:W, :], func=AF.Square, accum_out=ss[0:W, :]), reads=[ok_], writes=[t1k, ssk])
            S.op("act", lambda e: e.activation(out=ss[0:W, :], in_=ss[0:W, :], func=AF.Ln, bias=epsc[0:W, 0:1], scale=1.0 / 128), reads=[ssk, "epsc"], writes=[ssk])
            S.op("act", lambda e: e.activation(out=ss[0:W, :], in_=ss[0:W, :], func=AF.Exp, scale=-0.5), reads=[ssk], writes=[ssk])
            S.op("dve", lambda e: e.tensor_scalar(out=o[0:W, :], in0=o[0:W, :], scalar1=ss[0:W, 0:1], scalar2=None, op0=ALU.mult), reads=[ok_, ssk], writes=[ok_])
            ckpt("g10")
            S.op("pe", lambda e: e.transpose(out=P[7][:, 0:W], in_=o[0:W, :], identity=ident_f[0:W, 0:W]), reads=[ok_, "ident_f"], writes=[PK(7)])
            S.op("dve", lambda e: e.scalar_tensor_tensor(out=oT[:, h, c0:c0 + W], in0=P[7][:, 0:W], scalar=g_gdn[:, 0:1], in1=BG.sgate[:, c0:c0 + W], op0=ALU.mult, op1=ALU.mult), reads=[PK(7), "g_gdn", BG.kg], writes=[("oT", h)])
        ckpt("g11")
        S.op("pe", lambda e: e.matmul(P[7][:, 128:256], kgk[0:W, :], vnew[0:W, :], start=True, stop=True), reads=[kgkk, vnewk], writes=[PK(7)])
        S.op("dve", lambda e: e.scalar_tensor_tensor(out=st_f, in0=st_f, scalar=tokG[:, ti, 5, h:h + 1], in1=P[7][:, 128:256], op0=ALU.mult, op1=ALU.add), reads=[stk_f, tk, PK(7)], writes=[stk_f])
        if sb_ is None:
            S.op("act", lambda e: e.copy(out=st_b, in_=st_f), reads=[stk_f], writes=[stk_b])
        else:
            S.dma("sp", o_gdn_s[sb_, h], st_f, reads=[stk_f], writes=[("o_gdn_s", sb_, h)])

    oraw = S.sb("oraw", [128, NSMP], F32)
    r_o16 = Rot("o16", [128, NSMP], F32, 2)
    sq16 = S.sb("sq16", [128, NSMP], BF16)

    def gdn_samples_norm(h, npr, ns, BG):
        cs = slice(npr, npr + ns)
        rs, rsk = r_o16.get()
        S.op("act", lambda e: e.activation(out=sq16[:, 0:ns], in_=oraw[:, 0:ns], func=AF.Square), reads=["oraw"], writes=["sq16"])
        S.op("pe", lambda e: e.matmul(P[7][:, 0:ns], ones_b[:], sq16[:, 0:ns], start=True, stop=True), reads=["ones_b", "sq16"], writes=[PK(7)])
        S.op("act", lambda e: e.activation(out=rs[:, 0:ns], in_=P[7][:, 0:ns], func=AF.Ln, bias=epsc[:, 0:1], scale=1.0 / 128), reads=[PK(7), "epsc"], writes=[rsk])
        S.op("act", lambda e: e.activation(out=rs[:, 0:ns], in_=rs[:, 0:ns], func=AF.Exp, scale=-0.5), reads=[rsk], writes=[rsk])
        S.op("dve", lambda e: e.scalar_tensor_tensor(out=rs[:, 0:ns], in0=oraw[:, 0:ns], scalar=g_gdn[:, 0:1], in1=rs[:, 0:ns], op0=ALU.mult, op1=ALU.mult), reads=["oraw", "g_gdn", rsk], writes=[rsk])
        S.op("dve", lambda e: e.tensor_tensor(out=oT[:, h, cs], in0=rs[:, 0:ns], in1=BG.sgate[:, cs], op=ALU.mult), reads=[rsk, BG.kg], writes=[("oT", h)])

    r_c4 = Rot("c4", [128, 4], F32, 2)
    r_gy2 = Rot("gy2", [128, 128], BF16, 2)
    r_gn2 = Rot("gn2", [128, 128], BF16, 2)
    r_fp = Rot("tfp", [128, 128], F32, 2)
    r_c1p = Rot("c1p", [128, 1], F32, 2)
    NPT = SEGLEN // 128
    r_f4 = Rot("f4", [128, NPT, 128], F32, 3)
    r_b4 = Alias([(arena[:, i * NPT * 128:(i + 1) * NPT * 128].rearrange("p (t d) -> p t d", d=128), ("arena", i)) for i in range(NB4)])
    r_k4 = Rot("k4", [128, NPT, 128], BF16, 6)

    def gdn_head_prompt(h, npr, BG):
        npt = npr // 128
        W = 128

        def bc(ap2):
            return ap2.unsqueeze(1).to_broadcast([128, npt, 128])

        def tcol(idx):
            return tokG[:, 0:npt, idx, h:h + 1].to_broadcast([128, npt, 128])

        tkeys = [("tokG", t) for t in range(npt)]
        vtok4, vtk = r_k4.get()
        kgk4, kgkk = r_k4.get()
        qkT4, qkk = r_k4.get()
        MTa4, MTk = r_k4.get()
        Ma4, Mak = r_k4.get()
        TTf, TTfk = r_k4.get()
        for t in range(npt):
            S.op("pe", lambda e, t=t: e.transpose(out=P2b[:, t * 128:(t + 1) * 128], in_=BG.kT[:, t * 128:(t + 1) * 128], identity=ident_b[:]), reads=[BG.kk, "ident_b"], writes=[PK(2)])
        for t in range(npt):
            S.op("pe", lambda e, t=t: e.transpose(out=P2b[:, 512 + t * 128:512 + (t + 1) * 128], in_=BG.vT[:, t * 128:(t + 1) * 128], identity=ident_b[:]), reads=[BG.kv, "ident_b"], writes=[PK(2)])
        S.op("dve", lambda e: e.tensor_tensor(out=kgk4[:, 0:npt, :], in0=P2b[:, 0:npt * 128].rearrange("p (t d) -> p t d", d=128), in1=tcol(4), op=ALU.mult), reads=[PK(2)] + tkeys, writes=[kgkk])
        S.op("dve", lambda e: e.tensor_copy(out=vtok4[:, 0:npt, :], in_=P2b[:, 512:512 + npt * 128].rearrange("p (t d) -> p t d", d=128)), reads=[PK(2)], writes=[vtk])
        gU4, gUk = r_f4.get()
        S.op("pool", lambda e: e.tensor_tensor(out=gU4[:, 0:npt, :], in0=bc(mU[:]), in1=tcol(0), op=ALU.mult), reads=["mU"] + tkeys, writes=[gUk])
        for t in range(npt):
            S.op("pe", lambda e, t=t: e.matmul(P[3][:, t * 128:(t + 1) * 128], mT[:], gU4[:, t, :], start=True, stop=False), reads=["mT", gUk], writes=[PK(3)])
            S.op("pe", lambda e, t=t: e.matmul(P[3][:, t * 128:(t + 1) * 128], ident_f[:], mNEG[:], start=False, stop=True), reads=["ident_f", "mNEG"], writes=[PK(3)])
        Dge4, Dgk = r_f4.get()
        S.op("act", lambda e: e.activation(out=Dge4[:, 0:npt, :], in_=P[3][:, 0:npt * 128].rearrange("p (t d) -> p t d", d=128), func=AF.Exp), reads=[PK(3)], writes=[Dgk])
        for t in range(npt):
            S.op("pe", lambda e, t=t: e.matmul(P[4][:, t * 128:(t + 1) * 128], BG.kT[:, t * 128:(t + 1) * 128], BG.kT[:, t * 128:(t + 1) * 128], start=True, stop=True), reads=[BG.kk], writes=[PK(4)])
        for t in range(npt):
            S.op("pe", lambda e, t=t: e.matmul(P[5][:, t * 128:(t + 1) * 128], BG.kT[:, t * 128:(t + 1) * 128], BG.qT[:, t * 128:(t + 1) * 128], start=True, stop=True), reads=[BG.kk, BG.kq], writes=[PK(5)])
        S.op("dve", lambda e: e.tensor_tensor(out=qkT4[:, 0:npt, :], in0=P[5][:, 0:npt * 128].rearrange("p (t d) -> p t d", d=128), in1=Dge4[:, 0:npt, :], op=ALU.mult), reads=[PK(5), Dgk], writes=[qkk])
        tmp4, tmpk = r_f4.get()
        S.op("dve", lambda e: e.tensor_tensor(out=tmp4[:, 0:npt, :], in0=P[4][:, 0:npt * 128].rearrange("p (t d) -> p t d", d=128), in1=Dge4[:, 0:npt, :], op=ALU.mult), reads=[PK(4), Dgk], writes=[tmpk])
        bUS4, bUk = r_f4.get()
        S.op("pool", lambda e: e.tensor_tensor(out=bUS4[:, 0:npt, :], in0=bc(mUS[:]), in1=tcol(1), op=ALU.mult), reads=["mUS"] + tkeys, writes=[bUk])
        S.op("pool", lambda e: e.tensor_tensor(out=MTa4[:, 0:npt, :], in0=tmp4[:, 0:npt, :], in1=bUS4[:, 0:npt, :], op=ALU.mult), reads=[tmpk, bUk], writes=[MTk])
        for t in range(npt):
            S.op("pe", lambda e, t=t: e.transpose(out=P2b[:, t * 128:(t + 1) * 128], in_=MTa4[:, t, :], identity=ident_b[:]), reads=[MTk, "ident_b"], writes=[PK(2)])
        S.op("act", lambda e: e.copy(out=Ma4[:, 0:npt, :], in_=P2b[:, 0:npt * 128].rearrange("p (t d) -> p t d", d=128)), reads=[PK(2)], writes=[Mak])
        def slot(i):
            return arena[:, i * NPT * 128:(i + 1) * NPT * 128].rearrange("p (t d) -> p t d", d=128), ("arena", i)

        def mk_L(lv):
            Lj, Ljk = slot(4 + lv % 2)
            S.op("pool", lambda e, Lj=Lj, lv=lv: e.tensor_tensor(out=Lj[:, 0:npt, :], in0=Ma4[:, 0:npt, :], in1=bc(cmask[:, lv, :]), op=ALU.mult), reads=[Mak, "cmask"], writes=[Ljk])
            return Lj, Ljk

        def mk_N(lv):
            Nj, Njk = slot(6 + lv % 2)
            S.op("pool", lambda e, Nj=Nj, lv=lv: e.tensor_tensor(out=Nj[:, 0:npt, :], in0=MTa4[:, 0:npt, :], in1=bc(cmask[:, 7 + lv, :]), op=ALU.mult), reads=[MTk, "cmask"], writes=[Njk])
            return Nj, Njk

        nlev = 7
        L0, L0k = mk_L(0)
        N0, N0k = mk_N(0)
        T, Tk = slot(0)
        TT, TTk = slot(2)
        S.op("pool", lambda e, T=T, L0=L0: e.tensor_tensor(out=T[:, 0:npt, :], in0=bc(ident_b[:]), in1=L0[:, 0:npt, :], op=ALU.subtract), reads=["ident_b", L0k], writes=[Tk])
        S.op("pool", lambda e, TT=TT, N0=N0: e.tensor_tensor(out=TT[:, 0:npt, :], in0=bc(ident_b[:]), in1=N0[:, 0:npt, :], op=ALU.subtract), reads=["ident_b", N0k], writes=[TTk])
        nxtL = mk_L(1)
        nxtN = mk_N(1)
        Ub, Ubk = slot(8)
        Zb, Zbk = slot(9)
        for lv in range(1, nlev):
            last = (lv == nlev - 1)
            Lj, Ljk = nxtL
            Nj, Njk = nxtN if not last else (None, None)
            if lv + 1 < nlev:
                nxtL = mk_L(lv + 1)
                if lv + 1 < nlev - 1:
                    nxtN = mk_N(lv + 1)
            for t in range(npt):
                S.op("pe", lambda e, t=t, Lj=Lj, TT=TT: e.matmul(P[5][:, t * 128:(t + 1) * 128], Lj[:, t, :], TT[:, t, :], start=True, stop=True), reads=[Ljk, TTk], writes=[PK(5)])
            S.op("act", lambda e: e.copy(out=Ub[:, 0:npt, :], in_=P[5][:, 0:npt * 128].rearrange("p (t d) -> p t d", d=128)), reads=[PK(5)], writes=[Ubk])
            for t in range(npt):
                S.op("pe", lambda e, t=t, T=T: e.matmul(P[6][:, t * 128:(t + 1) * 128], T[:, t, :], Ub[:, t, :], start=True, stop=True), reads=[Tk, Ubk], writes=[PK(6)])
            if last:
                TTn, TTnk = TTf, TTfk
            else:
                TTn, TTnk = slot(2 + lv % 2)
                for t in range(npt):
                    S.op("pe", lambda e, t=t, Nj=Nj, T=T: e.matmul(P[4][:, t * 128:(t + 1) * 128], Nj[:, t, :], T[:, t, :], start=True, stop=True), reads=[Njk, Tk], writes=[PK(4)])
                S.op("act", lambda e: e.copy(out=Zb[:, 0:npt, :], in_=P[4][:, 0:npt * 128].rearrange("p (t d) -> p t d", d=128)), reads=[PK(4)], writes=[Zbk])
                for t in range(npt):
                    S.op("pe", lambda e, t=t, TT=TT: e.matmul(P[3][:, t * 128:(t + 1) * 128], TT[:, t, :], Zb[:, t, :], start=True, stop=True), reads=[TTk, Zbk], writes=[PK(3)])
                Tn, Tnk = slot(lv % 2)
                S.op("dve", lambda e, T=T, Tn=Tn: e.tensor_tensor(out=Tn[:, 0:npt, :], in0=T[:, 0:npt, :], in1=P[3][:, 0:npt * 128].rearrange("p (t d) -> p t d", d=128), op=ALU.subtract), reads=[Tk, PK(3)], writes=[Tnk])
            S.op("dve", lambda e, TT=TT, TTn=TTn: e.tensor_tensor(out=TTn[:, 0:npt, :], in0=TT[:, 0:npt, :], in1=P[6][:, 0:npt * 128].rearrange("p (t d) -> p t d", d=128), op=ALU.subtract), reads=[TTk, PK(6)], writes=[TTnk])
            TT, TTk = TTn, TTnk
            if not last:
                T, Tk = Tn, Tnk
        st_f, stk_f = Sg[:, h, :], ("Sg", h)
        st_b, stk_b = Sgb[:, h, :], ("Sgb", h)
        for t in range(npt):
            c0 = t * 128
            tk = ("tokG", t)
            S.op("pe", lambda e, c0=c0: e.matmul(P[3][:, 0:128], BG.kT[:, c0:c0 + W], st_b, start=True, stop=True), reads=[BG.kk, stk_b], writes=[PK(3)])
            Yb, Ybk = r_gy2.get()
            S.op("dve", lambda e, t=t, Yb=Yb: e.scalar_tensor_tensor(out=Yb[:], in0=P[3][:, 0:128], scalar=tokG[:, t, 3, h:h + 1], in1=vtok4[:, t, :], op0=ALU.mult, op1=ALU.add), reads=[PK(3), tk, vtk], writes=[Ybk])
            S.op("pe", lambda e, t=t, Yb=Yb: e.matmul(P[3][:, 128:256], TTf[:, t, :], Yb[:], start=True, stop=True), reads=[TTfk, Ybk], writes=[PK(3)])
            vnew, vnewk = r_gn2.get()
            S.op("act", lambda e, t=t, vnew=vnew: e.activation(out=vnew[:], in_=P[3][:, 128:256], func=AF.Copy, scale=tokG[:, t, 1, h:h + 1]), reads=[PK(3), tk], writes=[vnewk])
            S.op("pe", lambda e, c0=c0, t=t: e.matmul(P[4][:, t * 128:(t + 1) * 128], BG.qT[:, c0:c0 + W], st_b, start=True, stop=True), reads=[BG.kq, stk_b], writes=[PK(4)])
            S.op("pe", lambda e, t=t, vnew=vnew: e.matmul(P[5][:, t * 128:(t + 1) * 128], qkT4[:, t, :], vnew[:], start=True, stop=True), reads=[qkk, vnewk], writes=[PK(5)])
            S.op("pe", lambda e, t=t, vnew=vnew: e.matmul(P[6][:, 0:128], kgk4[:, t, :], vnew[:], start=True, stop=True), reads=[kgkk, vnewk], writes=[PK(6)])
            S.op("dve", lambda e, t=t: e.scalar_tensor_tensor(out=st_f, in0=st_f, scalar=tokG[:, t, 5, h:h + 1], in1=P[6][:, 0:128], op0=ALU.mult, op1=ALU.add), reads=[stk_f, tk, PK(6)], writes=[stk_f])
            S.op("act", lambda e: e.copy(out=st_b, in_=st_f), reads=[stk_f], writes=[stk_b])
        t14, t14k = r_f4.get()
        o4, o4k = r_f4.get()
        sq4, sq4k = r_f4.get()
        ss4, ss4k = r_c4.get()
        S.op("dve", lambda e: e.tensor_tensor(out=t14[:, 0:npt, :], in0=P[4][:, 0:npt * 128].rearrange("p (t d) -> p t d", d=128), in1=tcol(2), op=ALU.mult), reads=[PK(4)] + tkeys, writes=[t14k])
        S.op("dve", lambda e: e.tensor_tensor(out=o4[:, 0:npt, :], in0=P[5][:, 0:npt * 128].rearrange("p (t d) -> p t d", d=128), in1=t14[:, 0:npt, :], op=ALU.add), reads=[PK(5), t14k], writes=[o4k])
        S.op("pool", lambda e: e.tensor_tensor(out=sq4[:, 0:npt, :], in0=o4[:, 0:npt, :], in1=o4[:, 0:npt, :], op=ALU.mult), reads=[o4k], writes=[sq4k])
        S.op("dve", lambda e: e.tensor_reduce(out=ss4[:, 0:npt], in_=sq4[:, 0:npt, :], axis=AX.X, op=ALU.add), reads=[sq4k], writes=[ss4k])
        S.op("act", lambda e: e.activation(out=ss4[:, 0:npt], in_=ss4[:, 0:npt], func=AF.Ln, bias=epsc[:, 0:1], scale=1.0 / 128), reads=[ss4k, "epsc"], writes=[ss4k])
        S.op("act", lambda e: e.activation(out=ss4[:, 0:npt], in_=ss4[:, 0:npt], func=AF.Exp, scale=-0.5), reads=[ss4k], writes=[ss4k])
        S.op("pool", lambda e: e.tensor_tensor(out=o4[:, 0:npt, :], in0=o4[:, 0:npt, :], in1=ss4[:, 0:npt].unsqueeze(2).to_broadcast([128, npt, 128]), op=ALU.mult), reads=[o4k, ss4k], writes=[o4k])
        for t in range(npt):
            S.op("pe", lambda e, t=t: e.transpose(out=P[6][:, t * 128:(t + 1) * 128], in_=o4[:, t, :], identity=ident_f[:]), reads=[o4k, "ident_f"], writes=[PK(6)])
        S.op("dve", lambda e: e.scalar_tensor_tensor(out=oT[:, h, 0:npr], in0=P[6][:, 0:npr], scalar=g_gdn[:, 0:1], in1=BG.sgate[:, 0:npr], op0=ALU.mult, op1=ALU.mult), reads=[PK(6), "g_gdn", BG.kg], writes=[("oT", h)])

    f8b = S.sb("f8b", [128, NPT, 2, 128], F32)
    r_f8 = Alias([(xtok[:, 0:NPT * 256].rearrange("p (t a l) -> p t a l", t=NPT, a=2), "xtok"), (f8b, "f8b")])
    r_m8 = Rot("m8", [128, NPT, 2, 128], BF16, 1)
    r_x4 = r_k4
    r_g4 = r_f4

    snf = S.sb("snf", [128, NPT, 128], F32)
    snb = S.sb("snb", [128, NPT, 128], BF16)

    def ssd_pair_prompt(g, jj, j, npr, BS):
        npt = npr // 128
        h0 = 2 * j
        tkeys = [("tokS", t) for t in range(npt)]

        def hsc(idx):
            return tokS[:, 0:npt, idx, h0:h0 + 2].unsqueeze(3).to_broadcast([128, npt, 2, 64])

        def v4(ap):
            return ap.rearrange("p (t a c) -> p t a c", t=npt, a=2)

        st_f, stk_f = prevT[:, j * 128:(j + 1) * 128], ("prevT", j)
        st_b, stk_b = prevTb[:, j * 128:(j + 1) * 128], ("prevTb", j)
        for t in range(npt):
            S.op("pe", lambda e, t=t: e.transpose(out=P[2][:, t * 128:(t + 1) * 128], in_=BS.xsT[:, t * 128:(t + 1) * 128], identity=ident_f[:]), reads=[BS.kxs, "ident_f"], writes=[PK(2)])
        xsk4, xskk = r_g4.get()
        xdt4, xdtk = r_x4.get()
        xdd4, xddk = r_x4.get()
        S.op("act", lambda e: e.copy(out=xsk4[:, 0:npt, :], in_=P[2][:, 0:npt * 128].rearrange("p (t d) -> p t d", d=128)), reads=[PK(2)], writes=[xskk])
        S.op("dve", lambda e: e.tensor_tensor(out=v4(xdt4[:, 0:npt, :].rearrange("p t d -> p (t d)")), in0=v4(P[2][:, 0:npt * 128]), in1=hsc(0), op=ALU.mult), reads=[PK(2)] + tkeys, writes=[xdtk])
        S.op("dve", lambda e: e.tensor_tensor(out=v4(xdd4[:, 0:npt, :].rearrange("p t d -> p (t d)")), in0=v4(P[2][:, 0:npt * 128]), in1=hsc(2), op=ALU.mult), reads=[PK(2)] + tkeys, writes=[xddk])
        daU, daUk = r_f8.get()
        S.op("pool", lambda e: e.tensor_tensor(out=daU[:, 0:npt, :, :], in0=mU[:].unsqueeze(1).unsqueeze(1).to_broadcast([128, npt, 2, 128]), in1=tokS[:, 0:npt, 1, h0:h0 + 2].unsqueeze(3).to_broadcast([128, npt, 2, 128]), op=ALU.mult), reads=["mU"] + tkeys, writes=[daUk])
        Lh, Lhk = r_f8.get()
        for t in range(npt):
            pb = 3 + t // 2
            for hh in range(2):
                col = ((t % 2) * 2 + hh) * 128
                S.op("pe", lambda e, t=t, hh=hh, pb=pb, col=col: e.matmul(P[pb][:, col:col + 128], mT[:], daU[:, t, hh, :], start=True, stop=False), reads=["mT", daUk], writes=[PK(pb)])
                S.op("pe", lambda e, pb=pb, col=col: e.matmul(P[pb][:, col:col + 128], ident_f[:], mNEG[:], start=False, stop=True), reads=["ident_f", "mNEG"], writes=[PK(pb)])
        for half in range((npt + 1) // 2):
            nt = min(2, npt - 2 * half)
            S.op("act", lambda e, half=half, nt=nt: e.activation(out=Lh[:, 2 * half:2 * half + nt, :, :], in_=P[3 + half][:, 0:nt * 256].rearrange("p (t a l) -> p t a l", t=nt, a=2), func=AF.Exp), reads=[PK(3 + half)], writes=[Lhk])
        M8, M8k = r_m8.get()
        S.op("dve", lambda e: e.tensor_tensor(out=M8[:, 0:npt, :, :], in0=Lh[:, 0:npt, :, :], in1=cbm[:, 0:npt, :].unsqueeze(2).to_broadcast([128, npt, 2, 128]), op=ALU.mult), reads=[Lhk] + [("cbm", t) for t in range(npt)], writes=[M8k])
        for t in range(npt):
            for hh in range(2):
                S.op("pe", lambda e, t=t, hh=hh: e.matmul(P[5][:, t * 128 + hh * 64:t * 128 + (hh + 1) * 64], M8[:, t, hh, :], xdt4[:, t, hh * 64:(hh + 1) * 64], start=True, stop=True), reads=[M8k, xdtk], writes=[PK(5)])
        for t in range(npt):
            S.op("pe", lambda e, t=t: e.matmul(P[6][:, t * 128:(t + 1) * 128], bmtok[:, t, :], xdd4[:, t, :], start=True, stop=True), reads=[("bmtok", t), xddk], writes=[PK(6)])
        for t in range(npt):
            last = (t == npt - 1)
            src_f, src_fk = (st_f, stk_f) if t == 0 else (snf[:, t - 1, :], ("snf", t - 1))
            src_b, src_bk = (st_b, stk_b) if t == 0 else (snb[:, t - 1, :], ("snb", t - 1))
            dst_f, dst_fk = (st_f, stk_f) if last else (snf[:, t, :], ("snf", t))
            dst_b, dst_bk = (st_b, stk_b) if last else (snb[:, t, :], ("snb", t))
            S.op("pe", lambda e, t=t, src_b=src_b: e.matmul(P[7][:, t * 128:(t + 1) * 128], cmT[:, t * 128:(t + 1) * 128], src_b, start=True, stop=True), reads=["cmT", src_bk], writes=[PK(7)])
            for hh in range(2):
                h = h0 + hh
                S.op("dve", lambda e, t=t, hh=hh, h=h, src_f=src_f, dst_f=dst_f: e.scalar_tensor_tensor(out=dst_f[:, hh * 64:(hh + 1) * 64], in0=src_f[:, hh * 64:(hh + 1) * 64], scalar=tokS[:, t, 4, h:h + 1], in1=P[6][:, t * 128 + hh * 64:t * 128 + (hh + 1) * 64], op0=ALU.mult, op1=ALU.add), reads=[src_fk, ("tokS", t), PK(6)], writes=[dst_fk])
            S.op("act", lambda e, dst_f=dst_f, dst_b=dst_b: e.copy(out=dst_b, in_=dst_f), reads=[dst_fk], writes=[dst_bk])
        t14, t14k = r_g4.get()
        ys4, ys4k = r_g4.get()
        S.op("dve", lambda e: e.tensor_tensor(out=v4(t14[:, 0:npt, :].rearrange("p t d -> p (t d)")), in0=v4(P[7][:, 0:npt * 128]), in1=hsc(3), op=ALU.mult), reads=[PK(7)] + tkeys, writes=[t14k])
        S.op("dve", lambda e: e.tensor_tensor(out=ys4[:, 0:npt, :], in0=P[5][:, 0:npt * 128].rearrange("p (t d) -> p t d", d=128), in1=t14[:, 0:npt, :], op=ALU.add), reads=[PK(5), t14k], writes=[ys4k])
        S.op("pool", lambda e: e.tensor_tensor(out=v4(t14[:, 0:npt, :].rearrange("p t d -> p (t d)")), in0=v4(xsk4[:, 0:npt, :].rearrange("p t d -> p (t d)")), in1=dskB[:, h0:h0 + 2].unsqueeze(1).unsqueeze(3).to_broadcast([128, npt, 2, 64]), op=ALU.mult), reads=[xskk, "dskB", t14k], writes=[t14k])
        S.op("pool", lambda e: e.tensor_tensor(out=ys4[:, 0:npt, :], in0=ys4[:, 0:npt, :], in1=t14[:, 0:npt, :], op=ALU.add), reads=[ys4k, t14k], writes=[ys4k])
        for t in range(npt):
            S.op("pe", lambda e, t=t: e.transpose(out=P[2][:, t * 128:(t + 1) * 128], in_=ys4[:, t, :], identity=ident_f[:]), reads=[ys4k, "ident_f"], writes=[PK(2)])
        S.op("dve", lambda e: e.tensor_tensor(out=yg[:, jj, 0:npr], in0=P[2][:, 0:npr], in1=BS.szT[:, 0:npr], op=ALU.mult), reads=[PK(2), BS.kzs], writes=[("yg", jj)])

    def phase2(si, t0, npr, ns, NC, blks, load_xT):
        nb = len(blks)

        def ew(eng, fn, reads, writes):
            S.op(eng, fn, reads=reads, writes=writes)

        for m in range(8):
            wa, wak = load_w(w_in, 8, OFF_MG + m * 128)
            mm(0, 128, wa, wak, 8, lambda k, c0, n: hT[:, k, c0:c0 + n], ["hT"], blks, pbase=0)
            wb, wbk = load_w(w_bs, 16, m * 128)
            mm(0, 128, wb, wbk, 16, lambda k, c0, n: yT[:, k, c0:c0 + n], ["yT"], blks, pbase=2)
            wc, wck = load_w(w_in, 8, OFF_MG + 1024 + m * 128)
            mm(0, 128, wc, wck, 8, lambda k, c0, n: hT[:, k, c0:c0 + n], ["hT"], blks, pbase=4)
            wd, wdk = load_w(w_bg, 8, m * 128)
            mm(0, 128, wd, wdk, 8, lambda k, c0, n: oT[:, k, c0:c0 + n], ["oT"], blks, pbase=6)
            for bi, (c0, n) in enumerate(blks):
                sg, sgk = r_sig.get()
                tt, ttk = r_t1.get()
                S.op("act", lambda e, bi=bi, n=n, sg=sg: e.activation(out=sg[:, 0:n], in_=P[bi][:, 0:n], func=AF.Sigmoid), reads=[PK(bi)], writes=[sgk])
                S.op("dve", lambda e, bi=bi, n=n, sg=sg, tt=tt: e.tensor_tensor(out=tt[:, 0:n], in0=P[2 + bi][:, 0:n], in1=sg[:, 0:n], op=ALU.mult), reads=[PK(2 + bi), sgk], writes=[ttk])
                sg2, sg2k = r_sig.get()
                S.op("act", lambda e, bi=bi, n=n, sg2=sg2: e.activation(out=sg2[:, 0:n], in_=P[4 + bi][:, 0:n], func=AF.Sigmoid), reads=[PK(4 + bi)], writes=[sg2k])
                S.op("dve", lambda e, bi=bi, n=n, sg2=sg2: e.tensor_tensor(out=sg2[:, 0:n], in0=P[6 + bi][:, 0:n], in1=sg2[:, 0:n], op=ALU.mult), reads=[PK(6 + bi), sg2k], writes=[sg2k])
                S.op("pool", lambda e, c0=c0, n=n, sg2=sg2, tt=tt, m=m: e.tensor_tensor(out=mixT[:, m, c0:c0 + n], in0=sg2[:, 0:n], in1=tt[:, 0:n], op=ALU.add), reads=[sg2k, ttk], writes=[("arena",)])
        for m in range(8):
            w, wk = load_w(w_out, 8, m * 128)
            mm(0, 128, w, wk, 8, lambda k, c0, n: mixT[:, k, c0:c0 + n], [("arena",)], blks, pbase=(m % 2) * 2)
            for bi, (c0, n) in enumerate(blks):
                S.op("dve", lambda e, bi=bi, c0=c0, n=n, m=m: e.tensor_tensor(out=xT[:, m, c0:c0 + n], in0=P[(m % 2) * 2 + bi][:, 0:n], in1=xT[:, m, c0:c0 + n], op=ALU.add), reads=[PK((m % 2) * 2 + bi), ("xT", m)], writes=[("xT", m)])
        rmsnorm(xT, "xT", g_ffn, hT, "hT", _blocks256(NC))
        for half in range(2):
            for fi in range(11):
                f = half * 11 + fi
                wg_, wgk = load_w(w_ffn_in, 8, f * 128)
                mm(0, 128, wg_, wgk, 8, lambda k, c0, n: hT[:, k, c0:c0 + n], ["hT"], blks, pbase=(fi % 2) * 4)
                wu, wuk = load_w(w_ffn_in, 8, DFF + f * 128)
                mm(0, 128, wu, wuk, 8, lambda k, c0, n: hT[:, k, c0:c0 + n], ["hT"], blks, pbase=(fi % 2) * 4 + 2)
                for bi, (c0, n) in enumerate(blks):
                    sg, sgk = r_sig.get()
                    pb = (fi % 2) * 4
                    S.op("act", lambda e, bi=bi, n=n, sg=sg, pb=pb: e.activation(out=sg[:, 0:n], in_=P[pb + bi][:, 0:n], func=AF.Silu), reads=[PK(pb + bi)], writes=[sgk])
                    S.op("dve", lambda e, bi=bi, c0=c0, n=n, sg=sg, pb=pb, fi=fi: e.tensor_tensor(out=yT[:, fi, c0:c0 + n], in0=P[pb + 2 + bi][:, 0:n], in1=sg[:, 0:n], op=ALU.mult), reads=[PK(pb + 2 + bi), sgk], writes=[("yT", fi)])
            for m in range(8):
                w, wk = load_w(w_ffn_out, 11, m * 128, r0=half * 11 * 128)
                mm(0, 128, w, wk, 11, lambda k, c0, n: yT[:, k, c0:c0 + n], ["yT"], blks, pbase=(m % 2) * 2)
                for bi, (c0, n) in enumerate(blks):
                    S.op("dve", lambda e, bi=bi, c0=c0, n=n, m=m: e.tensor_tensor(out=xT[:, m, c0:c0 + n], in0=P[(m % 2) * 2 + bi][:, 0:n], in1=xT[:, m, c0:c0 + n], op=ALU.add), reads=[PK((m % 2) * 2 + bi), ("xT", m)], writes=[("xT", m)])
        rmsnorm(xT, "xT", g_pl, hT, "hT", _blocks256(NC))
        load_xT(mixT, "arena", p_p, p_s, 256, 2)
        for m in range(8):
            wg_, wgk = load_w(w_pl_gate, 8, m * 128)
            pb = (m % 2) * 4
            mm(0, 128, wg_, wgk, 8, lambda k, c0, n: hT[:, k, c0:c0 + n], ["hT"], blks, pbase=pb)
            wp, wpk = load_w(w_pl_proj, 2, m * 128)
            mm(0, 128, wp, wpk, 2, lambda k, c0, n: mixT[:, k, c0:c0 + n], [("arena",)], blks, pbase=pb + 2)
            for bi, (c0, n) in enumerate(blks):
                sg, sgk = r_sig.get()
                S.op("act", lambda e, bi=bi, n=n, sg=sg, pb=pb: e.activation(out=sg[:, 0:n], in_=P[pb + bi][:, 0:n], func=AF.Sigmoid), reads=[PK(pb + bi)], writes=[sgk])
                S.op("dve", lambda e, bi=bi, n=n, sg=sg, pb=pb: e.tensor_tensor(out=sg[:, 0:n], in0=P[pb + 2 + bi][:, 0:n], in1=sg[:, 0:n], op=ALU.mult), reads=[PK(pb + 2 + bi), sgk], writes=[sgk])
                S.op("pool", lambda e, c0=c0, n=n, sg=sg, m=m: e.tensor_tensor(out=xT[:, m, c0:c0 + n], in0=sg[:, 0:n], in1=xT[:, m, c0:c0 + n], op=ALU.add), reads=[sgk, ("xT", m)], writes=[("xT", m)])
        rmsnorm(xT, "xT", g_fin, xT, "xT", _blocks256(NC))
        otok = xtok
        for ti in range(npr // 128 + (1 if ns else 0)):
            if ti < npr // 128:
                rows, c0, dst = 128, ti * 128, y_p[t0 + ti * 128:t0 + (ti + 1) * 128, :]
            else:
                rows, c0, dst = ns, npr, y_s[:, :]
            for k in range(8):
                S.op("pe", lambda e, k=k, rows=rows, c0=c0: e.transpose(out=P[k // 4][0:rows, (k % 4) * 128:(k % 4 + 1) * 128], in_=xT[:, k, c0:c0 + rows], identity=ident_f[:]), reads=[("xT", k), "ident_f"], writes=[PK(k // 4)])
            S.op("act", lambda e, rows=rows: e.copy(out=otok[0:rows, 0:512], in_=P[0][0:rows, :]), reads=[PK(0)], writes=["xtok"])
            S.op("dve", lambda e, rows=rows: e.tensor_copy(out=otok[0:rows, 512:1024], in_=P[1][0:rows, :]), reads=[PK(1)], writes=["xtok"])
            S.dma("sp", dst, otok[0:rows, :], reads=["xtok"], writes=[("y", si, ti)])

    for si, (t0, npr, ns) in enumerate(SEGS):
        segment(si, t0, npr, ns)

    for j in range(16):
        so, sok = r_sst.get()
        S.op("pe", lambda e, j=j: e.transpose(out=P[6][:, 256:384], in_=prevT[:, j * 128:(j + 1) * 128], identity=ident_f[:]), reads=[("prevT", j), "ident_f"], writes=[PK(6, 2)])
        S.op("act", lambda e, so=so: e.copy(out=so[:], in_=P[6][:, 256:384]), reads=[PK(6, 2)], writes=[sok])
        S.dma("sp", o_ssm_p[j * 128:(j + 1) * 128, :], so[:], reads=[sok], writes=[("o_ssm_p", j)])
    S.dma("sp", o_gdn_p.rearrange("h k v -> k h v"), Sg[:], reads=["Sg"], writes=["o_gdn_p"])


_OUT_NAMES = ["y_p", "y_s", "o_ssm_p", "o_ssmc_p", "o_gdn_p", "o_gdnc_p", "o_ssm_s", "o_ssmc_s", "o_gdn_s", "o_gdnc_s"]


def _cmask():
    idx = np.arange(128)
    out = np.zeros((128, 14, 128), np.float32)
    for j in range(7):
        b = 1 << j
        m = ((idx[:, None] // (2 * b)) == (idx[None, :] // (2 * b))) & ((idx[:, None] // b) != (idx[None, :] // b)) & (idx[:, None] > idx[None, :])
        out[:, j, :] = m
        out[:, 7 + j, :] = m.T
    return out


def kernel(**inp):
    f = lambda a: np.ascontiguousarray(np.asarray(a, dtype=np.float32))
    nc, S = build_nc()
    common = {}
    for name in ["norm_mix", "w_in", "ssm_conv_w", "ssm_conv_b", "ssm_dt_bias", "ssm_a_log", "ssm_d", "ssm_norm",
                 "gdn_conv_w", "gdn_dt_bias", "gdn_a_log", "gdn_norm", "w_branch_ssm", "w_branch_gdn", "w_out",
                 "norm_ffn", "w_ffn_in", "w_ffn_out", "norm_pl", "w_pl_gate", "w_pl_proj"]:
        common[name] = f(inp[name][0])
    common["norm_final"] = f(inp["norm_final"])
    common["cmask_in"] = _cmask()
    in_maps = []
    for c in range(8):
        s0, s1 = c * NSMP, (c + 1) * NSMP
        m = dict(common)
        m["x_p"] = f(inp["x_prompt"][c])
        m["x_s"] = f(inp["x_sample"][s0:s1, 0])
        m["p_p"] = f(inp["p_prompt"][0, c])
        m["p_s"] = f(inp["p_sample"][0, s0:s1, 0])
        m["st_ssm"] = f(np.asarray(inp["state_ssm"][0, s0:s1]).reshape(NSMP, 2048, 128))
        m["st_ssm_conv"] = f(inp["state_ssm_conv"][0, s0:s1])
        m["st_gdn"] = f(inp["state_gdn"][0, s0:s1])
        m["st_gdn_conv"] = f(inp["state_gdn_conv"][0, s0:s1])
        in_maps.append(m)
    res = run_bass_kernel_spmd(nc, in_maps, core_ids=list(range(8)))
    R = res.results
    cat = lambda n: np.concatenate([R[c][n][None] if n.endswith("_p") else R[c][n] for c in range(8)], axis=0)
    y_prompt = cat("y_p")
    y_sample = cat("y_s")[:, None, :]
    new_ssm_p = cat("o_ssm_p").reshape(1, 8, 32, 64, 128)
    new_ssmc_p = cat("o_ssmc_p")[None]
    new_gdn_p = cat("o_gdn_p")[None]
    new_gdnc_p = cat("o_gdnc_p")[None]
    new_ssm_s = cat("o_ssm_s").reshape(1, 128, 32, 64, 128)
    new_ssmc_s = cat("o_ssmc_s")[None]
    new_gdn_s = cat("o_gdn_s")[None]
    new_gdnc_s = cat("o_gdnc_s")[None]
    outs = (y_prompt, y_sample, new_ssm_p, new_ssmc_p, new_gdn_p, new_gdnc_p, new_ssm_s, new_ssmc_s, new_gdn_s, new_gdnc_s)
    return tuple(np.ascontiguousarray(o, dtype=np.float32) for o in outs)
```

```python
from contextlib import ExitStack
import numpy as np
import concourse.bass as bass
import concourse.mybir as mybir
from concourse.bass_utils import run_bass_kernel_spmd

F32 = mybir.dt.float32
BF16 = mybir.dt.bfloat16
AF = mybir.ActivationFunctionType
ALU = mybir.AluOpType
AX = mybir.AxisListType

ENGS = ["pe", "act", "dve", "pool", "sp"]


class _Op:
    __slots__ = ("eng", "fn", "idx", "signal", "deps", "dma", "slot", "val", "ticket")

    def __init__(self, eng, fn, dma):
        self.eng = eng
        self.fn = fn
        self.dma = dma
        self.signal = False
        self.deps = []
        self.slot = None
        self.val = None
        self.ticket = None


class _St:
    __slots__ = ("writer", "readers")

    def __init__(self):
        self.writer = None
        self.readers = []


def _k(key):
    return key if isinstance(key, tuple) else (key,)


class Sched:
    NDMA = 16

    def __init__(self, nc):
        self.nc = nc
        self.stack = ExitStack()
        self.ops = {e: [] for e in ENGS}
        self.res = {}
        self.ndma = 0
        self.ndmaq = {}
        self.dma_prev = {}
        self.finished = False
        self._cap = None

    def sb(self, name, shape, dtype):
        return self.stack.enter_context(self.nc.sbuf_tensor(name, list(shape), dtype))

    def ps(self, name, shape, dtype=F32):
        return self.stack.enter_context(self.nc.psum_tensor(name, list(shape), dtype))

    def _overlaps(self, key):
        grp = self.res.get(key[0])
        if not grp:
            return []
        out = []
        n = len(key)
        for k2, st in grp.items():
            m = min(n, len(k2))
            if key[:m] == k2[:m]:
                out.append(st)
        return out

    def _state(self, key):
        grp = self.res.setdefault(key[0], {})
        st = grp.get(key)
        if st is None:
            st = grp[key] = _St()
        return st

    def _add(self, op, reads, writes):
        reads = [_k(r) for r in reads]
        writes = [_k(w) for w in writes]
        deps = {}

        def need(d, raw):
            if d is op:
                return
            if (not d.dma) and (not op.dma) and d.eng == op.eng and not raw and op.eng != "pool":
                return
            if d.dma:
                deps[id(d)] = d
            else:
                cur = deps.get(d.eng)
                if cur is None or cur.idx < d.idx:
                    deps[d.eng] = d

        for k in reads:
            psum = k[0][0] == "P" and k[0][1:].isdigit()
            for st in self._overlaps(k):
                if st.writer is not None:
                    need(st.writer, True)
                if psum:
                    for r in st.readers:
                        if r.eng != op.eng:
                            need(r, True)
        for k in writes:
            for st in self._overlaps(k):
                if st.writer is not None:
                    need(st.writer, False)
                for r in st.readers:
                    need(r, False)
        for d in deps.values():
            if not d.dma:
                d.signal = True
        op.deps = list(deps.values())
        op.idx = len(self.ops[op.eng])
        self.ops[op.eng].append(op)
        for k in reads:
            self._state(k).readers.append(op)
        for k in writes:
            st = self._state(k)
            st.writer = op
            st.readers = []
            grp = self.res[k[0]]
            n = len(k)
            for k2 in [k2 for k2 in grp if len(k2) > n and k2[:n] == k]:
                del grp[k2]

    def capture(self, f):
        self._cap = []
        f()
        c, self._cap = self._cap, None
        return c

    def replay(self, items):
        for it in items:
            if it[0] == "op":
                self.op(*it[1:])
            else:
                self.dma(*it[1:])

    @staticmethod
    def interleave(a, b, front=1.0):
        out = []
        ia = ib = 0
        while ia < len(a) or ib < len(b):
            if ib >= len(b) or (ia < len(a) and ia * len(b) * front <= ib * len(a)):
                out.append(a[ia]); ia += 1
            else:
                out.append(b[ib]); ib += 1
        return out

    def op(self, eng, fn, reads=(), writes=()):
        if self._cap is not None:
            self._cap.append(("op", eng, fn, tuple(reads), tuple(writes)))
            return None
        o = _Op(eng, fn, False)
        self._add(o, reads, writes)
        return o

    def dma(self, eng, out, in_, reads=(), writes=(), slow=False):
        if self._cap is not None:
            self._cap.append(("dma", eng, out, in_, tuple(reads), tuple(writes), slow))
            return None
        if slow:
            o = _Op(eng, lambda e: e.dma_start(out=out, in_=in_, allow_slow_non_contiguous=True), True)
        else:
            o = _Op(eng, lambda e: e.dma_start(out=out, in_=in_), True)
        j = self.ndmaq.get(eng, 0)
        self.ndmaq[eng] = j + 1
        self.ndma += 1
        o.slot = (eng, j % self.NDMA)
        o.val = 16 * (j // self.NDMA + 1)
        prev = self.dma_prev.get(o.slot)
        self._add(o, reads, writes)
        if prev is not None:
            o.deps.append(prev)
        self.dma_prev[o.slot] = o
        return o

    def finish(self, final_wait_eng="sp"):
        nc = self.nc
        last = _Op(final_wait_eng, None, False)
        last.deps = [o for e in ENGS for o in self.ops[e] if o.dma]
        for e in ENGS:
            if e != final_wait_eng and self.ops[e]:
                tail = [o for o in self.ops[e] if not o.dma and o.fn is not None]
                if tail:
                    tail[-1].signal = True
                    last.deps.append(tail[-1])
        last.idx = len(self.ops[final_wait_eng])
        self.ops[final_wait_eng].append(last)
        for e in ENGS:
            t = 0
            for o in self.ops[e]:
                if o.signal:
                    t += 1
                    o.ticket = t
        sems = {e: self.stack.enter_context(nc.semaphore("s_" + e)) for e in ENGS}
        dsem = {}
        for q, cnt in self.ndmaq.items():
            for i in range(min(self.NDMA, cnt)):
                dsem[(q, i)] = self.stack.enter_context(nc.semaphore("d_%s_%d" % (q, i)))
        ops = self.ops
        nwaits = [0]

        def emit(eng_name, eng):
            waited = {}
            for o in ops[eng_name]:
                for d in o.deps:
                    if d.dma:
                        key, val = ("d", d.slot), d.val
                        sem = dsem[d.slot]
                    else:
                        key, val = ("e", d.eng), d.ticket
                        sem = sems[d.eng]
                    if waited.get(key, 0) >= val:
                        continue
                    waited[key] = val
                    eng.wait_ge(sem, val)
                    nwaits[0] += 1
                if o.fn is None:
                    continue
                ins = o.fn(eng)
                if o.dma:
                    ins.then_inc(dsem[o.slot], 16)
                elif o.signal:
                    ins.then_inc(sems[eng_name], 1)

        with nc.Block() as block:
            @block.tensor
            def _(e):
                emit("pe", e)

            @block.scalar
            def _(e):
                emit("act", e)

            @block.vector
            def _(e):
                emit("dve", e)

            @block.gpsimd
            def _(e):
                emit("pool", e)

            @block.sync
            def _(e):
                emit("sp", e)
        self.stack.close()
        self.finished = True
        self.stats = {e: len(ops[e]) for e in ENGS}
        self.stats["waits"] = nwaits[0]


D = 1024
NP = 2048
NSMP = 16
SEGLEN = 512
CACHE_W = True
DBG_H = 0
HM = 32
OFF_Z = 0
OFF_XBC = 2048
OFF_DT = 5120
OFF_QKV = 5152
OFF_GATE = 8224
OFF_B = 9248
OFF_A = 9256
OFF_MG = 9264
N_IN = 11312
DFF = 2816
EPS = 1e-6


def _blocks(n):
    out = []
    c = 0
    while c < n:
        w = min(512, n - c)
        out.append((c, w))
        c += w
    return out


class _Stop(Exception):
    pass


STOP_AT = None
INPUT_SHAPES = {}


def build_nc(debug=None):
    nc = bass.Bass("TRN2", target_bir_lowering=False)
    S = Sched(nc)

    cnt = {}

    def ckpt(name):
        if STOP_AT is None:
            return
        cnt[name] = cnt.get(name, 0) + 1
        want, _, k = STOP_AT.partition("#")
        if name == want and cnt[name] == int(k or 1):
            raise _Stop()
    try:
        _build_body(nc, S, debug, ckpt)
    except _Stop:
        pass
    S.finish()
    return nc, S


def _build_body(nc, S, debug, ckpt):

    def din(name, shape):
        INPUT_SHAPES[name] = list(shape)
        return nc.dram_tensor(name, list(shape), F32, kind="ExternalInput").ap()

    def dout(name, shape):
        return nc.dram_tensor(name, list(shape), F32, kind="ExternalOutput").ap()

    x_p = din("x_p", [NP, D]); x_s = din("x_s", [NSMP, D])
    p_p = din("p_p", [NP, 256]); p_s = din("p_s", [NSMP, 256])
    st_ssm = din("st_ssm", [NSMP, 2048, 128])
    st_ssm_conv = din("st_ssm_conv", [NSMP, 3, 3072])
    st_gdn = din("st_gdn", [NSMP, 8, 128, 128])
    st_gdn_conv = din("st_gdn_conv", [NSMP, 3, 3072])
    norm_mix = din("norm_mix", [D]); w_in = din("w_in", [D, N_IN])
    ssm_conv_w = din("ssm_conv_w", [4, 3072]); ssm_conv_b = din("ssm_conv_b", [3072])
    ssm_dt_bias = din("ssm_dt_bias", [32]); ssm_a_log = din("ssm_a_log", [32]); ssm_d = din("ssm_d", [32])
    ssm_norm = din("ssm_norm", [2048]); gdn_conv_w = din("gdn_conv_w", [4, 3072])
    gdn_dt_bias = din("gdn_dt_bias", [8]); gdn_a_log = din("gdn_a_log", [8]); gdn_norm = din("gdn_norm", [128])
    w_bs = din("w_branch_ssm", [2048, D]); w_bg = din("w_branch_gdn", [1024, D]); w_out = din("w_out", [D, D])
    norm_ffn = din("norm_ffn", [D]); w_ffn_in = din("w_ffn_in", [D, 2 * DFF]); w_ffn_out = din("w_ffn_out", [DFF, D])
    norm_pl = din("norm_pl", [D]); w_pl_gate = din("w_pl_gate", [D, D]); w_pl_proj = din("w_pl_proj", [256, D])
    norm_final = din("norm_final", [D])

    y_p = dout("y_p", [NP, D]); y_s = dout("y_s", [NSMP, D])
    o_ssm_p = dout("o_ssm_p", [2048, 128]); o_ssmc_p = dout("o_ssmc_p", [3, 3072])
    o_gdn_p = dout("o_gdn_p", [8, 128, 128]); o_gdnc_p = dout("o_gdnc_p", [3, 3072])
    o_ssm_s = dout("o_ssm_s", [NSMP, 2048, 128]); o_ssmc_s = dout("o_ssmc_s", [NSMP, 3, 3072])
    o_gdn_s = dout("o_gdn_s", [NSMP, 8, 128, 128]); o_gdnc_s = dout("o_gdnc_s", [NSMP, 3, 3072])

    dbg_outs = {}

    def dbg(name, ap, keys, shape):
        if debug is None or name not in debug:
            return
        t = nc.dram_tensor("dbg_" + name, list(shape), ap.dtype, kind="ExternalOutput").ap()
        dbg_outs[name] = t
        S.dma("sp", t, ap, reads=list(keys), writes=[("dbg", name)])

    NCMAX = SEGLEN + NSMP
    NTMAX = SEGLEN // 128 + NSMP

    ident_f = S.sb("ident_f", [128, 128], F32)
    ident_b = S.sb("ident_b", [128, 128], BF16)
    ones_f = S.sb("ones_f", [128, 128], F32)
    ones_b = S.sb("ones_b", [128, 128], BF16)
    mU = S.sb("mU", [128, 128], F32)
    mT = S.sb("mT", [128, 128], F32)
    mNEG = S.sb("mNEG", [128, 128], F32)
    mNUS = S.sb("mNUS", [128, 128], F32)
    epsc = S.sb("epsc", [128, 1], F32)
    S.op("pool", lambda e: e.memset(epsc[:], EPS), writes=["epsc"])
    S.op("pool", lambda e: e.memset(ones_f[:], 1.0), writes=["ones_f"])
    S.op("pool", lambda e: e.memset(ones_b[:], 1.0), writes=["ones_b"])
    S.op("pool", lambda e: e.memset(ident_f[:], 1.0), writes=["ident_f"])
    S.op("pool", lambda e: e.affine_select(out=ident_f[:], in_=ident_f[:], pattern=[[1, 128]], compare_op=ALU.is_ge, fill=0.0, base=0, channel_multiplier=-1), reads=["ident_f"], writes=["ident_f"])
    S.op("pool", lambda e: e.affine_select(out=ident_f[:], in_=ident_f[:], pattern=[[-1, 128]], compare_op=ALU.is_ge, fill=0.0, base=0, channel_multiplier=1), reads=["ident_f"], writes=["ident_f"])
    S.op("pool", lambda e: e.tensor_copy(out=ident_b[:], in_=ident_f[:]), reads=["ident_f"], writes=["ident_b"])
    S.op("pool", lambda e: e.memset(mU[:], 1.0), writes=["mU"])
    S.op("pool", lambda e: e.affine_select(out=mU[:], in_=mU[:], pattern=[[1, 128]], compare_op=ALU.is_ge, fill=0.0, base=0, channel_multiplier=-1), reads=["mU"], writes=["mU"])
    S.op("pool", lambda e: e.memset(mT[:], 1.0), writes=["mT"])
    S.op("pool", lambda e: e.affine_select(out=mT[:], in_=mT[:], pattern=[[-1, 128]], compare_op=ALU.is_ge, fill=0.0, base=-1, channel_multiplier=1), reads=["mT"], writes=["mT"])
    S.op("pool", lambda e: e.memset(mNEG[:], 0.0), writes=["mNEG"])
    S.op("pool", lambda e: e.affine_select(out=mNEG[:], in_=mNEG[:], pattern=[[1, 128]], compare_op=ALU.is_ge, fill=-30000.0, base=0, channel_multiplier=-1), reads=["mNEG"], writes=["mNEG"])
    mUS = S.sb("mUS", [128, 128], F32)
    S.op("pool", lambda e: e.memset(mUS[:], 1.0), writes=["mUS"])
    S.op("pool", lambda e: e.affine_select(out=mUS[:], in_=mUS[:], pattern=[[1, 128]], compare_op=ALU.is_ge, fill=0.0, base=-1, channel_multiplier=-1), reads=["mUS"], writes=["mUS"])
    cmask = S.sb("cmask", [128, 14, 128], BF16)
    cmask_d = din("cmask_in", [128, 14, 128])
    S.dma("pool", cmask[:], cmask_d, writes=["cmask"])
    S.op("pool", lambda e: e.memset(mNUS[:], -1.0), writes=["mNUS"])
    S.op("pool", lambda e: e.affine_select(out=mNUS[:], in_=mNUS[:], pattern=[[1, 128]], compare_op=ALU.is_ge, fill=0.0, base=-1, channel_multiplier=-1), reads=["mNUS"], writes=["mNUS"])

    def col_param(name, src, nk):
        t = S.sb(name, [128, nk], F32)
        S.dma("sp", t[:], src.rearrange("(k p) -> p k", p=128), writes=[name], slow=True)
        return t

    g_mix = col_param("g_mix", norm_mix, 8)
    g_ffn = col_param("g_ffn", norm_ffn, 8)
    g_pl = col_param("g_pl", norm_pl, 8)
    g_fin = col_param("g_fin", norm_final, 8)
    g_ssm = col_param("g_ssm", ssm_norm, 16)
    g_gdn = col_param("g_gdn", gdn_norm, 1)
    cb_ssm = col_param("cb_ssm", ssm_conv_b, 24)
    cw_ssm = S.sb("cw_ssm", [128, 4, 24], F32)
    cw_gdn = S.sb("cw_gdn", [128, 4, 24], F32)
    for j in range(4):
        S.dma("sp", cw_ssm[:, j, :], ssm_conv_w[j].rearrange("(k p) -> p k", p=128), writes=[("cw_ssm", j)], slow=True)
        S.dma("sp", cw_gdn[:, j, :], gdn_conv_w[j].rearrange("(k p) -> p k", p=128), writes=[("cw_gdn", j)], slow=True)
    hp = S.sb("hp", [32, 8], F32)
    S.dma("sp", hp[:, 0:1], ssm_dt_bias.rearrange("(p o) -> p o", o=1), writes=[("hp", 0)], slow=True)
    S.dma("sp", hp[:, 1:2], ssm_a_log.rearrange("(p o) -> p o", o=1), writes=[("hp", 1)], slow=True)
    S.dma("sp", hp[0:8, 3:4], gdn_dt_bias.rearrange("(p o) -> p o", o=1), writes=[("hp", 3)], slow=True)
    S.dma("sp", hp[0:8, 4:5], gdn_a_log.rearrange("(p o) -> p o", o=1), writes=[("hp", 4)], slow=True)
    S.dma("sp", hp[:, 6:7], ssm_d.rearrange("(p o) -> p o", o=1), writes=[("hp", 6)], slow=True)
    S.op("act", lambda e: e.activation(out=hp[:, 2:3], in_=hp[:, 1:2], func=AF.Exp), reads=[("hp", 1)], writes=[("hp", 2)])
    S.op("act", lambda e: e.mul(out=hp[:, 2:3], in_=hp[:, 2:3], mul=-1.0), reads=[("hp", 2)], writes=[("hp", 2)])
    S.op("act", lambda e: e.activation(out=hp[0:8, 5:6], in_=hp[0:8, 4:5], func=AF.Exp), reads=[("hp", 4)], writes=[("hp", 5)])
    S.op("act", lambda e: e.mul(out=hp[0:8, 5:6], in_=hp[0:8, 5:6], mul=-1.0), reads=[("hp", 5)], writes=[("hp", 5)])
    dskB = S.sb("dskB", [128, 32], F32)
    S.dma("sp", dskB[:], ssm_d.partition_broadcast(128), writes=["dskB"])

    ckpt("params")
    P = [S.ps("P%d" % i, [128, 512], F32) for i in range(8)]

    xT = S.sb("xT", [128, 8, NCMAX], F32)
    hT = S.sb("hT", [128, 8, NCMAX], BF16)
    yT = S.sb("yT", [128, 16, NCMAX], BF16)
    oT = S.sb("oT", [128, 8, NCMAX], BF16)
    NB4 = 10
    arena = S.sb("arena", [128, max(NB4 * (SEGLEN // 128) * 128, 8 * NCMAX)], BF16)
    mixT = arena[:, 0:8 * NCMAX].rearrange("p (k c) -> p k c", k=8)
    NWST = 3
    wst = [S.sb("wst%d" % i, [128, 16, 128], BF16) for i in range(NWST)]
    wcnt = [0]

    NWT = 200
    wscr = nc.dram_tensor("wscr", [NWT, 128, 16 * 128], BF16).ap()
    wcache = {}

    def load_w(Wap, kc, m0, mw=128, r0=0):
        i = wcnt[0] % NWST
        wcnt[0] += 1
        ck = (Wap.name if hasattr(Wap, "name") else id(Wap), kc, m0, mw, r0)
        if ck in wcache:
            idx = wcache[ck]
            S.dma("sp", wst[i][:, 0:kc, 0:mw], wscr[idx, :, 0:kc * mw].rearrange("p (k m) -> p k m", m=mw), reads=[("wscr", idx)], writes=[("wst", i)])
            return wst[i], ("wst", i)
        S.dma("pool", wst[i][:, 0:kc, 0:mw], Wap[r0:r0 + kc * 128, m0:m0 + mw].rearrange("(kc p) m -> p kc m", p=128), writes=[("wst", i)])
        if CACHE_W and len(wcache) < NWT:
            idx = len(wcache)
            wcache[ck] = idx
            S.dma("sp", wscr[idx, :, 0:kc * mw].rearrange("p (k m) -> p k m", m=mw), wst[i][:, 0:kc, 0:mw], reads=[("wst", i)], writes=[("wscr", idx)])
        return wst[i], ("wst", i)

    def mm_acc(pbanks, mw, wbuf, wkey, kc, rhs_fn, rhs_keys, blks, first=True, last=True):
        for k in range(kc):
            for bi, (c0, n) in enumerate(blks):
                S.op("pe", lambda e, k=k, bi=bi, c0=c0, n=n: e.matmul(pbanks[bi][0:mw, 0:n], wbuf[:, k, 0:mw], rhs_fn(k, c0, n), start=(first and k == 0), stop=(last and k == kc - 1)),
                     reads=[wkey] + list(rhs_keys), writes=[("P", pbanks[bi].name)])

    def pk(t):
        return ("P", t.name)

    sqb = S.sb("sqb", [128, 8, 256], BF16)
    rstd = S.sb("rstd", [128, 512], F32)

    def PK(i, r=None):
        return ("P%d" % i,)

    def _blocks256(n):
        return [(c, min(256, n - c)) for c in range(0, n, 256)]

    def rmsnorm(src, skey, gcol, dst, dkey, blks):
        for (c0, n) in blks:
            S.op("act", lambda e, c0=c0, n=n: e.activation(out=sqb[:, :, 0:n], in_=src[:, :, c0:c0 + n], func=AF.Square), reads=[skey], writes=["sqb"])
            for k in range(8):
                S.op("pe", lambda e, k=k, n=n: e.matmul(P[7][:, 0:n], ones_b[:], sqb[:, k, 0:n], start=(k == 0), stop=(k == 7)), reads=["ones_b", "sqb"], writes=[PK(7)])
            S.op("act", lambda e, n=n: e.activation(out=rstd[:, 0:n], in_=P[7][:, 0:n], func=AF.Ln, bias=epsc[:, 0:1], scale=1.0 / D), reads=[PK(7), "epsc"], writes=["rstd"])
            S.op("act", lambda e, n=n: e.activation(out=rstd[:, 0:n], in_=rstd[:, 0:n], func=AF.Exp, scale=-0.5), reads=["rstd"], writes=["rstd"])
            for k in range(8):
                S.op("dve", lambda e, k=k, c0=c0, n=n: e.scalar_tensor_tensor(out=dst[:, k, c0:c0 + n], in0=src[:, k, c0:c0 + n], scalar=gcol[:, k:k + 1], in1=rstd[:, 0:n], op0=ALU.mult, op1=ALU.mult),
                     reads=[skey, "rstd", gcol.name], writes=[dkey])

    class Rot:
        def __init__(self, name, shape, dtype, n=2):
            self.bufs = [S.sb("%s%d" % (name, i), shape, dtype) for i in range(n)]
            self.i = 0
            self.name = name

        def get(self):
            j = self.i % len(self.bufs)
            self.i += 1
            return self.bufs[j], (self.name, j)

    def PK(i, r=None):
        return ("P%d" % i,)

    def mm(pi, mw, wbuf, wkey, kc, rhs_fn, rhs_keys, blks, pbase=0):
        for k in range(kc):
            for bi, (c0, n) in enumerate(blks):
                S.op("pe", lambda e, k=k, bi=bi, c0=c0, n=n: e.matmul(P[pbase + bi][0:mw, 0:n], wbuf[:, k, 0:mw], rhs_fn(k, c0, n), start=(k == 0), stop=(k == kc - 1)),
                     reads=[wkey] + list(rhs_keys), writes=[PK(pbase + bi)])

    xtok = S.sb("xtok", [128, 1024], F32)
    ubuf = S.sb("ubuf", [128, 3 + NCMAX], F32)
    cacc = S.sb("cacc", [128, NCMAX], F32)
    cacc2 = S.sb("cacc2", [128, NCMAX], F32)
    xsT = S.sb("xsT", [128, NCMAX], F32)
    szT = S.sb("szT", [128, NCMAX], F32)
    bmT = S.sb("bmT", [128, NCMAX], BF16)
    cmT = S.sb("cmT", [128, NCMAX], BF16)
    yg = S.sb("yg", [128, 4, NCMAX], F32)
    qT = S.sb("qT", [128, NCMAX], BF16)
    kT = S.sb("kT", [128, NCMAX], BF16)
    vT = S.sb("vT", [128, NCMAX], BF16)
    sgate = S.sb("sgate", [128, NCMAX], F32)
    class NS_:
        pass
    xsT2 = S.sb("xsT2", [128, NCMAX], F32)
    szT2 = S.sb("szT2", [128, NCMAX], F32)
    qT2 = S.sb("qT2", [128, NCMAX], BF16)
    kT2 = S.sb("kT2", [128, NCMAX], BF16)
    vT2 = S.sb("vT2", [128, NCMAX], BF16)
    sgate2 = S.sb("sgate2", [128, NCMAX], F32)
    BSs, BGs = [], []
    for i_, (a_, b_) in enumerate(((xsT, szT), (xsT2, szT2))):
        o_ = NS_(); o_.xsT, o_.szT, o_.kxs, o_.kzs = a_, b_, "xsT%d" % i_, "szT%d" % i_
        BSs.append(o_)
    for i_, (a_, b_, c_, d_) in enumerate(((qT, kT, vT, sgate), (qT2, kT2, vT2, sgate2))):
        o_ = NS_(); o_.qT, o_.kT, o_.vT, o_.sgate = a_, b_, c_, d_
        o_.kq, o_.kk, o_.kv, o_.kg = "qT%d" % i_, "kT%d" % i_, "vT%d" % i_, "sgate%d" % i_
        BGs.append(o_)
    dtT = S.sb("dtT", [32, 2, NCMAX], F32)
    gbT = S.sb("gbT", [8, 2, NCMAX], F32)
    tokS = S.sb("tokS", [128, SEGLEN // 128, 5, 32], F32)
    tokG = S.sb("tokG", [128, NTMAX, 6, 8], F32)
    cbm = S.sb("cbm", [128, SEGLEN // 128, 128], BF16)
    bmtok = S.sb("bmtok", [128, SEGLEN // 128, 128], BF16)
    prevT = S.sb("prevT", [128, 2048], F32)
    prevTb = S.sb("prevTb", [128, 2048], BF16)
    Sg = S.sb("Sg", [128, 8, 128], F32)
    Sgb = S.sb("Sgb", [128, 8, 128], BF16)
    tails = {"ssm": S.sb("tl_ssm", [128, 24, 3], F32), "gdn": S.sb("tl_gdn", [128, 24, 3], F32)}
    S.op("pool", lambda e: e.memset(tails["ssm"][:], 0.0), writes=["tl_ssm"])
    S.op("pool", lambda e: e.memset(tails["gdn"][:], 0.0), writes=["tl_gdn"])
    S.op("pool", lambda e: e.memset(prevT[:], 0.0), writes=["prevT"])
    S.op("pool", lambda e: e.memset(prevTb[:], 0.0), writes=["prevTb"])
    S.op("pool", lambda e: e.memset(Sg[:], 0.0), writes=["Sg"])
    S.op("pool", lambda e: e.memset(Sgb[:], 0.0), writes=["Sgb"])
    S.dma("sp", o_ssmc_s[:, 0:2, :], st_ssm_conv[:, 1:3, :], writes=["o_ssmc_s01"])
    S.dma("sp", o_gdnc_s[:, 0:2, :], st_gdn_conv[:, 1:3, :], writes=["o_gdnc_s01"])

    r_sc_in = Rot("sc_in", [3 * NSMP, 128], F32)
    r_sc = Rot("sc", [128, 3 * NSMP], F32)
    r_tl = Rot("tlo", [3 + NSMP, 128], F32)
    r_f = Rot("tf", [128, 128], F32, 3)
    r_gv = Rot("gv", [128, 128], BF16, 2)
    r_gk = Rot("gk", [128, 128], BF16, 2)
    r_gq = Rot("gq", [128, 128], BF16, 2)
    r_gn = Rot("gn", [128, 128], BF16, 2)
    r_gy = Rot("gy", [128, 128], BF16, 2)
    r_sst = Rot("sst", [128, 128], F32, 1)
    r_sprev = Rot("sprev", [128, 128], F32, 2)
    r_sprevb = Rot("sprevb", [128, 128], BF16, 2)
    r_c1 = Rot("c1", [128, 1], F32, 4)
    class Alias:
        def __init__(self, items):
            self.items = items
            self.i = 0

        def get(self):
            j = self.i % len(self.items)
            self.i += 1
            return self.items[j]

    r_sig = Alias([(yg[:, 0, :], ("yg", 0)), (yg[:, 1, :], ("yg", 1))])
    r_t1 = Alias([(yg[:, 2, :], ("yg", 2)), (yg[:, 3, :], ("yg", 3))])

    conv_w = {"ssm": cw_ssm, "gdn": cw_gdn}
    conv_in = {"ssm": st_ssm_conv, "gdn": st_gdn_conv}
    conv_out_p = {"ssm": o_ssmc_p, "gdn": o_gdnc_p}
    conv_out_s = {"ssm": o_ssmc_s, "gdn": o_gdnc_s}

    def conv_tile(kind, ci, npr, ns, blks, dst, dkey, seg_last):
        cw = conv_w[kind]
        wk = "cw_" + kind
        tl = tails[kind]
        tlk = ("tl_" + kind, ci)
        NC = npr + ns
        for bi, (c0, n) in enumerate(blks):
            S.op("act", lambda e, bi=bi, c0=c0, n=n: e.copy(out=ubuf[:, 3 + c0:3 + c0 + n], in_=P[bi][:, 0:n]), reads=[PK(bi)], writes=[("ubuf", 1 + bi)])
        S.op("pool", lambda e: e.tensor_copy(out=ubuf[:, 0:3], in_=tl[:, ci, :]), reads=[tlk], writes=[("ubuf", 0)])
        if kind == "ssm":
            S.op("dve", lambda e: e.tensor_scalar(out=cacc[:, 0:npr], in0=ubuf[:, 0:npr], scalar1=cw[:, 0, ci:ci + 1], scalar2=cb_ssm[:, ci:ci + 1], op0=ALU.mult, op1=ALU.add), reads=["ubuf", wk, "cb_ssm"], writes=["cacc"])
        else:
            S.op("dve", lambda e: e.tensor_scalar(out=cacc[:, 0:npr], in0=ubuf[:, 0:npr], scalar1=cw[:, 0, ci:ci + 1], scalar2=None, op0=ALU.mult), reads=["ubuf", wk], writes=["cacc"])
        for j in range(1, 4):
            S.op("dve", lambda e, j=j: e.scalar_tensor_tensor(out=cacc[:, 0:npr], in0=ubuf[:, j:j + npr], scalar=cw[:, j, ci:ci + 1], in1=cacc[:, 0:npr], op0=ALU.mult, op1=ALU.add), reads=["ubuf", wk, "cacc"], writes=["cacc"])
        S.op("pool", lambda e: e.tensor_copy(out=tl[:, ci, :], in_=ubuf[:, npr:npr + 3]), reads=["ubuf"], writes=[tlk])
        if ns:
            sci, scik = r_sc_in.get()
            sc, sck = r_sc.get()
            S.dma("sp", sci[:], conv_in[kind].rearrange("b j c -> (b j) c")[:, ci * 128:(ci + 1) * 128], writes=[scik])
            S.op("pe", lambda e: e.transpose(out=P[1][:, 256:256 + 3 * ns], in_=sci[:], identity=ident_f[0:3 * ns, 0:3 * ns]), reads=[scik, "ident_f"], writes=[PK(1)])
            S.op("act", lambda e: e.copy(out=sc[:], in_=P[1][:, 256:256 + 3 * ns]), reads=[PK(1)], writes=[sck])
            scv = sc[:].rearrange("p (b j) -> p j b", j=3)
            sacc = cacc[:, npr:NC]
            if kind == "ssm":
                S.op("dve", lambda e: e.tensor_scalar(out=sacc, in0=scv[:, 0, :], scalar1=cw[:, 0, ci:ci + 1], scalar2=cb_ssm[:, ci:ci + 1], op0=ALU.mult, op1=ALU.add), reads=[sck, wk, "cb_ssm"], writes=["cacc"])
            else:
                S.op("dve", lambda e: e.tensor_scalar(out=sacc, in0=scv[:, 0, :], scalar1=cw[:, 0, ci:ci + 1], scalar2=None, op0=ALU.mult), reads=[sck, wk], writes=["cacc"])
            for j in (1, 2):
                S.op("dve", lambda e, j=j: e.scalar_tensor_tensor(out=sacc, in0=scv[:, j, :], scalar=cw[:, j, ci:ci + 1], in1=sacc, op0=ALU.mult, op1=ALU.add), reads=[sck, wk, "cacc"], writes=["cacc"])
            S.op("dve", lambda e: e.scalar_tensor_tensor(out=sacc, in0=ubuf[:, 3 + npr:3 + NC], scalar=cw[:, 3, ci:ci + 1], in1=sacc, op0=ALU.mult, op1=ALU.add), reads=["ubuf", wk, "cacc"], writes=["cacc"])
        if seg_last:
            nt = 3 + ns
            tlo, tlok = r_tl.get()
            S.op("pe", lambda e: e.transpose(out=P[1][0:nt, 128:256], in_=ubuf[:, npr:npr + nt], identity=ident_f[:]), reads=["ubuf", "ident_f"], writes=[PK(1)])
            S.op("act", lambda e: e.copy(out=tlo[0:nt, :], in_=P[1][0:nt, 128:256]), reads=[PK(1)], writes=[tlok])
            S.dma("sp", conv_out_p[kind][:, ci * 128:(ci + 1) * 128], tlo[0:3, :], reads=[tlok], writes=[("o_cp", kind, ci)])
            if ns:
                S.dma("sp", conv_out_s[kind][:, 2, ci * 128:(ci + 1) * 128], tlo[3:3 + ns, :], reads=[tlok], writes=[("o_cs", kind, ci)])
        S.op("act", lambda e: e.activation(out=dst[:, 0:NC], in_=cacc[:, 0:NC], func=AF.Silu), reads=["cacc"], writes=[dkey])

    def proj_tile(col0, mw, blks):
        wb, wk = load_w(w_in, 8, col0, mw)
        mm(0, mw, wb, wk, 8, lambda k, c0, n: hT[:, k, c0:c0 + n], ["hT"], blks)

    nseg = NP // SEGLEN
    SEGS = [(i * SEGLEN, SEGLEN, NSMP if i == nseg - 1 else 0) for i in range(nseg)]

    def segment(si, t0, npr, ns):
        NC = npr + ns
        blks = _blocks(NC)
        seg_last = (si == len(SEGS) - 1)
        tiles = [(i * 128, 128, None) for i in range(npr // 128)] + [(npr + b, 1, b) for b in range(ns)]
        NT = len(tiles)

        def load_xT(dst, dkey, src_p, src_s, width, nkc):
            for ti in range(npr // 128 + (1 if ns else 0)):
                if ti < npr // 128:
                    rows, c0, src = 128, ti * 128, src_p[t0 + ti * 128:t0 + (ti + 1) * 128, :]
                else:
                    rows, c0, src = ns, npr, src_s[:, :]
                S.dma("sp", xtok[0:rows, 0:width], src, writes=["xtok"])
                for k in range(nkc):
                    S.op("pe", lambda e, k=k, rows=rows: e.transpose(out=P[2 + k % 2][:, (k // 2 % 4) * 128:(k // 2 % 4) * 128 + rows], in_=xtok[0:rows, k * 128:(k + 1) * 128], identity=ident_f[0:rows, 0:rows]),
                         reads=["xtok", "ident_f"], writes=[PK(2 + k % 2, k // 2 % 4)])
                    S.op("act" if k % 2 else "dve", (lambda e, k=k, rows=rows, c0=c0: e.copy(out=dst[:, k, c0:c0 + rows], in_=P[2 + k % 2][:, (k // 2 % 4) * 128:(k // 2 % 4) * 128 + rows])) if k % 2 else
                         (lambda e, k=k, rows=rows, c0=c0: e.tensor_copy(out=dst[:, k, c0:c0 + rows], in_=P[2 + k % 2][:, (k // 2 % 4) * 128:(k // 2 % 4) * 128 + rows])),
                         reads=[PK(2 + k % 2, k // 2 % 4)], writes=[(dkey, k)])

        load_xT(xT, "xT", x_p, x_s, 1024, 8)
        ckpt("loadx")
        rmsnorm(xT, "xT", g_mix, hT, "hT", _blocks256(NC))
        ckpt("norm1")

        proj_tile(OFF_DT, 32, blks)
        for bi, (c0, n) in enumerate(blks):
            S.op("act", lambda e, bi=bi, c0=c0, n=n: e.activation(out=dtT[:, 0, c0:c0 + n], in_=P[bi][0:32, 0:n], func=AF.Exp, bias=hp[:, 0:1], scale=1.0), reads=[PK(bi), "hp"], writes=["dtT"])
        S.op("act", lambda e: e.activation(out=dtT[:, 0, 0:NC], in_=dtT[:, 0, 0:NC], func=AF.Ln, bias=1.0, scale=1.0), reads=["dtT"], writes=["dtT"])
        S.op("dve", lambda e: e.tensor_scalar(out=dtT[:, 1, 0:NC], in0=dtT[:, 0, 0:NC], scalar1=hp[:, 2:3], scalar2=None, op0=ALU.mult), reads=["dtT", "hp"], writes=["dtT"])
        proj_tile(OFF_A, 8, blks)
        for bi, (c0, n) in enumerate(blks):
            S.op("act", lambda e, bi=bi, c0=c0, n=n: e.activation(out=gbT[:, 0, c0:c0 + n], in_=P[bi][0:8, 0:n], func=AF.Exp, bias=hp[0:8, 3:4], scale=1.0), reads=[PK(bi), "hp"], writes=["gbT"])
        S.op("act", lambda e: e.activation(out=gbT[:, 0, 0:NC], in_=gbT[:, 0, 0:NC], func=AF.Ln, bias=1.0, scale=1.0), reads=["gbT"], writes=["gbT"])
        S.op("dve", lambda e: e.tensor_scalar(out=gbT[:, 0, 0:NC], in0=gbT[:, 0, 0:NC], scalar1=hp[0:8, 5:6], scalar2=None, op0=ALU.mult), reads=["gbT", "hp"], writes=["gbT"])
        proj_tile(OFF_B, 8, blks)
        for bi, (c0, n) in enumerate(blks):
            S.op("act", lambda e, bi=bi, c0=c0, n=n: e.activation(out=gbT[:, 1, c0:c0 + n], in_=P[bi][0:8, 0:n], func=AF.Sigmoid), reads=[PK(bi)], writes=["gbT"])

        ckpt("smallproj")
        for ti, (c0, W, sb_) in enumerate(tiles):
            if sb_ is None:
                S.op("pe", lambda e, c0=c0, W=W: e.transpose(out=P[4][0:W, 0:32], in_=dtT[:, 0, c0:c0 + W], identity=ident_f[0:32, 0:32]), reads=["dtT", "ident_f"], writes=[PK(4)])
                S.op("pe", lambda e, c0=c0, W=W: e.transpose(out=P[4][0:W, 32:64], in_=dtT[:, 1, c0:c0 + W], identity=ident_f[0:32, 0:32]), reads=["dtT", "ident_f"], writes=[PK(4)])
                S.op("act", lambda e, ti=ti, W=W: e.copy(out=tokS[0:W, ti, 0:2, :], in_=P[4][0:W, 0:64].rearrange("p (a h) -> p a h", a=2)), reads=[PK(4)], writes=[("tokS", ti)])
                S.op("pe", lambda e, ti=ti, W=W: e.matmul(P[4][0:W, 64:96], mU[0:W, 0:W], tokS[0:W, ti, 1, :], start=True, stop=True), reads=["mU", ("tokS", ti)], writes=[PK(4)])
                S.op("pe", lambda e, ti=ti, W=W: e.matmul(P[4][:, 96:128], ones_f[0:W, :], tokS[0:W, ti, 1, :], start=True, stop=True), reads=["ones_f", ("tokS", ti)], writes=[PK(4)])
                S.op("act", lambda e, ti=ti, W=W: e.activation(out=tokS[0:W, ti, 3, :], in_=P[4][0:W, 64:96], func=AF.Exp), reads=[PK(4)], writes=[("tokS", ti)])
                S.op("act", lambda e, ti=ti: e.activation(out=tokS[:, ti, 4, :], in_=P[4][:, 96:128], func=AF.Exp), reads=[PK(4)], writes=[("tokS", ti)])
                S.op("dve", lambda e, ti=ti, W=W: e.tensor_copy(out=tokS[0:W, ti, 2, :], in_=P[4][0:W, 64:96]), reads=[PK(4)], writes=[("tokS", ti)])
                S.op("dve", lambda e, ti=ti, W=W: e.tensor_tensor(out=tokS[0:W, ti, 2, :], in0=P[4][0:W, 96:128], in1=tokS[0:W, ti, 2, :], op=ALU.subtract), reads=[PK(4), ("tokS", ti)], writes=[("tokS", ti)])
                S.op("act", lambda e, ti=ti, W=W: e.activation(out=tokS[0:W, ti, 2, :], in_=tokS[0:W, ti, 2, :], func=AF.Exp), reads=[("tokS", ti)], writes=[("tokS", ti)])
                S.op("dve", lambda e, ti=ti, W=W: e.tensor_tensor(out=tokS[0:W, ti, 2, :], in0=tokS[0:W, ti, 2, :], in1=tokS[0:W, ti, 0, :], op=ALU.mult), reads=[("tokS", ti)], writes=[("tokS", ti)])
            S.op("pe", lambda e, c0=c0, W=W: e.transpose(out=P[4][0:W, 128:136], in_=gbT[:, 0, c0:c0 + W], identity=ident_f[0:8, 0:8]), reads=["gbT", "ident_f"], writes=[PK(4)])
            S.op("pe", lambda e, c0=c0, W=W: e.transpose(out=P[4][0:W, 136:144], in_=gbT[:, 1, c0:c0 + W], identity=ident_f[0:8, 0:8]), reads=["gbT", "ident_f"], writes=[PK(4)])
            S.op("act", lambda e, ti=ti, W=W: e.copy(out=tokG[0:W, ti, 0:2, :], in_=P[4][0:W, 128:144].rearrange("p (a h) -> p a h", a=2)), reads=[PK(4)], writes=[("tokG", ti)])
            S.op("pe", lambda e, ti=ti, W=W: e.matmul(P[4][0:W, 144:152], mU[0:W, 0:W], tokG[0:W, ti, 0, :], start=True, stop=True), reads=["mU", ("tokG", ti)], writes=[PK(4)])
            S.op("pe", lambda e, ti=ti, W=W: e.matmul(P[4][:, 152:160], ones_f[0:W, :], tokG[0:W, ti, 0, :], start=True, stop=True), reads=["ones_f", ("tokG", ti)], writes=[PK(4)])
            S.op("act", lambda e, ti=ti, W=W: e.activation(out=tokG[0:W, ti, 2, :], in_=P[4][0:W, 144:152], func=AF.Exp), reads=[PK(4)], writes=[("tokG", ti)])
            S.op("dve", lambda e, ti=ti, W=W: e.tensor_scalar(out=tokG[0:W, ti, 3, :], in0=tokG[0:W, ti, 2, :], scalar1=-1.0, scalar2=None, op0=ALU.mult), reads=[("tokG", ti)], writes=[("tokG", ti)])
            S.op("act", lambda e, ti=ti: e.activation(out=tokG[:, ti, 5, :], in_=P[4][:, 152:160], func=AF.Exp), reads=[PK(4)], writes=[("tokG", ti)])
            S.op("dve", lambda e, ti=ti, W=W: e.tensor_copy(out=tokG[0:W, ti, 4, :], in_=P[4][0:W, 144:152]), reads=[PK(4)], writes=[("tokG", ti)])
            S.op("dve", lambda e, ti=ti, W=W: e.tensor_tensor(out=tokG[0:W, ti, 4, :], in0=P[4][0:W, 152:160], in1=tokG[0:W, ti, 4, :], op=ALU.subtract), reads=[PK(4), ("tokG", ti)], writes=[("tokG", ti)])
            S.op("act", lambda e, ti=ti, W=W: e.activation(out=tokG[0:W, ti, 4, :], in_=tokG[0:W, ti, 4, :], func=AF.Exp), reads=[("tokG", ti)], writes=[("tokG", ti)])

        ckpt("tokscal")
        def ssd_A(j):
            BS = BSs[j % 2]
            proj_tile(OFF_XBC + j * 128, 128, blks)
            conv_tile("ssm", j, npr, ns, blks, BS.xsT, BS.kxs, seg_last)
            proj_tile(OFF_Z + j * 128, 128, blks)
            for bi, (c0, n) in enumerate(blks):
                S.op("act", lambda e, bi=bi, c0=c0, n=n, BS=BS: e.activation(out=BS.szT[:, c0:c0 + n], in_=P[bi][:, 0:n], func=AF.Silu), reads=[PK(bi)], writes=[BS.kzs])

        S.replay(S.capture(lambda: ssd_A(0)))
        for g in range(4):
            proj_tile(OFF_XBC + 2048 + g * 128, 128, blks)
            conv_tile("ssm", 16 + g, npr, ns, blks, bmT, "bmT", seg_last)
            proj_tile(OFF_XBC + 2560 + g * 128, 128, blks)
            conv_tile("ssm", 20 + g, npr, ns, blks, cmT, "cmT", seg_last)
            if ns:
                ssd_samples_group(npr, ns)
            for ti, (c0, W, sb_) in enumerate(tiles[:npr // 128]):
                S.op("pe", lambda e, c0=c0, W=W: e.transpose(out=P[2].bitcast(BF16)[0:W, 0:128], in_=bmT[:, c0:c0 + W], identity=ident_b[:]), reads=["bmT", "ident_b"], writes=[PK(2)])
                S.op("act", lambda e, ti=ti, W=W: e.copy(out=bmtok[0:W, ti, :], in_=P[2].bitcast(BF16)[0:W, 0:128]), reads=[PK(2)], writes=[("bmtok", ti)])
                S.op("pe", lambda e, c0=c0, W=W: e.matmul(P[3][0:W, 384:384 + W], bmT[:, c0:c0 + W], cmT[:, c0:c0 + W], start=True, stop=True), reads=["bmT", "cmT"], writes=[PK(3, 3)])
                S.op("dve", lambda e, ti=ti, W=W: e.tensor_tensor(out=cbm[0:W, ti, 0:W], in0=P[3][0:W, 384:384 + W], in1=mU[0:W, 0:W], op=ALU.mult), reads=[PK(3, 3), "mU"], writes=[("cbm", ti)])
            ckpt("ssd_bc")
            for jj in range(4):
                j = 4 * g + jj
                ca = S.capture(lambda: ssd_A(j + 1)) if j + 1 < 16 else []

                def _b():
                    ssd_pair_prompt(g, jj, j, npr, BSs[j % 2])
                    if ns:
                        ssd_samples_pair(jj, j, npr, ns, BSs[j % 2])
                cb = S.capture(_b)
                S.replay(S.interleave(ca, cb, front=1.25))
            ckpt("ssd_steps")
            for (c0, n) in _blocks256(NC):
                S.op("act", lambda e, c0=c0, n=n: e.activation(out=sqb[:, 0:4, 0:n], in_=yg[:, :, c0:c0 + n], func=AF.Square), reads=["yg"], writes=["sqb"])
                for k in range(4):
                    S.op("pe", lambda e, k=k, n=n: e.matmul(P[7][:, 0:n], ones_b[:], sqb[:, k, 0:n], start=(k == 0), stop=(k == 3)), reads=["ones_b", "sqb"], writes=[PK(7)])
                S.op("act", lambda e, n=n: e.activation(out=rstd[:, 0:n], in_=P[7][:, 0:n], func=AF.Ln, bias=epsc[:, 0:1], scale=1.0 / 512), reads=[PK(7), "epsc"], writes=["rstd"])
                S.op("act", lambda e, n=n: e.activation(out=rstd[:, 0:n], in_=rstd[:, 0:n], func=AF.Exp, scale=-0.5), reads=["rstd"], writes=["rstd"])
                for k in range(4):
                    S.op("dve", lambda e, k=k, c0=c0, n=n, g=g: e.scalar_tensor_tensor(out=yT[:, 4 * g + k, c0:c0 + n], in0=yg[:, k, c0:c0 + n], scalar=g_ssm[:, 4 * g + k:4 * g + k + 1], in1=rstd[:, 0:n], op0=ALU.mult, op1=ALU.mult),
                         reads=["yg", "rstd", "g_ssm"], writes=[("yT", 4 * g + k)])

        ckpt("ssd_done")
        scr, kscr = cacc2, "cacc2"

        def gdn_A(h):
            BG = BGs[h % 2]
            for (nm, off, dst, dk_) in (("q", 0, BG.qT, BG.kq), ("k", 1024, BG.kT, BG.kk), ("v", 2048, BG.vT, BG.kv)):
                proj_tile(OFF_QKV + off + h * 128, 128, blks)
                if nm == "v":
                    conv_tile("gdn", (off // 128) + h, npr, ns, blks, dst, dk_, seg_last)
                    continue
                conv_tile("gdn", (off // 128) + h, npr, ns, blks, scr, kscr, seg_last)
                for (c0, n) in _blocks256(NC):
                    S.op("act", lambda e, c0=c0, n=n: e.activation(out=sqb[:, 0, 0:n], in_=scr[:, c0:c0 + n], func=AF.Square), reads=[kscr], writes=["sqb"])
                    S.op("pe", lambda e, n=n: e.matmul(P[1][:, 256:256 + n], ones_b[:], sqb[:, 0, 0:n], start=True, stop=True), reads=["ones_b", "sqb"], writes=[PK(1)])
                    S.op("act", lambda e, n=n: e.activation(out=rstd[:, 0:n], in_=P[1][:, 256:256 + n], func=AF.Ln, bias=epsc[:, 0:1], scale=1.0), reads=[PK(1), "epsc"], writes=["rstd"])
                    S.op("act", lambda e, n=n: e.activation(out=rstd[:, 0:n], in_=rstd[:, 0:n], func=AF.Exp, scale=-0.5), reads=["rstd"], writes=["rstd"])
                    if nm == "q":
                        S.op("dve", lambda e, c0=c0, n=n, dst=dst: e.scalar_tensor_tensor(out=dst[:, c0:c0 + n], in0=scr[:, c0:c0 + n], scalar=128.0 ** -0.5, in1=rstd[:, 0:n], op0=ALU.mult, op1=ALU.mult), reads=[kscr, "rstd"], writes=[dk_])
                    else:
                        S.op("dve", lambda e, c0=c0, n=n, dst=dst: e.tensor_tensor(out=dst[:, c0:c0 + n], in0=scr[:, c0:c0 + n], in1=rstd[:, 0:n], op=ALU.mult), reads=[kscr, "rstd"], writes=[dk_])
            proj_tile(OFF_GATE + h * 128, 128, blks)
            for bi, (c0, n) in enumerate(blks):
                S.op("act", lambda e, bi=bi, c0=c0, n=n, BG=BG: e.activation(out=BG.sgate[:, c0:c0 + n], in_=P[bi][:, 0:n], func=AF.Silu), reads=[PK(bi)], writes=[BG.kg])

        S.replay(S.capture(lambda: gdn_A(0)))
        for h in range(8):
            ca = S.capture(lambda: gdn_A(h + 1)) if h + 1 < 8 else []
            BG = BGs[h % 2]

            cbp = S.capture(lambda: gdn_head_prompt(h, npr, BG))

            def _bs():
                for ti, (c0, W, sb_) in enumerate(tiles):
                    if sb_ is not None:
                        gdn_step(h, ti, c0, W, sb_, BG)
                if ns:
                    gdn_samples_norm(h, npr, ns, BG)
            cbs = S.capture(_bs)
            S.replay(S.interleave(S.interleave(ca, cbp, front=1.25), cbs))

        ckpt("gdn_done")
        phase2(si, t0, npr, ns, NC, blks, load_xT)
        ckpt("seg_done")

    def decay_mat(ti, W, col_ap, colkey, pslot):
        tf, tfk = r_f.get()
        S.op("pool", lambda e: e.tensor_scalar(out=tf[0:W, 0:W], in0=mU[0:W, 0:W], scalar1=col_ap, scalar2=None, op0=ALU.mult), reads=["mU", colkey], writes=[tfk])
        ckpt("ssdA3")
        S.op("pe", lambda e: e.matmul(P[3][0:W, pslot * 128:pslot * 128 + W], mT[0:W, 0:W], tf[0:W, 0:W], start=True, stop=False), reads=["mT", tfk], writes=[PK(3, pslot)])
        ckpt("ssdA4")
        S.op("pe", lambda e: e.matmul(P[3][0:W, pslot * 128:pslot * 128 + W], ident_f[0:W, 0:W], mNEG[0:W, 0:W], start=False, stop=True), reads=["ident_f", "mNEG"], writes=[PK(3, pslot)])


    P2b = P[2].bitcast(BF16)
    P7b = P[7].bitcast(BF16)
    bdB = S.sb("bdB", [NSMP, NSMP, 128], BF16)
    bdC = S.sb("bdC", [NSMP, NSMP, 128], BF16)
    esel = S.sb("esel", [32, 2, 64], F32)
    r_s0 = Rot("s0c", [128, 4, 128], F32, 2)
    r_stmp = Rot("stmp", [128, 4, 128], F32, 1)
    r_sm = Rot("ssm16", [128, NSMP], F32, 4)
    r_xtk = Rot("xtk", [NSMP, 128], BF16, 2)

    def ssd_samples_group(npr, ns):
        cs = slice(npr, npr + ns)
        for (srcT, skey, dst, dkey) in ((bmT, "bmT", bdB, "bdB"), (cmT, "cmT", bdC, "bdC")):
            tk_, tkk = r_xtk.get()
            S.op("pe", lambda e, srcT=srcT: e.transpose(out=P2b[0:ns, 0:128], in_=srcT[:, cs], identity=ident_b[:]), reads=[skey, "ident_b"], writes=[PK(2)])
            S.op("act", lambda e, tk_=tk_: e.copy(out=tk_[0:ns, :], in_=P2b[0:ns, 0:128]), reads=[PK(2)], writes=[tkk])
            S.op("pool", lambda e, tk_=tk_, dst=dst: e.tensor_tensor(out=dst[0:ns, 0:ns, :], in0=tk_[0:ns, :].unsqueeze(1).to_broadcast([ns, ns, 128]), in1=ident_b[0:ns, 0:ns].unsqueeze(2).to_broadcast([ns, ns, 128]), op=ALU.mult), reads=[tkk, "ident_b"], writes=[dkey])

    def ssd_samples_pair(jj, j, npr, ns, BS):
        cs = slice(npr, npr + ns)
        h0 = 2 * j
        S.op("pool", lambda e: e.memset(esel[:], 1.0), writes=["esel"])
        S.op("pool", lambda e: e.affine_select(out=esel[:], in_=esel[:], pattern=[[1, 2], [0, 64]], compare_op=ALU.is_equal, fill=0.0, base=h0, channel_multiplier=-1), reads=["esel"], writes=["esel"])
        ev = esel[:].rearrange("h a p -> h (a p)")
        S.op("pe", lambda e: e.matmul(P[4][:, 0:ns], ev, dtT[:, 0, cs], start=True, stop=True), reads=["esel", "dtT"], writes=[PK(4)])
        S.op("pe", lambda e: e.matmul(P[4][:, 32:32 + ns], ev, dtT[:, 1, cs], start=True, stop=True), reads=["esel", "dtT"], writes=[PK(4)])
        S.op("pe", lambda e: e.matmul(P[4][:, 64:65], ev, hp[:, 6:7], start=True, stop=True), reads=["esel", "hp"], writes=[PK(4)])
        xdtE, xdtEk = r_sm.get()
        decE, decEk = r_sm.get()
        dE, dEk = r_c1.get()
        ysum, ysumk = r_sm.get()
        S.op("dve", lambda e: e.tensor_tensor(out=xdtE[:, 0:ns], in0=P[4][:, 0:ns], in1=BS.xsT[:, cs], op=ALU.mult), reads=[PK(4), BS.kxs], writes=[xdtEk])
        S.op("act", lambda e: e.activation(out=decE[:, 0:ns], in_=P[4][:, 32:32 + ns], func=AF.Exp), reads=[PK(4)], writes=[decEk])
        S.op("act", lambda e: e.copy(out=dE[:, 0:1], in_=P[4][:, 64:65]), reads=[PK(4)], writes=[dEk])
        xtk, xtkk = r_xtk.get()
        S.op("pe", lambda e: e.transpose(out=P[2][0:ns, 128:256], in_=xdtE[:, 0:ns], identity=ident_f[:]), reads=[xdtEk, "ident_f"], writes=[PK(2)])
        S.op("act", lambda e: e.copy(out=xtk[0:ns, :], in_=P[2][0:ns, 128:256]), reads=[PK(2)], writes=[xtkk])
        nch = (ns + 3) // 4
        for c in range(nch):
            b0 = 4 * c
            nb = min(4, ns - b0)
            s0c, s0k = r_s0.get()
            tmp, tmpk = r_stmp.get()
            S.dma("sp", s0c[:, 0:nb, :], st_ssm[b0:b0 + nb, j * 128:(j + 1) * 128, :].rearrange("b p d -> p b d"), writes=[s0k])
            S.op("pe", lambda e, b0=b0, nb=nb: e.matmul(P[5][:, 0:nb * 128], xtk[0:ns, :], bdB[0:ns, b0:b0 + nb, :].rearrange("k b d -> k (b d)"), start=True, stop=True), reads=[xtkk, "bdB"], writes=[PK(5)])
            S.op("pe", lambda e, b0=b0, nb=nb: e.matmul(P[6][:, 0:nb * 128], ones_b[0:ns, :], bdC[0:ns, b0:b0 + nb, :].rearrange("k b d -> k (b d)"), start=True, stop=True), reads=["ones_b", "bdC"], writes=[PK(6)])
            S.op("pool", lambda e, s0c=s0c, b0=b0, nb=nb: e.tensor_tensor(out=s0c[:, 0:nb, :], in0=s0c[:, 0:nb, :], in1=decE[:, b0:b0 + nb].unsqueeze(2).to_broadcast([128, nb, 128]), op=ALU.mult), reads=[s0k, decEk], writes=[s0k])
            S.op("dve", lambda e, s0c=s0c, nb=nb: e.tensor_tensor(out=s0c[:, 0:nb, :], in0=s0c[:, 0:nb, :], in1=P[5][:, 0:nb * 128].rearrange("p (b d) -> p b d", d=128), op=ALU.add), reads=[s0k, PK(5)], writes=[s0k])
            S.dma("sp", o_ssm_s[b0:b0 + nb, j * 128:(j + 1) * 128, :].rearrange("b p d -> p b d"), s0c[:, 0:nb, :], reads=[s0k], writes=[("o_ssm_s", j, c)])
            S.op("dve", lambda e, s0c=s0c, tmp=tmp, nb=nb: e.tensor_tensor(out=tmp[:, 0:nb, :], in0=s0c[:, 0:nb, :], in1=P[6][:, 0:nb * 128].rearrange("p (b d) -> p b d", d=128), op=ALU.mult), reads=[s0k, PK(6)], writes=[tmpk])
            S.op("dve", lambda e, tmp=tmp, b0=b0, nb=nb: e.tensor_reduce(out=ysum[:, b0:b0 + nb], in_=tmp[:, 0:nb, :], axis=AX.X, op=ALU.add), reads=[tmpk], writes=[ysumk])
        S.op("dve", lambda e: e.scalar_tensor_tensor(out=ysum[:, 0:ns], in0=BS.xsT[:, cs], scalar=dE[:, 0:1], in1=ysum[:, 0:ns], op0=ALU.mult, op1=ALU.add), reads=[BS.kxs, dEk, ysumk], writes=[ysumk])
        S.op("dve", lambda e: e.tensor_tensor(out=yg[:, jj, cs], in0=ysum[:, 0:ns], in1=BS.szT[:, cs], op=ALU.mult), reads=[ysumk, BS.kzs], writes=[("yg", jj)])

    def ssd_step(g, jj, j, ti, c0, W, sb_):
        h0 = 2 * j
        tk = ("tokS", ti)
        if sb_ is None:
            st_f, stk_f = prevT[:, j * 128:(j + 1) * 128], ("prevT", j)
            st_b, stk_b = prevTb[:, j * 128:(j + 1) * 128], ("prevTb", j)
        else:
            sst, sstk = r_sst.get()
            spv, stk_f = r_sprev.get()
            spb, stk_b = r_sprevb.get()
            S.dma("sp", sst[:], st_ssm[sb_, j * 128:(j + 1) * 128, :], writes=[sstk])
            ckpt("ssdS1")
            S.op("pe", lambda e: e.transpose(out=P[6][:, 128:256], in_=sst[:], identity=ident_f[:]), reads=[sstk, "ident_f"], writes=[PK(6, 1)])
            ckpt("ssdS2")
            S.op("act", lambda e: e.copy(out=spv[:], in_=P[6][:, 128:256]), reads=[PK(6, 1)], writes=[stk_f])
            ckpt("ssdS3")
            S.op("dve", lambda e: e.tensor_copy(out=spb[:], in_=P[6][:, 128:256]), reads=[PK(6, 1)], writes=[stk_b])
            ckpt("ssdS4")
            st_f, st_b = spv[:], spb[:]
        S.op("pe", lambda e: e.transpose(out=P[2][0:W, 128:256], in_=xsT[:, c0:c0 + W], identity=ident_f[:]), reads=["xsT", "ident_f"], writes=[PK(2, 1)])
        ckpt("ssdA")
        xsk, xskk = r_f.get()
        xdt, xdtk = r_b.get()
        xdd, xddk = r_b.get()
        S.op("act", lambda e: e.copy(out=xsk[0:W, :], in_=P[2][0:W, 128:256]), reads=[PK(2, 1)], writes=[xskk])
        ckpt("ssdA1")
        for hh in range(2):
            h = h0 + hh
            S.op("act", lambda e, hh=hh, h=h: e.activation(out=xdt[0:W, hh * 64:(hh + 1) * 64], in_=P[2][0:W, 128 + hh * 64:128 + (hh + 1) * 64], func=AF.Copy, scale=tokS[0:W, ti, 0, h:h + 1]), reads=[PK(2, 1), tk], writes=[xdtk])
            S.op("act", lambda e, hh=hh, h=h: e.activation(out=xdd[0:W, hh * 64:(hh + 1) * 64], in_=P[2][0:W, 128 + hh * 64:128 + (hh + 1) * 64], func=AF.Copy, scale=tokS[0:W, ti, 2, h:h + 1]), reads=[PK(2, 1), tk], writes=[xddk])
            ckpt("ssdA2")
            decay_mat(ti, W, tokS[0:W, ti, 1, h:h + 1], tk, hh)
        ckpt("ssdB")
        Lh, Lhk = r_f2.get()
        MT, MTk = r_b2.get()
        S.op("act", lambda e: e.activation(out=Lh[0:W, :, 0:W], in_=P[3][0:W, 0:256].rearrange("p (a l) -> p a l", a=2)[:, :, 0:W], func=AF.Exp), reads=[PK(3, 0), PK(3, 1)], writes=[Lhk])
        S.op("dve", lambda e: e.tensor_tensor(out=MT[0:W, :, 0:W], in0=Lh[0:W, :, 0:W], in1=cbm[0:W, ti, 0:W].unsqueeze(1).to_broadcast([W, 2, W]), op=ALU.mult), reads=[Lhk, ("cbm", ti)], writes=[MTk])
        ckpt("ssdC")
        for hh in range(2):
            S.op("pe", lambda e, hh=hh: e.matmul(P[5][0:W, hh * 64:(hh + 1) * 64], MT[0:W, hh, 0:W], xdt[0:W, hh * 64:(hh + 1) * 64], start=True, stop=True), reads=[MTk, xdtk], writes=[PK(5, 0)])
        S.op("pe", lambda e: e.matmul(P[5][0:W, 128:256], cmT[:, c0:c0 + W], st_b, start=True, stop=True), reads=["cmT", stk_b], writes=[PK(5, 1)])
        ckpt("ssdD")
        t1, t1k = r_f.get()
        ys, ysk = r_f.get()
        for hh in range(2):
            h = h0 + hh
            S.op("act", lambda e, hh=hh, h=h: e.activation(out=t1[0:W, hh * 64:(hh + 1) * 64], in_=P[5][0:W, 128 + hh * 64:128 + (hh + 1) * 64], func=AF.Copy, scale=tokS[0:W, ti, 3, h:h + 1]), reads=[PK(5, 1), tk], writes=[t1k])
        ckpt("ssdE")
        S.op("dve", lambda e: e.tensor_tensor(out=ys[0:W, :], in0=P[5][0:W, 0:128], in1=t1[0:W, :], op=ALU.add), reads=[PK(5, 0), t1k], writes=[ysk])
        for hh in range(2):
            h = h0 + hh
            S.op("dve", lambda e, hh=hh, h=h: e.scalar_tensor_tensor(out=ys[0:W, hh * 64:(hh + 1) * 64], in0=xsk[0:W, hh * 64:(hh + 1) * 64], scalar=dskB[0:W, h:h + 1], in1=ys[0:W, hh * 64:(hh + 1) * 64], op0=ALU.mult, op1=ALU.add), reads=[xskk, "dskB", ysk], writes=[ysk])
        ckpt("ssdF")
        S.op("pe", lambda e: e.transpose(out=P[2][:, 256:256 + W], in_=ys[0:W, :], identity=ident_f[0:W, 0:W]), reads=[ysk, "ident_f"], writes=[PK(2, 2)])
        S.op("dve", lambda e: e.tensor_tensor(out=yg[:, jj, c0:c0 + W], in0=P[2][:, 256:256 + W], in1=szT[:, c0:c0 + W], op=ALU.mult), reads=[PK(2, 2), "szT"], writes=[("yg", jj)])
        ckpt("ssdG")
        S.op("pe", lambda e: e.matmul(P[6][:, 0:128], bmtok[0:W, ti, :], xdd[0:W, :], start=True, stop=True), reads=[("bmtok", ti), xddk], writes=[PK(6, 0)])
        for hh in range(2):
            h = h0 + hh
            S.op("dve", lambda e, hh=hh, h=h: e.scalar_tensor_tensor(out=st_f[:, hh * 64:(hh + 1) * 64], in0=st_f[:, hh * 64:(hh + 1) * 64], scalar=tokS[:, ti, 4, h:h + 1], in1=P[6][:, hh * 64:(hh + 1) * 64], op0=ALU.mult, op1=ALU.add), reads=[stk_f, tk, PK(6, 0)], writes=[stk_f])
        if sb_ is None:
            S.op("act", lambda e: e.copy(out=st_b, in_=st_f), reads=[stk_f], writes=[stk_b])
        else:
            so, sok = r_sst.get()
            S.op("pe", lambda e: e.transpose(out=P[6][:, 256:384], in_=st_f, identity=ident_f[:]), reads=[stk_f, "ident_f"], writes=[PK(6, 2)])
            S.op("act", lambda e: e.copy(out=so[:], in_=P[6][:, 256:384]), reads=[PK(6, 2)], writes=[sok])
            S.dma("sp", o_ssm_s[sb_, j * 128:(j + 1) * 128, :], so[:], reads=[sok], writes=[("o_ssm_s", sb_, j)])

    def gdn_step(h, ti, c0, W, sb_, BG):
        tk = ("tokG", ti)
        if sb_ is None:
            st_f, stk_f = Sg[:, h, :], ("Sg", h)
            st_b, stk_b = Sgb[:, h, :], ("Sgb", h)
        else:
            spv, stk_f = r_sprev.get()
            spb, stk_b = r_sprevb.get()
            S.dma("sp", spv[:], st_gdn[sb_, h], writes=[stk_f])
            S.op("pool", lambda e: e.tensor_copy(out=spb[:], in_=spv[:]), reads=[stk_f], writes=[stk_b])
            st_f, st_b = spv[:], spb[:]
        vtok, vtokk = r_gv.get()
        kgk, kgkk = r_gk.get()
        S.op("pe", lambda e: e.transpose(out=P7b[0:W, 0:128], in_=BG.kT[:, c0:c0 + W], identity=ident_b[:]), reads=[BG.kk, "ident_b"], writes=[PK(7)])
        S.op("pe", lambda e: e.transpose(out=P7b[0:W, 128:256], in_=BG.vT[:, c0:c0 + W], identity=ident_b[:]), reads=[BG.kv, "ident_b"], writes=[PK(7)])
        S.op("act", lambda e: e.copy(out=vtok[0:W, :], in_=P7b[0:W, 128:256]), reads=[PK(7)], writes=[vtokk])
        ckpt("g1")
        S.op("act", lambda e: e.activation(out=kgk[0:W, :], in_=P7b[0:W, 0:128], func=AF.Copy, scale=tokG[0:W, ti, 4, h:h + 1]), reads=[PK(7), tk], writes=[kgkk])
        ckpt("g2")
        S.op("pe", lambda e: e.matmul(P[7][0:W, 128:256], BG.kT[:, c0:c0 + W], st_b, start=True, stop=True), reads=[BG.kk, stk_b], writes=[PK(7)])
        Y, Yk = r_f.get()
        Yb, Ybk = r_gy.get()
        S.op("dve", lambda e, Y=Y: e.scalar_tensor_tensor(out=Y[0:W, :], in0=P[7][0:W, 128:256], scalar=tokG[0:W, ti, 3, h:h + 1], in1=vtok[0:W, :], op0=ALU.mult, op1=ALU.add), reads=[PK(7), tk, vtokk], writes=[Yk])
        if W > 1:
            S.op("act", lambda e, Y=Y, Yb=Yb: e.copy(out=Yb[0:W, :], in_=Y[0:W, :]), reads=[Yk], writes=[Ybk])
        ckpt("g3")
        if W > 1:
            decay_mat(ti, W, tokG[0:W, ti, 0, h:h + 1], tk, 2)
            Dge, Dgek = r_f.get()
            S.op("act", lambda e: e.activation(out=Dge[0:W, 0:W], in_=P[3][0:W, 256:256 + W], func=AF.Exp), reads=[PK(3, 2)], writes=[Dgek])
        S.op("pe", lambda e: e.matmul(P[7][0:W, 256:256 + W], BG.kT[:, c0:c0 + W], BG.qT[:, c0:c0 + W], start=True, stop=True), reads=[BG.kk, BG.kq], writes=[PK(7)])
        qkT, qkTk = r_gq.get()
        if W > 1:
            S.op("dve", lambda e: e.tensor_tensor(out=qkT[0:W, 0:W], in0=P[7][0:W, 256:256 + W], in1=Dge[0:W, 0:W], op=ALU.mult), reads=[PK(7), Dgek], writes=[qkTk])
        else:
            S.op("dve", lambda e: e.tensor_copy(out=qkT[0:W, 0:W], in_=P[7][0:W, 256:256 + W]), reads=[PK(7)], writes=[qkTk])
        ckpt("g4")
        if W > 1:
            S.op("pe", lambda e: e.matmul(P[4][0:W, 0:W], BG.kT[:, c0:c0 + W], BG.kT[:, c0:c0 + W], start=True, stop=True), reads=[BG.kk], writes=[PK(4, 0)])
            tmp, tmpk = r_f.get()
            S.op("dve", lambda e: e.scalar_tensor_tensor(out=tmp[0:W, 0:W], in0=P[4][0:W, 0:W], scalar=tokG[0:W, ti, 1, h:h + 1], in1=Dge[0:W, 0:W], op0=ALU.mult, op1=ALU.mult), reads=[PK(4, 0), tk, Dgek], writes=[tmpk])
            MTa, MTak = r_gm.get()
            Ma, Mak = r_gm.get()
            S.op("pool", lambda e: e.tensor_tensor(out=MTa[0:W, 0:W], in0=tmp[0:W, 0:W], in1=mUS[0:W, 0:W], op=ALU.mult), reads=[tmpk, "mUS"], writes=[MTak])
            S.op("pe", lambda e: e.transpose(out=P7b[0:W, 768:768 + W], in_=MTa[0:W, 0:W], identity=ident_b[0:W, 0:W]), reads=[MTak, "ident_b"], writes=[PK(7)])
            S.op("act", lambda e: e.copy(out=Ma[0:W, 0:W], in_=P7b[0:W, 768:768 + W]), reads=[PK(7)], writes=[Mak])
            ckpt("g5")
            nlev = 0
            while (1 << nlev) < W:
                nlev += 1
            T, Tk = r_b.get()
            TT, TTk = r_b.get()
            L0, L0k = r_b.get()
            N0, N0k = r_b.get()
            S.op("pool", lambda e, L0=L0: e.tensor_tensor(out=L0[0:W, 0:W], in0=Ma[0:W, 0:W], in1=cmask[0:W, 0, 0:W], op=ALU.mult), reads=[Mak, "cmask"], writes=[L0k])
            S.op("pool", lambda e, N0=N0: e.tensor_tensor(out=N0[0:W, 0:W], in0=MTa[0:W, 0:W], in1=cmask[0:W, 7, 0:W], op=ALU.mult), reads=[MTak, "cmask"], writes=[N0k])
            S.op("pool", lambda e, T=T, L0=L0: e.tensor_tensor(out=T[0:W, 0:W], in0=ident_b[0:W, 0:W], in1=L0[0:W, 0:W], op=ALU.subtract), reads=["ident_b", L0k], writes=[Tk])
            S.op("pool", lambda e, TT=TT, N0=N0: e.tensor_tensor(out=TT[0:W, 0:W], in0=ident_b[0:W, 0:W], in1=N0[0:W, 0:W], op=ALU.subtract), reads=["ident_b", N0k], writes=[TTk])
            ckpt("g6")
            for lv in range(1, nlev):
                last = (lv == nlev - 1)
                Lj, Ljk = r_b.get()
                S.op("pool", lambda e, Lj=Lj, lv=lv: e.tensor_tensor(out=Lj[0:W, 0:W], in0=Ma[0:W, 0:W], in1=cmask[0:W, lv, 0:W], op=ALU.mult), reads=[Mak, "cmask"], writes=[Ljk])
                Ub, Ubk = r_b.get()
                S.op("pe", lambda e, Lj=Lj, TT=TT: e.matmul(P[5][0:W, 384:384 + W], Lj[0:W, 0:W], TT[0:W, 0:W], start=True, stop=True), reads=[Ljk, TTk], writes=[PK(5, 3)])
                S.op("act", lambda e, Ub=Ub: e.copy(out=Ub[0:W, 0:W], in_=P[5][0:W, 384:384 + W]), reads=[PK(5, 3)], writes=[Ubk])
                S.op("pe", lambda e, T=T, Ub=Ub: e.matmul(P[6][0:W, 128:128 + W], T[0:W, 0:W], Ub[0:W, 0:W], start=True, stop=True), reads=[Tk, Ubk], writes=[PK(6, 1)])
                TTn, TTnk = r_b.get()
                if not last:
                    Nj, Njk = r_b.get()
                    S.op("pool", lambda e, Nj=Nj, lv=lv: e.tensor_tensor(out=Nj[0:W, 0:W], in0=MTa[0:W, 0:W], in1=cmask[0:W, 7 + lv, 0:W], op=ALU.mult), reads=[MTak, "cmask"], writes=[Njk])
                    Zb, Zbk = r_b.get()
                    S.op("pe", lambda e, Nj=Nj, T=T: e.matmul(P[4][0:W, 256:256 + W], Nj[0:W, 0:W], T[0:W, 0:W], start=True, stop=True), reads=[Njk, Tk], writes=[PK(4, 2)])
                    S.op("act", lambda e, Zb=Zb: e.copy(out=Zb[0:W, 0:W], in_=P[4][0:W, 256:256 + W]), reads=[PK(4, 2)], writes=[Zbk])
                    S.op("pe", lambda e, TT=TT, Zb=Zb: e.matmul(P[3][0:W, 0:W], TT[0:W, 0:W], Zb[0:W, 0:W], start=True, stop=True), reads=[TTk, Zbk], writes=[PK(3, 0)])
                    Tn, Tnk = r_b.get()
                    S.op("dve", lambda e, T=T, Tn=Tn: e.tensor_tensor(out=Tn[0:W, 0:W], in0=T[0:W, 0:W], in1=P[3][0:W, 0:W], op=ALU.subtract), reads=[Tk, PK(3, 0)], writes=[Tnk])
                S.op("dve", lambda e, TT=TT, TTn=TTn: e.tensor_tensor(out=TTn[0:W, 0:W], in0=TT[0:W, 0:W], in1=P[6][0:W, 128:128 + W], op=ALU.subtract), reads=[TTk, PK(6, 1)], writes=[TTnk])
                TT, TTk = TTn, TTnk
                if not last:
                    T, Tk = Tn, Tnk
            ckpt("g7")
            S.op("pe", lambda e, TT=TT, Yb=Yb: e.matmul(P[5][0:W, 384:512], TT[0:W, 0:W], Yb[0:W, :], start=True, stop=True), reads=[TTk, Ybk], writes=[PK(5, 3)])
            Y2, Y2k = r_f.get()
            S.op("act", lambda e, Y2=Y2: e.copy(out=Y2[0:W, :], in_=P[5][0:W, 384:512]), reads=[PK(5, 3)], writes=[Y2k])
            Y, Yk = Y2, Y2k
        if h == DBG_H and ti == 0 and c0 == 0 and not dbg_outs.get("_done"):
            dbg("Dge", Dge[:], [Dgek], [128, 128]); dbg("qkT", qkT[:], [qkTk], [128, 128]); dbg("Yfin", Y[:], [Yk], [128, 128])
            dbg("kgk", kgk[:], [kgkk], [128, 128]); dbg("vtok", vtok[:], [vtokk], [128, 128]); dbg("tokG", tokG[:, 0, :, :], [tk], [128, 6, 8])
            dbg(BG.kk, BG.kT[:, 0:128], [BG.kk], [128, 128]); dbg(BG.kq, BG.qT[:, 0:128], [BG.kq], [128, 128])
        vnew, vnewk = r_gn.get()
        S.op("dve", lambda e, Y=Y: e.tensor_scalar(out=vnew[0:W, :], in0=Y[0:W, :], scalar1=tokG[0:W, ti, 1, h:h + 1], scalar2=None, op0=ALU.mult), reads=[Yk, tk], writes=[vnewk])
        ckpt("g8")
        S.op("pe", lambda e: e.matmul(P[7][0:W, 256:384], BG.qT[:, c0:c0 + W], st_b, start=True, stop=True), reads=[BG.kq, stk_b], writes=[PK(7)])
        S.op("pe", lambda e: e.matmul(P[7][0:W, 384:512], qkT[0:W, 0:W], vnew[0:W, :], start=True, stop=True), reads=[qkTk, vnewk], writes=[PK(7)])
        t1, t1k = r_f.get()
        o, ok_ = r_f.get()
        S.op("act", lambda e: e.activation(out=t1[0:W, :], in_=P[7][0:W, 256:384], func=AF.Copy, scale=tokG[0:W, ti, 2, h:h + 1]), reads=[PK(7), tk], writes=[t1k])
        S.op("dve", lambda e: e.tensor_tensor(out=o[0:W, :], in0=P[7][0:W, 384:512], in1=t1[0:W, :], op=ALU.add), reads=[PK(7), t1k], writes=[ok_])
        if W == 1:
            S.op("pe", lambda e: e.transpose(out=P[7][:, 0:W], in_=o[0:W, :], identity=ident_f[0:W, 0:W]), reads=[ok_, "ident_f"], writes=[PK(7)])
            S.op("act", lambda e: e.copy(out=oraw[:, sb_:sb_ + 1], in_=P[7][:, 0:1]), reads=[PK(7)], writes=[("oraw", sb_)])
        else:
            ckpt("g9")
            ss, ssk = r_c1.get()
            S.op("act", lambda e: e.activation(out=t1[0:W, :], in_=o[0:W, :], func=AF.Square, accum_out=ss[0:W, :]), reads=[ok_], writes=[t1k, ssk])
            S.op("act", lambda e: e.activation(out=ss[0:W, :], in_=ss[0:W, :], func=AF.Ln, bias=epsc[0:W, 0:1], scale=1.0 / 128), reads=[ssk, "epsc"], writes=[ssk])
            S.op("act", lambda e: e.activation(out=ss[0:W, :], in_=ss[0:W, :], func=AF.Exp, scale=-0.5), reads=[ssk], writes=[ssk])
            S.op("dve", lambda e: e.tensor_scalar(out=o[0:W, :], in0=o[0:W, :], scalar1=ss[0:W, 0:1], scalar2=None, op0=ALU.mult), reads=[ok_, ssk], writes=[ok_])
            ckpt("g10")
            S.op("pe", lambda e: e.transpose(out=P[7][:, 0:W], in_=o[0:W, :], identity=ident_f[0:W, 0:W]), reads=[ok_, "ident_f"], writes=[PK(7)])
            S.op("dve", lambda e: e.scalar_tensor_tensor(out=oT[:, h, c0:c0 + W], in0=P[7][:, 0:W], scalar=g_gdn[:, 0:1], in1=BG.sgate[:, c0:c0 + W], op0=ALU.mult, op1=ALU.mult), reads=[PK(7), "g_gdn", BG.kg], writes=[("oT", h)])
        ckpt("g11")
        S.op("pe", lambda e: e.matmul(P[7][:, 128:256], kgk[0:W, :], vnew[0:W, :], start=True, stop=True), reads=[kgkk, vnewk], writes=[PK(7)])
        S.op("dve", lambda e: e.scalar_tensor_tensor(out=st_f, in0=st_f, scalar=tokG[:, ti, 5, h:h + 1], in1=P[7][:, 128:256], op0=ALU.mult, op1=ALU.add), reads=[stk_f, tk, PK(7)], writes=[stk_f])
        if sb_ is None:
            S.op("act", lambda e: e.copy(out=st_b, in_=st_f), reads=[stk_f], writes=[stk_b])
        else:
            S.dma("sp", o_gdn_s[sb_, h], st_f, reads=[stk_f], writes=[("o_gdn_s", sb_, h)])

    oraw = S.sb("oraw", [128, NSMP], F32)
    r_o16 = Rot("o16", [128, NSMP], F32, 2)
    sq16 = S.sb("sq16", [128, NSMP], BF16)

    def gdn_samples_norm(h, npr, ns, BG):
        cs = slice(npr, npr + ns)
        rs, rsk = r_o16.get()
        S.op("act", lambda e: e.activation(out=sq16[:, 0:ns], in_=oraw[:, 0:ns], func=AF.Square), reads=["oraw"], writes=["sq16"])
        S.op("pe", lambda e: e.matmul(P[7][:, 0:ns], ones_b[:], sq16[:, 0:ns], start=True, stop=True), reads=["ones_b", "sq16"], writes=[PK(7)])
        S.op("act", lambda e: e.activation(out=rs[:, 0:ns], in_=P[7][:, 0:ns], func=AF.Ln, bias=epsc[:, 0:1], scale=1.0 / 128), reads=[PK(7), "epsc"], writes=[rsk])
        S.op("act", lambda e: e.activation(out=rs[:, 0:ns], in_=rs[:, 0:ns], func=AF.Exp, scale=-0.5), reads=[rsk], writes=[rsk])
        S.op("dve", lambda e: e.scalar_tensor_tensor(out=rs[:, 0:ns], in0=oraw[:, 0:ns], scalar=g_gdn[:, 0:1], in1=rs[:, 0:ns], op0=ALU.mult, op1=ALU.mult), reads=["oraw", "g_gdn", rsk], writes=[rsk])
        S.op("dve", lambda e: e.tensor_tensor(out=oT[:, h, cs], in0=rs[:, 0:ns], in1=BG.sgate[:, cs], op=ALU.mult), reads=[rsk, BG.kg], writes=[("oT", h)])

    r_c4 = Rot("c4", [128, 4], F32, 2)
    r_gy2 = Rot("gy2", [128, 128], BF16, 2)
    r_gn2 = Rot("gn2", [128, 128], BF16, 2)
    r_fp = Rot("tfp", [128, 128], F32, 2)
    r_c1p = Rot("c1p", [128, 1], F32, 2)
    NPT = SEGLEN // 128
    r_f4 = Rot("f4", [128, NPT, 128], F32, 3)
    r_b4 = Alias([(arena[:, i * NPT * 128:(i + 1) * NPT * 128].rearrange("p (t d) -> p t d", d=128), ("arena", i)) for i in range(NB4)])
    r_k4 = Rot("k4", [128, NPT, 128], BF16, 6)

    def gdn_head_prompt(h, npr, BG):
        npt = npr // 128
        W = 128

        def bc(ap2):
            return ap2.unsqueeze(1).to_broadcast([128, npt, 128])

        def tcol(idx):
            return tokG[:, 0:npt, idx, h:h + 1].to_broadcast([128, npt, 128])

        tkeys = [("tokG", t) for t in range(npt)]
        vtok4, vtk = r_k4.get()
        kgk4, kgkk = r_k4.get()
        qkT4, qkk = r_k4.get()
        MTa4, MTk = r_k4.get()
        Ma4, Mak = r_k4.get()
        TTf, TTfk = r_k4.get()
        for t in range(npt):
            S.op("pe", lambda e, t=t: e.transpose(out=P2b[:, t * 128:(t + 1) * 128], in_=BG.kT[:, t * 128:(t + 1) * 128], identity=ident_b[:]), reads=[BG.kk, "ident_b"], writes=[PK(2)])
        for t in range(npt):
            S.op("pe", lambda e, t=t: e.transpose(out=P2b[:, 512 + t * 128:512 + (t + 1) * 128], in_=BG.vT[:, t * 128:(t + 1) * 128], identity=ident_b[:]), reads=[BG.kv, "ident_b"], writes=[PK(2)])
        S.op("dve", lambda e: e.tensor_tensor(out=kgk4[:, 0:npt, :], in0=P2b[:, 0:npt * 128].rearrange("p (t d) -> p t d", d=128), in1=tcol(4), op=ALU.mult), reads=[PK(2)] + tkeys, writes=[kgkk])
        S.op("dve", lambda e: e.tensor_copy(out=vtok4[:, 0:npt, :], in_=P2b[:, 512:512 + npt * 128].rearrange("p (t d) -> p t d", d=128)), reads=[PK(2)], writes=[vtk])
        gU4, gUk = r_f4.get()
        S.op("pool", lambda e: e.tensor_tensor(out=gU4[:, 0:npt, :], in0=bc(mU[:]), in1=tcol(0), op=ALU.mult), reads=["mU"] + tkeys, writes=[gUk])
        for t in range(npt):
            S.op("pe", lambda e, t=t: e.matmul(P[3][:, t * 128:(t + 1) * 128], mT[:], gU4[:, t, :], start=True, stop=False), reads=["mT", gUk], writes=[PK(3)])
            S.op("pe", lambda e, t=t: e.matmul(P[3][:, t * 128:(t + 1) * 128], ident_f[:], mNEG[:], start=False, stop=True), reads=["ident_f", "mNEG"], writes=[PK(3)])
        Dge4, Dgk = r_f4.get()
        S.op("act", lambda e: e.activation(out=Dge4[:, 0:npt, :], in_=P[3][:, 0:npt * 128].rearrange("p (t d) -> p t d", d=128), func=AF.Exp), reads=[PK(3)], writes=[Dgk])
        for t in range(npt):
            S.op("pe", lambda e, t=t: e.matmul(P[4][:, t * 128:(t + 1) * 128], BG.kT[:, t * 128:(t + 1) * 128], BG.kT[:, t * 128:(t + 1) * 128], start=True, stop=True), reads=[BG.kk], writes=[PK(4)])
        for t in range(npt):
            S.op("pe", lambda e, t=t: e.matmul(P[5][:, t * 128:(t + 1) * 128], BG.kT[:, t * 128:(t + 1) * 128], BG.qT[:, t * 128:(t + 1) * 128], start=True, stop=True), reads=[BG.kk, BG.kq], writes=[PK(5)])
        S.op("dve", lambda e: e.tensor_tensor(out=qkT4[:, 0:npt, :], in0=P[5][:, 0:npt * 128].rearrange("p (t d) -> p t d", d=128), in1=Dge4[:, 0:npt, :], op=ALU.mult), reads=[PK(5), Dgk], writes=[qkk])
        tmp4, tmpk = r_f4.get()
        S.op("dve", lambda e: e.tensor_tensor(out=tmp4[:, 0:npt, :], in0=P[4][:, 0:npt * 128].rearrange("p (t d) -> p t d", d=128), in1=Dge4[:, 0:npt, :], op=ALU.mult), reads=[PK(4), Dgk], writes=[tmpk])
        bUS4, bUk = r_f4.get()
        S.op("pool", lambda e: e.tensor_tensor(out=bUS4[:, 0:npt, :], in0=bc(mUS[:]), in1=tcol(1), op=ALU.mult), reads=["mUS"] + tkeys, writes=[bUk])
        S.op("pool", lambda e: e.tensor_tensor(out=MTa4[:, 0:npt, :], in0=tmp4[:, 0:npt, :], in1=bUS4[:, 0:npt, :], op=ALU.mult), reads=[tmpk, bUk], writes=[MTk])
        for t in range(npt):
            S.op("pe", lambda e, t=t: e.transpose(out=P2b[:, t * 128:(t + 1) * 128], in_=MTa4[:, t, :], identity=ident_b[:]), reads=[MTk, "ident_b"], writes=[PK(2)])
        S.op("act", lambda e: e.copy(out=Ma4[:, 0:npt, :], in_=P2b[:, 0:npt * 128].rearrange("p (t d) -> p t d", d=128)), reads=[PK(2)], writes=[Mak])
        def slot(i):
            return arena[:, i * NPT * 128:(i + 1) * NPT * 128].rearrange("p (t d) -> p t d", d=128), ("arena", i)

        def mk_L(lv):
            Lj, Ljk = slot(4 + lv % 2)
            S.op("pool", lambda e, Lj=Lj, lv=lv: e.tensor_tensor(out=Lj[:, 0:npt, :], in0=Ma4[:, 0:npt, :], in1=bc(cmask[:, lv, :]), op=ALU.mult), reads=[Mak, "cmask"], writes=[Ljk])
            return Lj, Ljk

        def mk_N(lv):
            Nj, Njk = slot(6 + lv % 2)
            S.op("pool", lambda e, Nj=Nj, lv=lv: e.tensor_tensor(out=Nj[:, 0:npt, :], in0=MTa4[:, 0:npt, :], in1=bc(cmask[:, 7 + lv, :]), op=ALU.mult), reads=[MTk, "cmask"], writes=[Njk])
            return Nj, Njk

        nlev = 7
        L0, L0k = mk_L(0)
        N0, N0k = mk_N(0)
        T, Tk = slot(0)
        TT, TTk = slot(2)
        S.op("pool", lambda e, T=T, L0=L0: e.tensor_tensor(out=T[:, 0:npt, :], in0=bc(ident_b[:]), in1=L0[:, 0:npt, :], op=ALU.subtract), reads=["ident_b", L0k], writes=[Tk])
        S.op("pool", lambda e, TT=TT, N0=N0: e.tensor_tensor(out=TT[:, 0:npt, :], in0=bc(ident_b[:]), in1=N0[:, 0:npt, :], op=ALU.subtract), reads=["ident_b", N0k], writes=[TTk])
        nxtL = mk_L(1)
        nxtN = mk_N(1)
        Ub, Ubk = slot(8)
        Zb, Zbk = slot(9)
        for lv in range(1, nlev):
            last = (lv == nlev - 1)
            Lj, Ljk = nxtL
            Nj, Njk = nxtN if not last else (None, None)
            if lv + 1 < nlev:
                nxtL = mk_L(lv + 1)
                if lv + 1 < nlev - 1:
                    nxtN = mk_N(lv + 1)
            for t in range(npt):
                S.op("pe", lambda e, t=t, Lj=Lj, TT=TT: e.matmul(P[5][:, t * 128:(t + 1) * 128], Lj[:, t, :], TT[:, t, :], start=True, stop=True), reads=[Ljk, TTk], writes=[PK(5)])
            S.op("act", lambda e: e.copy(out=Ub[:, 0:npt, :], in_=P[5][:, 0:npt * 128].rearrange("p (t d) -> p t d", d=128)), reads=[PK(5)], writes=[Ubk])
            for t in range(npt):
                S.op("pe", lambda e, t=t, T=T: e.matmul(P[6][:, t * 128:(t + 1) * 128], T[:, t, :], Ub[:, t, :], start=True, stop=True), reads=[Tk, Ubk], writes=[PK(6)])
            if last:
                TTn, TTnk = TTf, TTfk
            else:
                TTn, TTnk = slot(2 + lv % 2)
                for t in range(npt):
                    S.op("pe", lambda e, t=t, Nj=Nj, T=T: e.matmul(P[4][:, t * 128:(t + 1) * 128], Nj[:, t, :], T[:, t, :], start=True, stop=True), reads=[Njk, Tk], writes=[PK(4)])
                S.op("act", lambda e: e.copy(out=Zb[:, 0:npt, :], in_=P[4][:, 0:npt * 128].rearrange("p (t d) -> p t d", d=128)), reads=[PK(4)], writes=[Zbk])
                for t in range(npt):
                    S.op("pe", lambda e, t=t, TT=TT: e.matmul(P[3][:, t * 128:(t + 1) * 128], TT[:, t, :], Zb[:, t, :], start=True, stop=True), reads=[TTk, Zbk], writes=[PK(3)])
                Tn, Tnk = slot(lv % 2)
                S.op("dve", lambda e, T=T, Tn=Tn: e.tensor_tensor(out=Tn[:, 0:npt, :], in0=T[:, 0:npt, :], in1=P[3][:, 0:npt * 128].rearrange("p (t d) -> p t d", d=128), op=ALU.subtract), reads=[Tk, PK(3)], writes=[Tnk])
            S.op("dve", lambda e, TT=TT, TTn=TTn: e.tensor_tensor(out=TTn[:, 0:npt, :], in0=TT[:, 0:npt, :], in1=P[6][:, 0:npt * 128].rearrange("p (t d) -> p t d", d=128), op=ALU.subtract), reads=[TTk, PK(6)], writes=[TTnk])
            TT, TTk = TTn, TTnk
            if not last:
                T, Tk = Tn, Tnk
        st_f, stk_f = Sg[:, h, :], ("Sg", h)
        st_b, stk_b = Sgb[:, h, :], ("Sgb", h)
        for t in range(npt):
            c0 = t * 128
            tk = ("tokG", t)
            S.op("pe", lambda e, c0=c0: e.matmul(P[3][:, 0:128], BG.kT[:, c0:c0 + W], st_b, start=True, stop=True), reads=[BG.kk, stk_b], writes=[PK(3)])
            Yb, Ybk = r_gy2.get()
            S.op("dve", lambda e, t=t, Yb=Yb: e.scalar_tensor_tensor(out=Yb[:], in0=P[3][:, 0:128], scalar=tokG[:, t, 3, h:h + 1], in1=vtok4[:, t, :], op0=ALU.mult, op1=ALU.add), reads=[PK(3), tk, vtk], writes=[Ybk])
            S.op("pe", lambda e, t=t, Yb=Yb: e.matmul(P[3][:, 128:256], TTf[:, t, :], Yb[:], start=True, stop=True), reads=[TTfk, Ybk], writes=[PK(3)])
            vnew, vnewk = r_gn2.get()
            S.op("act", lambda e, t=t, vnew=vnew: e.activation(out=vnew[:], in_=P[3][:, 128:256], func=AF.Copy, scale=tokG[:, t, 1, h:h + 1]), reads=[PK(3), tk], writes=[vnewk])
            S.op("pe", lambda e, c0=c0, t=t: e.matmul(P[4][:, t * 128:(t + 1) * 128], BG.qT[:, c0:c0 + W], st_b, start=True, stop=True), reads=[BG.kq, stk_b], writes=[PK(4)])
            S.op("pe", lambda e, t=t, vnew=vnew: e.matmul(P[5][:, t * 128:(t + 1) * 128], qkT4[:, t, :], vnew[:], start=True, stop=True), reads=[qkk, vnewk], writes=[PK(5)])
            S.op("pe", lambda e, t=t, vnew=vnew: e.matmul(P[6][:, 0:128], kgk4[:, t, :], vnew[:], start=True, stop=True), reads=[kgkk, vnewk], writes=[PK(6)])
            S.op("dve", lambda e, t=t: e.scalar_tensor_tensor(out=st_f, in0=st_f, scalar=tokG[:, t, 5, h:h + 1], in1=P[6][:, 0:128], op0=ALU.mult, op1=ALU.add), reads=[stk_f, tk, PK(6)], writes=[stk_f])
            S.op("act", lambda e: e.copy(out=st_b, in_=st_f), reads=[stk_f], writes=[stk_b])
        t14, t14k = r_f4.get()
        o4, o4k = r_f4.get()
        sq4, sq4k = r_f4.get()
        ss4, ss4k = r_c4.get()
        S.op("dve", lambda e: e.tensor_tensor(out=t14[:, 0:npt, :], in0=P[4][:, 0:npt * 128].rearrange("p (t d) -> p t d", d=128), in1=tcol(2), op=ALU.mult), reads=[PK(4)] + tkeys, writes=[t14k])
        S.op("dve", lambda e: e.tensor_tensor(out=o4[:, 0:npt, :], in0=P[5][:, 0:npt * 128].rearrange("p (t d) -> p t d", d=128), in1=t14[:, 0:npt, :], op=ALU.add), reads=[PK(5), t14k], writes=[o4k])
        S.op("pool", lambda e: e.tensor_tensor(out=sq4[:, 0:npt, :], in0=o4[:, 0:npt, :], in1=o4[:, 0:npt, :], op=ALU.mult), reads=[o4k], writes=[sq4k])
        S.op("dve", lambda e: e.tensor_reduce(out=ss4[:, 0:npt], in_=sq4[:, 0:npt, :], axis=AX.X, op=ALU.add), reads=[sq4k], writes=[ss4k])
        S.op("act", lambda e: e.activation(out=ss4[:, 0:npt], in_=ss4[:, 0:npt], func=AF.Ln, bias=epsc[:, 0:1], scale=1.0 / 128), reads=[ss4k, "epsc"], writes=[ss4k])
        S.op("act", lambda e: e.activation(out=ss4[:, 0:npt], in_=ss4[:, 0:npt], func=AF.Exp, scale=-0.5), reads=[ss4k], writes=[ss4k])
        S.op("pool", lambda e: e.tensor_tensor(out=o4[:, 0:npt, :], in0=o4[:, 0:npt, :], in1=ss4[:, 0:npt].unsqueeze(2).to_broadcast([128, npt, 128]), op=ALU.mult), reads=[o4k, ss4k], writes=[o4k])
        for t in range(npt):
            S.op("pe", lambda e, t=t: e.transpose(out=P[6][:, t * 128:(t + 1) * 128], in_=o4[:, t, :], identity=ident_f[:]), reads=[o4k, "ident_f"], writes=[PK(6)])
        S.op("dve", lambda e: e.scalar_tensor_tensor(out=oT[:, h, 0:npr], in0=P[6][:, 0:npr], scalar=g_gdn[:, 0:1], in1=BG.sgate[:, 0:npr], op0=ALU.mult, op1=ALU.mult), reads=[PK(6), "g_gdn", BG.kg], writes=[("oT", h)])

    f8b = S.sb("f8b", [128, NPT, 2, 128], F32)
    r_f8 = Alias([(xtok[:, 0:NPT * 256].rearrange("p (t a l) -> p t a l", t=NPT, a=2), "xtok"), (f8b, "f8b")])
    r_m8 = Rot("m8", [128, NPT, 2, 128], BF16, 1)
    r_x4 = r_k4
    r_g4 = r_f4

    snf = S.sb("snf", [128, NPT, 128], F32)
    snb = S.sb("snb", [128, NPT, 128], BF16)

    def ssd_pair_prompt(g, jj, j, npr, BS):
        npt = npr // 128
        h0 = 2 * j
        tkeys = [("tokS", t) for t in range(npt)]

        def hsc(idx):
            return tokS[:, 0:npt, idx, h0:h0 + 2].unsqueeze(3).to_broadcast([128, npt, 2, 64])

        def v4(ap):
            return ap.rearrange("p (t a c) -> p t a c", t=npt, a=2)

        st_f, stk_f = prevT[:, j * 128:(j + 1) * 128], ("prevT", j)
        st_b, stk_b = prevTb[:, j * 128:(j + 1) * 128], ("prevTb", j)
        for t in range(npt):
            S.op("pe", lambda e, t=t: e.transpose(out=P[2][:, t * 128:(t + 1) * 128], in_=BS.xsT[:, t * 128:(t + 1) * 128], identity=ident_f[:]), reads=[BS.kxs, "ident_f"], writes=[PK(2)])
        xsk4, xskk = r_g4.get()
        xdt4, xdtk = r_x4.get()
        xdd4, xddk = r_x4.get()
        S.op("act", lambda e: e.copy(out=xsk4[:, 0:npt, :], in_=P[2][:, 0:npt * 128].rearrange("p (t d) -> p t d", d=128)), reads=[PK(2)], writes=[xskk])
        S.op("dve", lambda e: e.tensor_tensor(out=v4(xdt4[:, 0:npt, :].rearrange("p t d -> p (t d)")), in0=v4(P[2][:, 0:npt * 128]), in1=hsc(0), op=ALU.mult), reads=[PK(2)] + tkeys, writes=[xdtk])
        S.op("dve", lambda e: e.tensor_tensor(out=v4(xdd4[:, 0:npt, :].rearrange("p t d -> p (t d)")), in0=v4(P[2][:, 0:npt * 128]), in1=hsc(2), op=ALU.mult), reads=[PK(2)] + tkeys, writes=[xddk])
        daU, daUk = r_f8.get()
        S.op("pool", lambda e: e.tensor_tensor(out=daU[:, 0:npt, :, :], in0=mU[:].unsqueeze(1).unsqueeze(1).to_broadcast([128, npt, 2, 128]), in1=tokS[:, 0:npt, 1, h0:h0 + 2].unsqueeze(3).to_broadcast([128, npt, 2, 128]), op=ALU.mult), reads=["mU"] + tkeys, writes=[daUk])
        Lh, Lhk = r_f8.get()
        for t in range(npt):
            pb = 3 + t // 2
            for hh in range(2):
                col = ((t % 2) * 2 + hh) * 128
                S.op("pe", lambda e, t=t, hh=hh, pb=pb, col=col: e.matmul(P[pb][:, col:col + 128], mT[:], daU[:, t, hh, :], start=True, stop=False), reads=["mT", daUk], writes=[PK(pb)])
                S.op("pe", lambda e, pb=pb, col=col: e.matmul(P[pb][:, col:col + 128], ident_f[:], mNEG[:], start=False, stop=True), reads=["ident_f", "mNEG"], writes=[PK(pb)])
        for half in range((npt + 1) // 2):
            nt = min(2, npt - 2 * half)
            S.op("act", lambda e, half=half, nt=nt: e.activation(out=Lh[:, 2 * half:2 * half + nt, :, :], in_=P[3 + half][:, 0:nt * 256].rearrange("p (t a l) -> p t a l", t=nt, a=2), func=AF.Exp), reads=[PK(3 + half)], writes=[Lhk])
        M8, M8k = r_m8.get()
        S.op("dve", lambda e: e.tensor_tensor(out=M8[:, 0:npt, :, :], in0=Lh[:, 0:npt, :, :], in1=cbm[:, 0:npt, :].unsqueeze(2).to_broadcast([128, npt, 2, 128]), op=ALU.mult), reads=[Lhk] + [("cbm", t) for t in range(npt)], writes=[M8k])
        for t in range(npt):
            for hh in range(2):
                S.op("pe", lambda e, t=t, hh=hh: e.matmul(P[5][:, t * 128 + hh * 64:t * 128 + (hh + 1) * 64], M8[:, t, hh, :], xdt4[:, t, hh * 64:(hh + 1) * 64], start=True, stop=True), reads=[M8k, xdtk], writes=[PK(5)])
        for t in range(npt):
            S.op("pe", lambda e, t=t: e.matmul(P[6][:, t * 128:(t + 1) * 128], bmtok[:, t, :], xdd4[:, t, :], start=True, stop=True), reads=[("bmtok", t), xddk], writes=[PK(6)])
        for t in range(npt):
            last = (t == npt - 1)
            src_f, src_fk = (st_f, stk_f) if t == 0 else (snf[:, t - 1, :], ("snf", t - 1))
            src_b, src_bk = (st_b, stk_b) if t == 0 else (snb[:, t - 1, :], ("snb", t - 1))
            dst_f, dst_fk = (st_f, stk_f) if last else (snf[:, t, :], ("snf", t))
            dst_b, dst_bk = (st_b, stk_b) if last else (snb[:, t, :], ("snb", t))
            S.op("pe", lambda e, t=t, src_b=src_b: e.matmul(P[7][:, t * 128:(t + 1) * 128], cmT[:, t * 128:(t + 1) * 128], src_b, start=True, stop=True), reads=["cmT", src_bk], writes=[PK(7)])
            for hh in range(2):
                h = h0 + hh
                S.op("dve", lambda e, t=t, hh=hh, h=h, src_f=src_f, dst_f=dst_f: e.scalar_tensor_tensor(out=dst_f[:, hh * 64:(hh + 1) * 64], in0=src_f[:, hh * 64:(hh + 1) * 64], scalar=tokS[:, t, 4, h:h + 1], in1=P[6][:, t * 128 + hh * 64:t * 128 + (hh + 1) * 64], op0=ALU.mult, op1=ALU.add), reads=[src_fk, ("tokS", t), PK(6)], writes=[dst_fk])
            S.op("act", lambda e, dst_f=dst_f, dst_b=dst_b: e.copy(out=dst_b, in_=dst_f), reads=[dst_fk], writes=[dst_bk])
        t14, t14k = r_g4.get()
        ys4, ys4k = r_g4.get()
        S.op("dve", lambda e: e.tensor_tensor(out=v4(t14[:, 0:npt, :].rearrange("p t d -> p (t d)")), in0=v4(P[7][:, 0:npt * 128]), in1=hsc(3), op=ALU.mult), reads=[PK(7)] + tkeys, writes=[t14k])
        S.op("dve", lambda e: e.tensor_tensor(out=ys4[:, 0:npt, :], in0=P[5][:, 0:npt * 128].rearrange("p (t d) -> p t d", d=128), in1=t14[:, 0:npt, :], op=ALU.add), reads=[PK(5), t14k], writes=[ys4k])
        S.op("pool", lambda e: e.tensor_tensor(out=v4(t14[:, 0:npt, :].rearrange("p t d -> p (t d)")), in0=v4(xsk4[:, 0:npt, :].rearrange("p t d -> p (t d)")), in1=dskB[:, h0:h0 + 2].unsqueeze(1).unsqueeze(3).to_broadcast([128, npt, 2, 64]), op=ALU.mult), reads=[xskk, "dskB", t14k], writes=[t14k])
        S.op("pool", lambda e: e.tensor_tensor(out=ys4[:, 0:npt, :], in0=ys4[:, 0:npt, :], in1=t14[:, 0:npt, :], op=ALU.add), reads=[ys4k, t14k], writes=[ys4k])
        for t in range(npt):
            S.op("pe", lambda e, t=t: e.transpose(out=P[2][:, t * 128:(t + 1) * 128], in_=ys4[:, t, :], identity=ident_f[:]), reads=[ys4k, "ident_f"], writes=[PK(2)])
        S.op("dve", lambda e: e.tensor_tensor(out=yg[:, jj, 0:npr], in0=P[2][:, 0:npr], in1=BS.szT[:, 0:npr], op=ALU.mult), reads=[PK(2), BS.kzs], writes=[("yg", jj)])

    def phase2(si, t0, npr, ns, NC, blks, load_xT):
        nb = len(blks)

        def ew(eng, fn, reads, writes):
            S.op(eng, fn, reads=reads, writes=writes)

        for m in range(8):
            wa, wak = load_w(w_in, 8, OFF_MG + m * 128)
            mm(0, 128, wa, wak, 8, lambda k, c0, n: hT[:, k, c0:c0 + n], ["hT"], blks, pbase=0)
            wb, wbk = load_w(w_bs, 16, m * 128)
            mm(0, 128, wb, wbk, 16, lambda k, c0, n: yT[:, k, c0:c0 + n], ["yT"], blks, pbase=2)
            wc, wck = load_w(w_in, 8, OFF_MG + 1024 + m * 128)
            mm(0, 128, wc, wck, 8, lambda k, c0, n: hT[:, k, c0:c0 + n], ["hT"], blks, pbase=4)
            wd, wdk = load_w(w_bg, 8, m * 128)
            mm(0, 128, wd, wdk, 8, lambda k, c0, n: oT[:, k, c0:c0 + n], ["oT"], blks, pbase=6)
            for bi, (c0, n) in enumerate(blks):
                sg, sgk = r_sig.get()
                tt, ttk = r_t1.get()
                S.op("act", lambda e, bi=bi, n=n, sg=sg: e.activation(out=sg[:, 0:n], in_=P[bi][:, 0:n], func=AF.Sigmoid), reads=[PK(bi)], writes=[sgk])
                S.op("dve", lambda e, bi=bi, n=n, sg=sg, tt=tt: e.tensor_tensor(out=tt[:, 0:n], in0=P[2 + bi][:, 0:n], in1=sg[:, 0:n], op=ALU.mult), reads=[PK(2 + bi), sgk], writes=[ttk])
                sg2, sg2k = r_sig.get()
                S.op("act", lambda e, bi=bi, n=n, sg2=sg2: e.activation(out=sg2[:, 0:n], in_=P[4 + bi][:, 0:n], func=AF.Sigmoid), reads=[PK(4 + bi)], writes=[sg2k])
                S.op("dve", lambda e, bi=bi, n=n, sg2=sg2: e.tensor_tensor(out=sg2[:, 0:n], in0=P[6 + bi][:, 0:n], in1=sg2[:, 0:n], op=ALU.mult), reads=[PK(6 + bi), sg2k], writes=[sg2k])
                S.op("pool", lambda e, c0=c0, n=n, sg2=sg2, tt=tt, m=m: e.tensor_tensor(out=mixT[:, m, c0:c0 + n], in0=sg2[:, 0:n], in1=tt[:, 0:n], op=ALU.add), reads=[sg2k, ttk], writes=[("arena",)])
        for m in range(8):
            w, wk = load_w(w_out, 8, m * 128)
            mm(0, 128, w, wk, 8, lambda k, c0, n: mixT[:, k, c0:c0 + n], [("arena",)], blks, pbase=(m % 2) * 2)
            for bi, (c0, n) in enumerate(blks):
                S.op("dve", lambda e, bi=bi, c0=c0, n=n, m=m: e.tensor_tensor(out=xT[:, m, c0:c0 + n], in0=P[(m % 2) * 2 + bi][:, 0:n], in1=xT[:, m, c0:c0 + n], op=ALU.add), reads=[PK((m % 2) * 2 + bi), ("xT", m)], writes=[("xT", m)])
        rmsnorm(xT, "xT", g_ffn, hT, "hT", _blocks256(NC))
        for half in range(2):
            for fi in range(11):
                f = half * 11 + fi
                wg_, wgk = load_w(w_ffn_in, 8, f * 128)
                mm(0, 128, wg_, wgk, 8, lambda k, c0, n: hT[:, k, c0:c0 + n], ["hT"], blks, pbase=(fi % 2) * 4)
                wu, wuk = load_w(w_ffn_in, 8, DFF + f * 128)
                mm(0, 128, wu, wuk, 8, lambda k, c0, n: hT[:, k, c0:c0 + n], ["hT"], blks, pbase=(fi % 2) * 4 + 2)
                for bi, (c0, n) in enumerate(blks):
                    sg, sgk = r_sig.get()
                    pb = (fi % 2) * 4
                    S.op("act", lambda e, bi=bi, n=n, sg=sg, pb=pb: e.activation(out=sg[:, 0:n], in_=P[pb + bi][:, 0:n], func=AF.Silu), reads=[PK(pb + bi)], writes=[sgk])
                    S.op("dve", lambda e, bi=bi, c0=c0, n=n, sg=sg, pb=pb, fi=fi: e.tensor_tensor(out=yT[:, fi, c0:c0 + n], in0=P[pb + 2 + bi][:, 0:n], in1=sg[:, 0:n], op=ALU.mult), reads=[PK(pb + 2 + bi), sgk], writes=[("yT", fi)])
            for m in range(8):
                w, wk = load_w(w_ffn_out, 11, m * 128, r0=half * 11 * 128)
                mm(0, 128, w, wk, 11, lambda k, c0, n: yT[:, k, c0:c0 + n], ["yT"], blks, pbase=(m % 2) * 2)
                for bi, (c0, n) in enumerate(blks):
                    S.op("dve", lambda e, bi=bi, c0=c0, n=n, m=m: e.tensor_tensor(out=xT[:, m, c0:c0 + n], in0=P[(m % 2) * 2 + bi][:, 0:n], in1=xT[:, m, c0:c0 + n], op=ALU.add), reads=[PK((m % 2) * 2 + bi), ("xT", m)], writes=[("xT", m)])
        rmsnorm(xT, "xT", g_pl, hT, "hT", _blocks256(NC))
        load_xT(mixT, "arena", p_p, p_s, 256, 2)
        for m in range(8):
            wg_, wgk = load_w(w_pl_gate, 8, m * 128)
            pb = (m % 2) * 4
            mm(0, 128, wg_, wgk, 8, lambda k, c0, n: hT[:, k, c0:c0 + n], ["hT"], blks, pbase=pb)
            wp, wpk = load_w(w_pl_proj, 2, m * 128)
            mm(0, 128, wp, wpk, 2, lambda k, c0, n: mixT[:, k, c0:c0 + n], [("arena",)], blks, pbase=pb + 2)
            for bi, (c0, n) in enumerate(blks):
                sg, sgk = r_sig.get()
                S.op("act", lambda e, bi=bi, n=n, sg=sg, pb=pb: e.activation(out=sg[:, 0:n], in_=P[pb + bi][:, 0:n], func=AF.Sigmoid), reads=[PK(pb + bi)], writes=[sgk])
                S.op("dve", lambda e, bi=bi, n=n, sg=sg, pb=pb: e.tensor_tensor(out=sg[:, 0:n], in0=P[pb + 2 + bi][:, 0:n], in1=sg[:, 0:n], op=ALU.mult), reads=[PK(pb + 2 + bi), sgk], writes=[sgk])
                S.op("pool", lambda e, c0=c0, n=n, sg=sg, m=m: e.tensor_tensor(out=xT[:, m, c0:c0 + n], in0=sg[:, 0:n], in1=xT[:, m, c0:c0 + n], op=ALU.add), reads=[sgk, ("xT", m)], writes=[("xT", m)])
        rmsnorm(xT, "xT", g_fin, xT, "xT", _blocks256(NC))
        otok = xtok
        for ti in range(npr // 128 + (1 if ns else 0)):
            if ti < npr // 128:
                rows, c0, dst = 128, ti * 128, y_p[t0 + ti * 128:t0 + (ti + 1) * 128, :]
            else:
                rows, c0, dst = ns, npr, y_s[:, :]
            for k in range(8):
                S.op("pe", lambda e, k=k, rows=rows, c0=c0: e.transpose(out=P[k // 4][0:rows, (k % 4) * 128:(k % 4 + 1) * 128], in_=xT[:, k, c0:c0 + rows], identity=ident_f[:]), reads=[("xT", k), "ident_f"], writes=[PK(k // 4)])
            S.op("act", lambda e, rows=rows: e.copy(out=otok[0:rows, 0:512], in_=P[0][0:rows, :]), reads=[PK(0)], writes=["xtok"])
            S.op("dve", lambda e, rows=rows: e.tensor_copy(out=otok[0:rows, 512:1024], in_=P[1][0:rows, :]), reads=[PK(1)], writes=["xtok"])
            S.dma("sp", dst, otok[0:rows, :], reads=["xtok"], writes=[("y", si, ti)])

    for si, (t0, npr, ns) in enumerate(SEGS):
        segment(si, t0, npr, ns)

    for j in range(16):
        so, sok = r_sst.get()
        S.op("pe", lambda e, j=j: e.transpose(out=P[6][:, 256:384], in_=prevT[:, j * 128:(j + 1) * 128], identity=ident_f[:]), reads=[("prevT", j), "ident_f"], writes=[PK(6, 2)])
        S.op("act", lambda e, so=so: e.copy(out=so[:], in_=P[6][:, 256:384]), reads=[PK(6, 2)], writes=[sok])
        S.dma("sp", o_ssm_p[j * 128:(j + 1) * 128, :], so[:], reads=[sok], writes=[("o_ssm_p", j)])
    S.dma("sp", o_gdn_p.rearrange("h k v -> k h v"), Sg[:], reads=["Sg"], writes=["o_gdn_p"])


_OUT_NAMES = ["y_p", "y_s", "o_ssm_p", "o_ssmc_p", "o_gdn_p", "o_gdnc_p", "o_ssm_s", "o_ssmc_s", "o_gdn_s", "o_gdnc_s"]


def _cmask():
    idx = np.arange(128)
    out = np.zeros((128, 14, 128), np.float32)
    for j in range(7):
        b = 1 << j
        m = ((idx[:, None] // (2 * b)) == (idx[None, :] // (2 * b))) & ((idx[:, None] // b) != (idx[None, :] // b)) & (idx[:, None] > idx[None, :])
        out[:, j, :] = m
        out[:, 7 + j, :] = m.T
    return out


def kernel(**inp):
    f = lambda a: np.ascontiguousarray(np.asarray(a, dtype=np.float32))
    nc, S = build_nc()
    common = {}
    for name in ["norm_mix", "w_in", "ssm_conv_w", "ssm_conv_b", "ssm_dt_bias", "ssm_a_log", "ssm_d", "ssm_norm",
                 "gdn_conv_w", "gdn_dt_bias", "gdn_a_log", "gdn_norm", "w_branch_ssm", "w_branch_gdn", "w_out",
                 "norm_ffn", "w_ffn_in", "w_ffn_out", "norm_pl", "w_pl_gate", "w_pl_proj"]:
        common[name] = f(inp[name][0])
    common["norm_final"] = f(inp["norm_final"])
    common["cmask_in"] = _cmask()
    in_maps = []
    for c in range(8):
        s0, s1 = c * NSMP, (c + 1) * NSMP
        m = dict(common)
        m["x_p"] = f(inp["x_prompt"][c])
        m["x_s"] = f(inp["x_sample"][s0:s1, 0])
        m["p_p"] = f(inp["p_prompt"][0, c])
        m["p_s"] = f(inp["p_sample"][0, s0:s1, 0])
        m["st_ssm"] = f(np.asarray(inp["state_ssm"][0, s0:s1]).reshape(NSMP, 2048, 128))
        m["st_ssm_conv"] = f(inp["state_ssm_conv"][0, s0:s1])
        m["st_gdn"] = f(inp["state_gdn"][0, s0:s1])
        m["st_gdn_conv"] = f(inp["state_gdn_conv"][0, s0:s1])
        in_maps.append(m)
    res = run_bass_kernel_spmd(nc, in_maps, core_ids=list(range(8)))
    R = res.results
    cat = lambda n: np.concatenate([R[c][n][None] if n.endswith("_p") else R[c][n] for c in range(8)], axis=0)
    y_prompt = cat("y_p")
    y_sample = cat("y_s")[:, None, :]
    new_ssm_p = cat("o_ssm_p").reshape(1, 8, 32, 64, 128)
    new_ssmc_p = cat("o_ssmc_p")[None]
    new_gdn_p = cat("o_gdn_p")[None]
    new_gdnc_p = cat("o_gdnc_p")[None]
    new_ssm_s = cat("o_ssm_s").reshape(1, 128, 32, 64, 128)
    new_ssmc_s = cat("o_ssmc_s")[None]
    new_gdn_s = cat("o_gdn_s")[None]
    new_gdnc_s = cat("o_gdnc_s")[None]
    outs = (y_prompt, y_sample, new_ssm_p, new_ssmc_p, new_gdn_p, new_gdnc_p, new_ssm_s, new_ssmc_s, new_gdn_s, new_gdnc_s)
    return tuple(np.ascontiguousarray(o, dtype=np.float32) for o in outs)
```
